# Optimizing a Trainium2 kernel written in Bass

```python
import jax
import jax.numpy as jnp
from jax import lax
import numpy as np


D_MODEL = 1024
BATCH = 8
SEQ = 4096
DEPTH = 1

CTX_LEN = 256
GRID_W = 64

RET_HEADS = 4
RET_DK = 128
RET_DV = 128
RET_CHUNK = 128
RET_W = RET_HEADS * RET_DV
RET_SCALE = RET_DK ** -0.5

ATT_HEADS = 8
ATT_KV_HEADS = 2
ATT_GROUP = ATT_HEADS // ATT_KV_HEADS
ATT_HD = 64
ATT_W = ATT_HEADS * ATT_HD
ATT_SCALE = ATT_HD ** -0.5
Q_BLOCK = 128

MIX_W = RET_W + ATT_W
D_FF = 4 * D_MODEL
ROPE_BASE = 10000.0
EPS = 1e-6

RQ_OFF = 0
RK_OFF = RQ_OFF + RET_HEADS * RET_DK
RV_OFF = RK_OFF + RET_HEADS * RET_DK
RG_OFF = RV_OFF + RET_W
AQ_OFF = RG_OFF + RET_W
AK_OFF = AQ_OFF + ATT_W
AV_OFF = AK_OFF + ATT_KV_HEADS * ATT_HD
D_IN = AV_OFF + ATT_KV_HEADS * ATT_HD
SPLIT_IDX = [RK_OFF, RV_OFF, RG_OFF, AQ_OFF, AK_OFF, AV_OFF]

kernel_name = "hybrid_retention_gqa_dit_layer"


def _rmsnorm(x, g):
    xf = x.astype(jnp.float32)
    y = xf * lax.rsqrt(jnp.mean(xf * xf, axis=-1, keepdims=True) + EPS)
    return (y * g.astype(jnp.float32)).astype(x.dtype)


def _modulate(h, shift, scale):
    return h * (1 + scale) + shift


def _freqs(n_pairs):
    return ROPE_BASE ** (-jnp.arange(n_pairs, dtype=jnp.float32) / n_pairs)


def _apply_rope(x, cos, sin):
    x1, x2 = jnp.split(x, 2, axis=-1)
    return jnp.concatenate([x1 * cos - x2 * sin, x1 * sin + x2 * cos], axis=-1).astype(x.dtype)


def _heads(t, n_heads):
    B, L, W = t.shape
    return t.reshape(B, L, n_heads, W // n_heads).transpose(0, 2, 1, 3)


def _retention_chunkwise(q, k, v, log_gamma, s0):
    B, H, L, dk = q.shape
    dv = v.shape[-1]
    n_chunks = L // RET_CHUNK

    def chunks(t):
        return jnp.moveaxis(t.reshape(B, H, n_chunks, RET_CHUNK, t.shape[-1]), 2, 0)

    idx = jnp.arange(RET_CHUNK, dtype=jnp.float32)
    lg = log_gamma.astype(jnp.float32)[:, None]
    rel = idx[:, None] - idx[None, :]
    intra = jnp.where(rel >= 0, jnp.exp(lg[:, :, None] * jnp.maximum(rel, 0.0)), 0.0)
    q_dec = jnp.exp(lg * (idx + 1.0))[:, :, None]
    k_dec = jnp.exp(lg * (RET_CHUNK - 1.0 - idx))[:, :, None]
    c_dec = jnp.exp(lg * RET_CHUNK)[:, :, None]

    def step(state, qkv):
        qf, kf, vf = (t.astype(jnp.float32) for t in qkv)
        scores = jnp.einsum('bhid,bhjd->bhij', qf, kf) * intra
        o = (jnp.einsum('bhij,bhjv->bhiv', scores, vf)
             + jnp.einsum('bhid,bhdv->bhiv', qf, state) * q_dec)
        state = state * c_dec + jnp.einsum('bhjd,bhjv->bhdv', kf * k_dec, vf)
        return state, o

    s_fin, o = lax.scan(step, s0, (chunks(q), chunks(k), chunks(v)))
    o = jnp.moveaxis(o, 0, 2).reshape(B, H, L, dv)
    return o, s_fin


def _retention_final_state(k, v, log_gamma):
    L = k.shape[2]
    pos = jnp.arange(L, dtype=jnp.float32)
    w = jnp.exp(log_gamma.astype(jnp.float32)[:, None] * (L - 1.0 - pos))
    return jnp.einsum('bhld,bhlv->bhdv', k.astype(jnp.float32) * w[:, :, None], v.astype(jnp.float32))


def _flip(t):
    return jnp.flip(t, axis=2)


def _bidir_retention(q, k, v, log_gamma, s0_fwd, s0_bwd):
    o_f, s_f = _retention_chunkwise(q, k, v, log_gamma[0], s0_fwd)
    o_b, s_b = _retention_chunkwise(_flip(q), _flip(k), _flip(v), log_gamma[1], s0_bwd)
    return o_f + _flip(o_b), s_f, s_b


def _ret_out(o, gate, gn_g):
    B, H, L, dv = o.shape
    mu = jnp.mean(o, axis=-1, keepdims=True)
    var = jnp.mean(jnp.square(o - mu), axis=-1, keepdims=True)
    o = ((o - mu) * lax.rsqrt(var + EPS)).transpose(0, 2, 1, 3).reshape(B, L, H * dv)
    o = o * gn_g.astype(jnp.float32)
    return (o * jax.nn.silu(gate.astype(jnp.float32))).astype(gate.dtype)


def _dense_attention(q, k, v):
    s = jnp.einsum('bqkgd,bskd->bkgqs', q, k).astype(jnp.float32) * ATT_SCALE
    p = jax.nn.softmax(s, axis=-1).astype(v.dtype)
    return jnp.einsum('bkgqs,bskd->bqkgd', p, v)


def _latent_attention(q, k_lat, v_lat, k_ctx, v_ctx):
    B, L = q.shape[:2]
    k_all = jnp.concatenate([k_ctx, k_lat], axis=1)
    v_all = jnp.concatenate([v_ctx, v_lat], axis=1)
    qb = jnp.moveaxis(q.reshape(B, L // Q_BLOCK, Q_BLOCK, ATT_KV_HEADS, ATT_GROUP, ATT_HD), 1, 0)
    o = lax.map(lambda qblk: _dense_attention(qblk, k_all, v_all), qb)
    return jnp.moveaxis(o, 0, 1).reshape(B, L, ATT_W)


def _sqrelu_mlp(h, w1, w2):
    return jnp.square(jax.nn.relu(h @ w1)) @ w2


def _layer(x, xc, c_act, cc_act, w_mod, b_mod, norm1_g, norm2_g, w_in, w_out,
           ret_log_rate, ret_gn_g, q_norm_g, k_norm_g, w_ff1, w_ff2,
           ret_cs, att_cs, ctx_out):
    B, L, _ = x.shape
    Lc = xc.shape[1]
    mod = (c_act @ w_mod + b_mod)[:, None, :]
    mod_c = (cc_act @ w_mod + b_mod)[None, None, :]
    sh1, sc1, g1, sh2, sc2, g2 = jnp.split(mod, 6, axis=-1)
    csh1, csc1, cg1, csh2, csc2, cg2 = jnp.split(mod_c, 6, axis=-1)
    log_gamma = jnp.log1p(-jnp.exp(ret_log_rate.astype(jnp.float32)))

    h = _modulate(_rmsnorm(x, norm1_g), sh1, sc1)
    hc = _modulate(_rmsnorm(xc, norm1_g), csh1, csc1)
    rq, rk, rv, rg, aq, ak, av = jnp.split(h @ w_in, SPLIT_IDX, axis=-1)
    if ctx_out:
        crq, crk, crv, crg, caq, cak, cav = jnp.split(hc @ w_in, SPLIT_IDX, axis=-1)
    else:
        crk, crv = jnp.split(hc @ w_in[:, RK_OFF:RG_OFF], 2, axis=-1)
        cak, cav = jnp.split(hc @ w_in[:, AK_OFF:D_IN], 2, axis=-1)

    crk_h = _heads(crk, RET_HEADS) * RET_SCALE
    crv_h = _heads(crv, RET_HEADS)
    if ctx_out:
        zeros = jnp.zeros((B, RET_HEADS, RET_DK, RET_DV), jnp.float32)
        ret_c, s_f, s_b = _bidir_retention(_heads(crq, RET_HEADS), crk_h, crv_h, log_gamma, zeros, zeros)
    else:
        s_f = _retention_final_state(crk_h, crv_h, log_gamma[0])
        s_b = _retention_final_state(_flip(crk_h), _flip(crv_h), log_gamma[1])
    rq_h = _apply_rope(_heads(rq, RET_HEADS), *ret_cs)
    rk_h = _apply_rope(_heads(rk, RET_HEADS), *ret_cs) * RET_SCALE
    ret, _, _ = _bidir_retention(rq_h, rk_h, _heads(rv, RET_HEADS), log_gamma, s_f, s_b)
    ret = _ret_out(ret, rg, ret_gn_g)

    q = _apply_rope(_rmsnorm(aq.reshape(B, L, ATT_HEADS, ATT_HD), q_norm_g), *att_cs)
    k = _apply_rope(_rmsnorm(ak.reshape(B, L, ATT_KV_HEADS, ATT_HD), k_norm_g), *att_cs)
    v = av.reshape(B, L, ATT_KV_HEADS, ATT_HD)
    kc = _rmsnorm(cak.reshape(B, Lc, ATT_KV_HEADS, ATT_HD), k_norm_g)
    vc = cav.reshape(B, Lc, ATT_KV_HEADS, ATT_HD)
    att = _latent_attention(q.reshape(B, L, ATT_KV_HEADS, ATT_GROUP, ATT_HD), k, v, kc, vc)

    x = x + g1 * (jnp.concatenate([ret, att], axis=-1) @ w_out)

    h2 = _modulate(_rmsnorm(x, norm2_g), sh2, sc2)
    x = x + g2 * _sqrelu_mlp(h2, w_ff1, w_ff2)

    if ctx_out:
        ret_c = _ret_out(ret_c, crg, ret_gn_g)
        qc = _rmsnorm(caq.reshape(B, Lc, ATT_KV_HEADS, ATT_GROUP, ATT_HD), q_norm_g)
        att_c = _dense_attention(qc, kc, vc).reshape(B, Lc, ATT_W)
        xc = xc + cg1 * (jnp.concatenate([ret_c, att_c], axis=-1) @ w_out)
        hc2 = _modulate(_rmsnorm(xc, norm2_g), csh2, csc2)
        xc = xc + cg2 * _sqrelu_mlp(hc2, w_ff1, w_ff2)
    return x, xc


def setup_inputs(seed: int = 0) -> dict:
    key = jax.random.key(seed)
    ks = jax.random.split(key, 17)
    f32 = jnp.float32
    nrm = lambda k, shape, s: jax.random.normal(k, shape, f32) * s
    base_rate = -(5.0 + jnp.arange(RET_HEADS, dtype=f32)) * np.float32(np.log(2.0))
    return {
        "x": nrm(ks[0], (BATCH, SEQ, D_MODEL), 1.0),
        "c": nrm(ks[1], (BATCH, D_MODEL), 1.0),
        "ctx": nrm(ks[2], (BATCH, CTX_LEN, D_MODEL), 1.0),
        "c_ctx": nrm(ks[3], (D_MODEL,), 1.0),
        "w_mod": nrm(ks[4], (DEPTH, D_MODEL, 6 * D_MODEL), 0.01),
        "b_mod": nrm(ks[5], (DEPTH, 6 * D_MODEL), 0.02),
        "norm1_g": 1.0 + nrm(ks[6], (DEPTH, D_MODEL), 0.02),
        "norm2_g": 1.0 + nrm(ks[7], (DEPTH, D_MODEL), 0.02),
        "w_in": nrm(ks[8], (DEPTH, D_MODEL, D_IN), D_MODEL ** -0.5),
        "w_out": nrm(ks[9], (DEPTH, MIX_W, D_MODEL), MIX_W ** -0.5),
        "ret_log_rate": base_rate[None, None, :] + nrm(ks[10], (DEPTH, 2, RET_HEADS), 0.05),
        "ret_gn_g": 1.0 + nrm(ks[11], (DEPTH, RET_W), 0.02),
        "q_norm_g": 1.0 + nrm(ks[12], (DEPTH, ATT_HD), 0.02),
        "k_norm_g": 1.0 + nrm(ks[13], (DEPTH, ATT_HD), 0.02),
        "w_ff1": nrm(ks[14], (DEPTH, D_MODEL, D_FF), D_MODEL ** -0.5),
        "w_ff2": nrm(ks[15], (DEPTH, D_FF, D_MODEL), D_FF ** -0.5),
        "final_norm_g": 1.0 + nrm(ks[16], (D_MODEL,), 0.02),
    }


def reference(x, c, ctx, c_ctx, w_mod, b_mod, norm1_g, norm2_g, w_in, w_out,
              ret_log_rate, ret_gn_g, q_norm_g, k_norm_g, w_ff1, w_ff2, final_norm_g):
    L = x.shape[1]
    ROWS = L // GRID_W
    t = jnp.arange(L, dtype=jnp.float32)
    ret_ang = t[:, None] * _freqs(RET_DK // 2)
    ret_cs = (jnp.cos(ret_ang), jnp.sin(ret_ang))
    row = jnp.repeat(jnp.arange(ROWS, dtype=jnp.float32), GRID_W)
    col = jnp.tile(jnp.arange(GRID_W, dtype=jnp.float32), ROWS)
    af = _freqs(ATT_HD // 4)
    att_ang = jnp.concatenate([row[:, None] * af, col[:, None] * af], axis=-1)[:, None, :]
    att_cs = (jnp.cos(att_ang), jnp.sin(att_ang))

    c_act = jax.nn.silu(c)
    cc_act = jax.nn.silu(c_ctx)
    xc = ctx
    for l in range(DEPTH):
        x, xc = _layer(x, xc, c_act, cc_act, w_mod[l], b_mod[l], norm1_g[l], norm2_g[l],
                       w_in[l], w_out[l], ret_log_rate[l], ret_gn_g[l], q_norm_g[l], k_norm_g[l],
                       w_ff1[l], w_ff2[l], ret_cs, att_cs, l < DEPTH - 1)
    return _rmsnorm(x, final_norm_g)
```

```python
import numpy as np
import concourse.bass as bass
import concourse.mybir as mybir
from contextlib import ExitStack
from concourse.bass_utils import run_bass_kernel_spmd

F32 = mybir.dt.float32
BF16 = mybir.dt.bfloat16
AF = mybir.ActivationFunctionType
ALU = mybir.AluOpType
AX = mybir.AxisListType


class Dep:
    __slots__ = ("w", "readers", "name", "excl")

    def __init__(self, name="", excl=False):
        self.w = None
        self.readers = {}
        self.name = name
        self.excl = excl


class V:
    __slots__ = ("ap", "deps")

    def __init__(self, ap, deps):
        self.ap = ap
        self.deps = deps


class Tile:
    def __init__(self, handle, deps):
        self.h = handle
        self.deps = deps

    def __getitem__(self, idx):
        return V(self.h[idx], self.deps)

    def v(self, ap):
        return V(ap, self.deps)


class Op:
    __slots__ = ("eng", "fn", "waits", "marked", "tick", "dma_sem", "dma_val", "key")


class Kern:
    ENGS = ("pe", "act", "dve", "pool", "sp")

    def __init__(self, nc, es):
        self.nc = nc
        self.es = es
        self.ops = []
        self.eng_obj = {"pe": nc.tensor, "act": nc.scalar, "dve": nc.vector,
                        "pool": nc.gpsimd, "sp": nc.sync}
        self.eng_sem = {e: es.enter_context(nc.semaphore("sem_" + e)) for e in self.ENGS}
        self.dma_sems = {}
        self.dma_cnt = {}
        self.last_op = {e: None for e in self.ENGS}
        self.same_engine_sync = True

    def sb(self, name, shape, dt, es=None):
        h = (es or self.es).enter_context(self.nc.sbuf_tensor(name, list(shape), dt))
        return Tile(h, [Dep(name)])

    def ps(self, name, shape, dt):
        h = self.es.enter_context(self.nc.psum_tensor(name, list(shape), dt))
        return Tile(h, [Dep(name, excl=True)])

    def dsem(self, name):
        s = self.es.enter_context(self.nc.semaphore("dq_" + name))
        self.dma_sems[name] = s
        self.dma_cnt[name] = 0
        return name

    def _add_wait(self, op, prod):
        if prod is None or prod is op:
            return
        if prod.dma_sem is None:
            if prod.eng == op.eng and op.dma_sem is None:
                if op.eng in ("pe", "sp"):
                    return
                if not self.same_engine_sync:
                    return
            prod.marked = True
        op.waits.append(prod)

    def _record(self, eng, fn, reads, writes, dma_sem=None):
        op = Op()
        op.eng = eng
        op.fn = fn
        op.waits = []
        op.marked = False
        op.tick = None
        op.dma_sem = dma_sem
        op.dma_val = None
        if dma_sem is not None:
            self.dma_cnt[dma_sem] += 16
            op.dma_val = self.dma_cnt[dma_sem]
            op.key = ("dma", dma_sem)
        else:
            op.key = eng
        rd = []
        wd = []
        for v in reads:
            for d in v.deps:
                if d.excl:
                    if d not in wd:
                        wd.append(d)
                elif d not in rd:
                    rd.append(d)
        for v in writes:
            for d in v.deps:
                if d not in wd:
                    wd.append(d)
        for d in rd:
            self._add_wait(op, d.w)
        for d in wd:
            self._add_wait(op, d.w)
            for r in d.readers.values():
                self._add_wait(op, r)
        for d in rd:
            if d not in wd:
                d.readers[op.key] = op
        for d in wd:
            d.w = op
            d.readers = {}
        self.ops.append(op)
        if dma_sem is None:
            self.last_op[eng] = op
        return op

    def I(self, eng, method, *, r=(), w=(), **kw):
        reads = list(r)
        writes = list(w)
        args = {}
        for k, v in kw.items():
            if isinstance(v, V):
                args[k] = v.ap
                if k in ("out", "accum_out", "ap"):
                    writes.append(v)
                else:
                    reads.append(v)
            else:
                args[k] = v

        def fn(e, method=method, args=args):
            return getattr(e, method)(**args)

        return self._record(eng, fn, reads, writes)

    def mm(self, out, lhsT, rhs, start=True, stop=True):
        def fn(e, o=out.ap, l=lhsT.ap, r_=rhs.ap, s=start, t=stop):
            return e.matmul(o, lhsT=l, rhs=r_, start=s, stop=t)
        return self._record("pe", fn, [lhsT, rhs], [out])

    def tr(self, out, in_, ident):
        def fn(e, o=out.ap, i=in_.ap, d=ident.ap):
            return e.transpose(o, i, d)
        return self._record("pe", fn, [in_, ident], [out])

    def dma(self, queue, out, in_, sem, **dkw):
        reads = [in_] if isinstance(in_, V) else []
        writes = [out] if isinstance(out, V) else []
        o = out.ap if isinstance(out, V) else out
        i = in_.ap if isinstance(in_, V) else in_

        def fn(e, o=o, i=i, dkw=dkw):
            return e.dma_start(out=o, in_=i, **dkw)
        return self._record(queue, fn, reads, writes, dma_sem=sem)

    def barrier(self):
        lasts = []
        for e in self.ENGS:
            if self.last_op[e] is not None and self.last_op[e].dma_sem is None:
                lasts.append(self.last_op[e])
        dmas = {}
        for op in self.ops:
            if op.dma_sem is not None:
                dmas[op.dma_sem] = op
        for e in ("pe", "act", "dve", "pool", "sp"):
            op = Op()
            op.eng = e
            op.fn = None
            op.waits = []
            op.marked = False
            op.tick = None
            op.dma_sem = None
            op.dma_val = None
            op.key = e
            for p in lasts:
                if p.eng != e or e not in ("pe", "sp"):
                    p.marked = True
                    op.waits.append(p)
            for p in dmas.values():
                op.waits.append(p)
            self.ops.append(op)

    def emit(self):
        cnt = {e: 0 for e in self.ENGS}
        seen = {e: {} for e in self.ENGS}
        n_wait = 0
        for op in self.ops:
            e = self.eng_obj[op.eng]
            for p in op.waits:
                if p.dma_sem is not None:
                    ck = ("dma", p.dma_sem)
                    val = p.dma_val
                    sem = self.dma_sems[p.dma_sem]
                else:
                    ck = p.eng
                    val = p.tick
                    sem = self.eng_sem[p.eng]
                    assert val is not None, "waiting on unticked op"
                if seen[op.eng].get(ck, 0) >= val:
                    continue
                seen[op.eng][ck] = val
                e.wait_ge(sem, val)
                n_wait += 1
            if op.fn is None:
                continue
            ins = op.fn(e)
            if op.dma_sem is not None:
                ins.then_inc(self.dma_sems[op.dma_sem], 16)
            elif op.marked:
                cnt[op.eng] += 1
                op.tick = cnt[op.eng]
                ins.then_inc(self.eng_sem[op.eng], 1)
        return n_wait, cnt

    def finish(self, queue="sp"):
        e = self.eng_obj[queue]
        for name, c in self.dma_cnt.items():
            if c > 0:
                e.wait_ge(self.dma_sems[name], c)

L = 4096
NT = 32
D = 1024
EPS = 1e-6
RET_SCALE = 128 ** -0.5
ATT_SCALE = 64 ** -0.5
NKT = 34
NA = 32
NB = 32
NC = 8


def build_nc():
    nc = bass.Bass("TRN2", target_bir_lowering=False)

    def din(name, shape):
        return nc.dram_tensor(name, list(shape), F32, kind="ExternalInput").ap()

    x = din("x", [L, D])
    ctx = din("ctx", [256, D])
    cvec = din("cvec", [2, D])
    w_mod = din("w_mod", [D, 6144])
    b_mod = din("b_mod", [6144])
    n1g = din("norm1_g", [D])
    n2g = din("norm2_g", [D])
    w_in = din("w_in", [D, 2816])
    w_out = din("w_out", [D, D])
    rate = din("rate", [8])
    gng = din("ret_gn_g", [512])
    qng = din("q_norm_g", [64])
    kng = din("k_norm_g", [64])
    w1 = din("w_ff1", [D, 4096])
    w2 = din("w_ff2", [4096, D])
    fng = din("final_norm_g", [D])
    rope_r = din("rope_r", [L, 128])
    rope_a = din("rope_a", [L, 64])
    cmat = din("cmat", [128, 6, 128])
    pcols = din("pcols", [128, 4])
    ident = din("ident", [128, 128])
    out = nc.dram_tensor("out", [L, D], F32, kind="ExternalOutput").ap()
    x1s = nc.dram_tensor("x1s", [L, D], F32).ap()
    h2s = nc.dram_tensor("h2s", [128, 8, L], BF16).ap()

    with ExitStack() as es:
        K = Kern(nc, es)
        I = K.I
        idb = K.sb("idb", [128, 128], BF16)
        ones = K.sb("ones", [128, 128], BF16)
        g2_t = K.sb("g2_t", [128, 1024], F32)
        gf_b = K.sb("gf_b", [128, 1024], F32)
        ssq = K.sb("ssq", [128, 8], F32)
        rst = K.sb("rst", [128, 8], F32)
        es_ab = ExitStack()
        es.enter_context(es_ab)
        K.es_save = K.es
        K.es = es_ab
        modb = K.sb("modb", [128, 5120], F32)
        gn_b = K.sb("gn_b", [128, 512], F32)
        qg_b = K.sb("qg_b", [128, 64], F32)
        kg_b = K.sb("kg_b", [128, 64], F32)
        intraT = K.sb("intraT", [128, 2, 512], F32)
        QD = K.sb("QD", [128, 2, 512], F32)
        KD = K.sb("KD", [128, 2, 4], F32)
        cdec = K.sb("cdec", [128, 8], F32)
        wc = K.sb("wc", [128, 2, 2, 4], F32)
        SbAll = K.sb("SbAll", [128, NT, 512], BF16)
        attKT = K.sb("attKT", [128, NKT * 128], BF16)
        attV = K.sb("attV", [128, NKT, 128], BF16)
        Sf = K.sb("Sf", [128, 512], F32)
        Sf_bf = K.sb("Sf_bf", [128, 512], BF16)
        Sb = K.sb("Sb", [128, 512], F32)
        K.es = K.es_save
        es_c = ExitStack()
        es_ab.enter_context(es_c)
        cmodb = K.sb("cmodb", [128, 2048], F32, es_c)
        psT = K.ps("psT", [128, 1024], BF16)
        psT2 = K.ps("psT2", [128, 1024], BF16)
        F = [K.ps("F%d" % i, [128, 512], F32) for i in range(6)]
        for nm in ["x", "rp", "rp2", "w0", "w1", "w2", "w3", "w4", "v0", "v1", "v2", "v3", "st1", "st2", "out", "hl", "xl"] + ["cst%d" % i for i in range(2, 15)]:
            K.dsem(nm)

        def rstd_from(acc_v, n, out_v):
            I("act", "activation", out=out_v, in_=acc_v, func=AF.Ln, scale=1.0 / n, bias=EPS)
            I("act", "activation", out=out_v, in_=out_v, func=AF.Exp, scale=-0.5)

        def row_rstd(src_v, junk_v, n):
            I("dve", "memset", ap=ssq[:, 0:1], constant=0.0)
            I("act", "activation", out=junk_v, in_=src_v, func=AF.Square, accum_out=ssq[:, 0:1])
            rstd_from(ssq[:, 0:1], n, rst[:, 0:1])

        with ExitStack() as s0:
            idf = K.sb("idf", [128, 128], F32, s0)
            cc = K.sb("cc", [128, 16], F32, s0)
            cact = K.sb("cact", [128, 16], F32, s0)
            cb = K.sb("cb", [128, 16, 128], BF16, s0)
            wms = [K.sb("wms%d" % i, [128, 8, 512], BF16, s0) for i in range(2)]
            bmod_b = K.sb("bmod_b", [128, 6144], F32, s0)
            n1g_b = K.sb("n1g_b", [128, 1024], F32, s0)
            n2g_b = K.sb("n2g_b", [128, 1024], F32, s0)
            rate_b = K.sb("rate_b", [128, 8], F32, s0)
            lg = K.sb("lg", [128, 8], F32, s0)
            cm = K.sb("cm", [128, 6, 128], F32, s0)
            pc = K.sb("pc", [128, 4], F32, s0)
            tmpe = K.sb("tmpe", [128, 128], F32, s0)
            K.dma("sp", idf[:], ident[:, :], "cst2")
            K.dma("sp", cc[:, 0:8], cvec[0, :].rearrange("(k p) -> p k", p=128), "cst3", allow_slow_non_contiguous=True)
            K.dma("sp", cc[:, 8:16], cvec[1, :].rearrange("(k p) -> p k", p=128), "cst4", allow_slow_non_contiguous=True)
            K.dma("sp", bmod_b[:], b_mod.partition_broadcast(128), "cst5")
            K.dma("sp", n1g_b[:], n1g.partition_broadcast(128), "cst6")
            K.dma("sp", n2g_b[:], n2g.partition_broadcast(128), "cst7")
            K.dma("sp", gf_b[:], fng.partition_broadcast(128), "cst8")
            K.dma("sp", gn_b[:], gng.partition_broadcast(128), "cst9")
            K.dma("sp", qg_b[:], qng.partition_broadcast(128), "cst10")
            K.dma("sp", kg_b[:], kng.partition_broadcast(128), "cst11")
            K.dma("sp", rate_b[:], rate.partition_broadcast(128), "cst12")
            K.dma("sp", cm[:], cmat[:, :, :], "cst13")
            K.dma("sp", pc[:], pcols[:, :], "cst14")
            I("dve", "tensor_copy", out=idb[:], in_=idf[:])
            I("dve", "memset", ap=ones[:], constant=1.0)
            I("act", "activation", out=cact[:], in_=cc[:], func=AF.Silu)
            for j in range(16):
                I("dve", "tensor_copy", out=cb[:, j, :], in_=cact.v(cact.h[:, j:j + 1].to_broadcast([128, 128])))
            wmv = w_mod.rearrange("(k p) n -> p k n", p=128)
            for jb in range(12):
                ws = wms[jb % 2]
                K.dma("pool", ws[:], wmv[:, :, jb * 512:(jb + 1) * 512], "w%d" % (jb % 2))
                for k in range(8):
                    K.mm(F[0][:], cb[:, k, :], ws[:, k, :], start=(k == 0), stop=(k == 7))
                dst = modb[:, jb * 512:(jb + 1) * 512] if jb < 10 else g2_t[:, (jb - 10) * 512:(jb - 9) * 512]
                I("dve", "tensor_tensor", out=dst, in0=F[0][:], in1=bmod_b[:, jb * 512:(jb + 1) * 512], op=ALU.add)
                if jb < 4:
                    for k in range(8):
                        K.mm(F[1][:], cb[:, 8 + k, :], ws[:, k, :], start=(k == 0), stop=(k == 7))
                    I("dve", "tensor_tensor", out=cmodb[:, jb * 512:(jb + 1) * 512], in0=F[1][:], in1=bmod_b[:, jb * 512:(jb + 1) * 512], op=ALU.add)
            I("dve", "scalar_tensor_tensor", out=modb[:, 1024:2048], in0=modb[:, 1024:2048], scalar=1.0, in1=n1g_b[:], op0=ALU.add, op1=ALU.mult)
            I("dve", "scalar_tensor_tensor", out=modb[:, 4096:5120], in0=modb[:, 4096:5120], scalar=1.0, in1=n2g_b[:], op0=ALU.add, op1=ALU.mult)
            I("dve", "scalar_tensor_tensor", out=cmodb[:, 1024:2048], in0=cmodb[:, 1024:2048], scalar=1.0, in1=n1g_b[:], op0=ALU.add, op1=ALU.mult)
            I("act", "activation", out=lg[:], in_=rate_b[:], func=AF.Exp)
            I("act", "activation", out=lg[:], in_=lg[:], func=AF.Ln, scale=-1.0, bias=1.0)
            for d in range(2):
                for h in range(4):
                    c = d * 4 + h
                    sc = lg[:, c:c + 1]
                    I("act", "activation", out=tmpe[:], in_=cm[:, 2 * d, :], func=AF.Exp, scale=sc)
                    I("dve", "tensor_tensor", out=intraT[:, d, h * 128:(h + 1) * 128], in0=tmpe[:], in1=cm[:, 2 * d + 1, :], op=ALU.mult)
                    I("act", "activation", out=QD[:, d, h * 128:(h + 1) * 128], in_=cm[:, 4 + d, :], func=AF.Exp, scale=sc)
                    I("act", "activation", out=KD[:, d, h:h + 1], in_=pc[:, (1 if d == 0 else 0):(2 if d == 0 else 1)], func=AF.Exp, scale=sc)
                    for t in range(2):
                        col = (2 if t == 0 else 1) if d == 0 else (0 if t == 0 else 3)
                        I("act", "activation", out=wc[:, t, d, h:h + 1], in_=pc[:, col:col + 1], func=AF.Exp, scale=sc)
            I("act", "activation", out=cdec[:], in_=lg[:], func=AF.Exp, scale=128.0)
            I("dve", "tensor_scalar", out=KD[:], in0=KD[:], scalar1=RET_SCALE, scalar2=None, op0=ALU.mult)
            I("dve", "tensor_scalar", out=wc[:], in0=wc[:], scalar1=RET_SCALE, scalar2=None, op0=ALU.mult)
            K.barrier()

        sh1_b = modb[:, 0:1024]; A1_b = modb[:, 1024:2048]; g1_b = modb[:, 2048:3072]
        sh2_b = modb[:, 3072:4096]; A2_b = modb[:, 4096:5120]; g2_b = g2_t[:, :]
        csh1_b = cmodb[:, 0:1024]; cA1_b = cmodb[:, 1024:2048]

        def norm_mod_T(xt, A_v, sh_v, tmp, hm, hT):
            row_rstd(xt[:], tmp[:], 1024.0)
            I("dve", "scalar_tensor_tensor", out=tmp[:], in0=xt[:], scalar=rst[:, 0:1], in1=A_v, op0=ALU.mult, op1=ALU.mult)
            I("dve", "tensor_tensor", out=hm[:], in0=tmp[:], in1=sh_v, op=ALU.add)
            for k in range(8):
                K.tr(psT[:, k * 128:(k + 1) * 128], hm[:, k * 128:(k + 1) * 128], idb[:])
            I("act", "activation", out=hT.v(hT.h[:].rearrange("p a b -> p (a b)")), in_=psT[:], func=AF.Copy)

        def v3(t, ap):
            return t.v(ap)

        def rope(src, H, half, cos_v, sin_v, t1, t2, dst_lo, dst_hi):
            s_lo = V(src.ap[:, :, 0:half], src.deps)
            s_hi = V(src.ap[:, :, half:2 * half], src.deps)
            cb_ = V(cos_v.ap.unsqueeze(1).to_broadcast([128, H, half]), cos_v.deps)
            sb_ = V(sin_v.ap.unsqueeze(1).to_broadcast([128, H, half]), sin_v.deps)
            I("dve", "tensor_tensor", out=t1, in0=s_lo, in1=cb_, op=ALU.mult)
            I("dve", "tensor_tensor", out=t2, in0=s_hi, in1=sb_, op=ALU.mult)
            I("dve", "tensor_tensor", out=dst_lo, in0=t1, in1=t2, op=ALU.subtract)
            I("dve", "tensor_tensor", out=t1, in0=s_lo, in1=sb_, op=ALU.mult)
            I("dve", "tensor_tensor", out=t2, in0=s_hi, in1=cb_, op=ALU.mult)
            I("dve", "tensor_tensor", out=dst_hi, in0=t1, in1=t2, op=ALU.add)

        def head_rstd(src_v, H, n, sq, nh):
            I("act", "activation", out=sq, in_=src_v, func=AF.Square)
            I("dve", "tensor_reduce", out=ssq[:, 0:H], in_=sq, axis=AX.X, op=ALU.add)
            rstd_from(ssq[:, 0:H], float(n), rst[:, 0:H])

        with ExitStack() as s1:
            wA = K.sb("wA", [128, 8, 1280], BF16, s1)
            xt = K.sb("xtA", [128, 1024], F32, s1)
            tmp = K.sb("tmpA", [128, 1024], F32, s1)
            hm = K.sb("hmA", [128, 1024], BF16, s1)
            hT = K.sb("hTA", [128, 8, 128], BF16, s1)
            cr = K.sb("crA", [128, 128], F32, s1)
            ca = K.sb("caA", [128, 64], F32, s1)
            kr = K.sb("krA", [128, 4, 128], F32, s1)
            t1 = K.sb("t1A", [128, 4, 64], F32, s1)
            t2 = K.sb("t2A", [128, 4, 64], F32, s1)
            kdb = [K.sb("kdbA%d" % i, [128, 4, 128], BF16, s1) for i in range(2)]
            kdf = [K.sb("kdfA%d" % i, [128, 4, 128], BF16, s1) for i in range(2)]
            vbf = [K.sb("vbfA%d" % i, [128, 4, 128], BF16, s1) for i in range(2)]
            sqk = K.sb("sqkA", [128, 2, 64], F32, s1)
            kn = K.sb("knA", [128, 2, 64], F32, s1)
            kro = K.sb("kroA", [128, 2, 64], BF16, s1)
            wiv = w_in.rearrange("(k p) n -> p k n", p=128)
            K.dma("pool", wA[:, :, 0:1024], wiv[:, :, 512:1536], "w0")
            K.dma("pool", wA[:, :, 1024:1280], wiv[:, :, 2560:2816], "w1")

            def tileA(src_ap, is_ctx, idx, kt):
                K.dma("sp", xt[:], src_ap, "x")
                if not is_ctx:
                    K.dma("sp", cr[:], rope_r[idx * 128:(idx + 1) * 128, :], "rp")
                    K.dma("sp", ca[:], rope_a[idx * 128:(idx + 1) * 128, :], "rp2")
                norm_mod_T(xt, cA1_b if is_ctx else A1_b, csh1_b if is_ctx else sh1_b, tmp, hm, hT)
                for k in range(8):
                    K.mm(F[0][:], hT[:, k, :], wA[:, k, 0:512], start=(k == 0), stop=(k == 7))
                for k in range(8):
                    K.mm(F[1][:], hT[:, k, :], wA[:, k, 512:1024], start=(k == 0), stop=(k == 7))
                for k in range(8):
                    K.mm(F[2][:, 0:256], hT[:, k, :], wA[:, k, 1024:1280], start=(k == 0), stop=(k == 7))
                akv = F[2].v(F[2].h[:, 0:128].rearrange("p (h d) -> p h d", h=2))
                head_rstd(akv, 2, 64, sqk[:], 2)
                I("dve", "tensor_tensor", out=kn[:], in0=akv, in1=rst.v(rst.h[:, 0:2].unsqueeze(2).to_broadcast([128, 2, 64])), op=ALU.mult)
                I("dve", "tensor_tensor", out=kn[:], in0=kn[:], in1=kg_b.v(kg_b.h[:].unsqueeze(1).to_broadcast([128, 2, 64])), op=ALU.mult)
                if is_ctx:
                    I("dve", "tensor_copy", out=kro[:], in_=kn[:])
                else:
                    rope(kn[:], 2, 32, ca[:, 0:32], ca[:, 32:64], t1.v(t1.h[:, 0:2, 0:32]), t2.v(t2.h[:, 0:2, 0:32]),
                         kro[:, :, 0:32], kro[:, :, 32:64])
                K.tr(psT2[:, 0:128], kro.v(kro.h[:].rearrange("p h d -> p (h d)")), idb[:])
                I("act", "activation", out=attKT[:, kt * 128:(kt + 1) * 128], in_=psT2[:, 0:128], func=AF.Copy)
                I("dve", "tensor_copy", out=attV[:, kt, :], in_=F[2][:, 128:256])
                rk4 = F[0].v(F[0].h[:].rearrange("p (h d) -> p h d", h=4))
                if is_ctx:
                    I("dve", "tensor_copy", out=kr[:], in_=rk4)
                else:
                    rope(rk4, 4, 64, cr[:, 0:64], cr[:, 64:128], t1[:], t2[:], kr[:, :, 0:64], kr[:, :, 64:128])
                b = idx % 2 if is_ctx else 0
                I("act", "activation", out=vbf[b].v(vbf[b].h[:].rearrange("p h d -> p (h d)")), in_=F[1][:], func=AF.Copy)
                if is_ctx:
                    I("dve", "tensor_tensor", out=kdf[b][:], in0=kr[:], in1=wc.v(wc.h[:, idx, 0, :].unsqueeze(2).to_broadcast([128, 4, 128])), op=ALU.mult)
                    I("dve", "tensor_tensor", out=kdb[b][:], in0=kr[:], in1=wc.v(wc.h[:, idx, 1, :].unsqueeze(2).to_broadcast([128, 4, 128])), op=ALU.mult)
                else:
                    I("dve", "tensor_tensor", out=kdb[0][:], in0=kr[:], in1=KD.v(KD.h[:, 1, :].unsqueeze(2).to_broadcast([128, 4, 128])), op=ALU.mult)
                    for h in range(4):
                        K.mm(F[3][:, h * 128:(h + 1) * 128], kdb[0][:, h, :], vbf[0][:, h, :])
                    I("act", "activation", out=SbAll[:, idx, :], in_=Sb[:], func=AF.Copy)
                    for h in range(4):
                        I("dve", "scalar_tensor_tensor", out=Sb[:, h * 128:(h + 1) * 128], in0=Sb[:, h * 128:(h + 1) * 128],
                          scalar=cdec[:, 4 + h:5 + h], in1=F[3][:, h * 128:(h + 1) * 128], op0=ALU.mult, op1=ALU.add)

            tileA(ctx[0:128, :], True, 0, 0)
            tileA(ctx[128:256, :], True, 1, 1)
            for h in range(4):
                for t in range(2):
                    K.mm(F[3][:, h * 128:(h + 1) * 128], kdf[t][:, h, :], vbf[t][:, h, :], start=(t == 0), stop=(t == 1))
            for h in range(4):
                for t in range(2):
                    K.mm(F[4][:, h * 128:(h + 1) * 128], kdb[t][:, h, :], vbf[t][:, h, :], start=(t == 0), stop=(t == 1))
            I("dve", "tensor_copy", out=Sf[:], in_=F[3][:])
            I("dve", "tensor_copy", out=Sb[:], in_=F[4][:])
            I("dve", "tensor_copy", out=Sf_bf[:], in_=Sf[:])
            for t in range(NT - 1, NT - 1 - NA, -1):
                tileA(x[t * 128:(t + 1) * 128, :], False, t, 2 + t)
            K.barrier()

        es_c.close()
        with ExitStack() as s2:
            wi = K.sb("wi", [128, 8, 2560], BF16, s2)
            wo_r = K.sb("wo_r", [128, 4, 1024], BF16, s2)
            wo_a = K.sb("wo_a", [128, 4, 1024], BF16, s2)
            xt = K.sb("xtB", [128, 1024], F32, s2)
            tmp = K.sb("tmpB", [128, 1024], F32, s2)
            hm = K.sb("hmB", [128, 1024], BF16, s2)
            hT = K.sb("hTB", [128, 8, 128], BF16, s2)
            cr = K.sb("crB", [128, 128], F32, s2)
            ca = K.sb("caB", [128, 64], F32, s2)
            kr = K.sb("krB", [128, 4, 128], F32, s2)
            t1 = K.sb("t1B", [128, 8, 64], F32, s2)
            t2 = K.sb("t2B", [128, 8, 64], F32, s2)
            qr = K.sb("qrB", [128, 4, 128], BF16, s2)
            k_s = K.sb("k_sB", [128, 4, 128], BF16, s2)
            kdf = K.sb("kdfB", [128, 4, 128], BF16, s2)
            vbf = K.sb("vbfB", [128, 4, 128], BF16, s2)
            gate = K.sb("gateB", [128, 512], F32, s2)
            qro = K.sb("qroB", [128, 8, 64], BF16, s2)
            qrp = K.sb("qrpB", [128, 4, 2, 64], BF16, s2)
            qT = K.sb("qTB", [128, 512], BF16, s2)
            qdTf = K.sb("qdTfB", [128, 512], BF16, s2)
            qdTb = K.sb("qdTbB", [128, 512], BF16, s2)
            kT = K.sb("kTB", [128, 512], BF16, s2)
            QTg = [K.sb("QTg%dB" % g, [128, 512], BF16, s2) for g in range(2)]
            STf = K.sb("STfB", [128, 512], BF16, s2)
            STb = K.sb("STbB", [128, 512], BF16, s2)
            o_sb = K.sb("o_sbB", [128, 4, 128], F32, s2)
            st4 = K.sb("st4B", [128, 16], F32, s2)
            mixr = K.sb("mixrB", [128, 512], BF16, s2)
            mixT_r = K.sb("mixT_rB", [128, 4, 128], BF16, s2)
            attT = K.sb("attTB", [128, 4, 128], BF16, s2)
            PT = [K.sb("PT%dB" % i, [128, 512], BF16, s2) for i in range(2)]
            x1 = K.sb("x1B", [128, 1024], F32, s2)
            sq8 = kr.v(kr.h[:].rearrange("p h (a d) -> p (h a) d", a=2))
            osq = kr
            qn = o_sb.v(o_sb.h[:].rearrange("p h (a d) -> p (h a) d", a=2))
            rs = gate
            h2T = hT
            wiv = w_in.rearrange("(k p) n -> p k n", p=128)
            K.dma("pool", wi[:, :, 0:1280], wiv[:, :, 0:1280], "w0")
            K.dma("pool", wi[:, :, 1280:2560], wiv[:, :, 1280:2560], "w1")
            K.dma("pool", wo_r[:], w_out[0:512, :].rearrange("(k p) n -> p k n", p=128), "w2")
            for g in range(2):
                K.dma("pool", wo_a[g * 64:(g + 1) * 64, :, :],
                      w_out[512 + g * 256:512 + (g + 1) * 256, :].rearrange("(pr d) n -> d pr n", d=64), "w%d" % (3 + g))
            I("dve", "memset", ap=QTg[0][:], constant=0.0)
            I("dve", "memset", ap=QTg[1][:], constant=0.0)

            for t in range(NB):
                K.dma("sp", xt[:], x[t * 128:(t + 1) * 128, :], "x")
                K.dma("sp", cr[:], rope_r[t * 128:(t + 1) * 128, :], "rp")
                K.dma("sp", ca[:], rope_a[t * 128:(t + 1) * 128, :], "rp2")
                norm_mod_T(xt, A1_b, sh1_b, tmp, hm, hT)
                for j in range(5):
                    for k in range(8):
                        K.mm(F[j][:], hT[:, k, :], wi[:, k, j * 512:(j + 1) * 512], start=(k == 0), stop=(k == 7))
                rq4 = F[0].v(F[0].h[:].rearrange("p (h d) -> p h d", h=4))
                rk4 = F[1].v(F[1].h[:].rearrange("p (h d) -> p h d", h=4))
                rope(rq4, 4, 64, cr[:, 0:64], cr[:, 64:128], t1.v(t1.h[:, 0:4, :]), t2.v(t2.h[:, 0:4, :]), qr[:, :, 0:64], qr[:, :, 64:128])
                rope(rk4, 4, 64, cr[:, 0:64], cr[:, 64:128], t1.v(t1.h[:, 0:4, :]), t2.v(t2.h[:, 0:4, :]), kr[:, :, 0:64], kr[:, :, 64:128])
                I("dve", "tensor_scalar", out=k_s[:], in0=kr[:], scalar1=RET_SCALE, scalar2=None, op0=ALU.mult)
                I("dve", "tensor_tensor", out=kdf[:], in0=kr[:], in1=KD.v(KD.h[:, 0, :].unsqueeze(2).to_broadcast([128, 4, 128])), op=ALU.mult)
                I("act", "activation", out=vbf.v(vbf.h[:].rearrange("p h d -> p (h d)")), in_=F[2][:], func=AF.Copy)
                I("act", "activation", out=gate[:], in_=F[3][:], func=AF.Exp, scale=-1.0)
                I("dve", "tensor_scalar", out=gate[:], in0=gate[:], scalar1=1.0, scalar2=None, op0=ALU.add)
                I("dve", "reciprocal", out=gate[:], in_=gate[:])
                I("dve", "tensor_tensor", out=gate[:], in0=F[3][:], in1=gate[:], op=ALU.mult)
                aq8 = F[4].v(F[4].h[:].rearrange("p (h d) -> p h d", h=8))
                head_rstd(aq8, 8, 64, sq8, 8)
                I("dve", "tensor_tensor", out=qn, in0=aq8, in1=rst.v(rst.h[:, 0:8].unsqueeze(2).to_broadcast([128, 8, 64])), op=ALU.mult)
                I("dve", "tensor_tensor", out=qn, in0=qn, in1=qg_b.v(qg_b.h[:].unsqueeze(1).to_broadcast([128, 8, 64])), op=ALU.mult)
                rope(qn, 8, 32, ca[:, 0:32], ca[:, 32:64], t1.v(t1.h[:, :, 0:32]), t2.v(t2.h[:, :, 0:32]), qro[:, :, 0:32], qro[:, :, 32:64])
                for e in range(2):
                    I("dve", "tensor_copy", out=qrp[:, :, e, :], in_=qro[:, e * 4:(e + 1) * 4, :])
                for h in range(4):
                    K.tr(psT[:, h * 128:(h + 1) * 128], qr[:, h, :], idb[:])
                for h in range(4):
                    K.tr(psT[:, 512 + h * 128:512 + (h + 1) * 128], k_s[:, h, :], idb[:])
                for pr in range(4):
                    K.tr(psT2[:, pr * 128:(pr + 1) * 128], qrp.v(qrp.h[:, pr, :, :].rearrange("p e d -> p (e d)")), idb[:])
                I("act", "activation", out=qT[:], in_=psT[:, 0:512], func=AF.Copy)
                I("dve", "tensor_tensor", out=qdTf[:], in0=psT[:, 0:512], in1=QD[:, 0, :], op=ALU.mult)
                I("dve", "tensor_tensor", out=qdTb[:], in0=psT[:, 0:512], in1=QD[:, 1, :], op=ALU.mult)
                I("act", "activation", out=kT[:], in_=psT[:, 512:1024], func=AF.Copy)
                I("dve", "tensor_copy", out=QTg[0][0:64, :], in_=psT2[0:64, 0:512])
                I("dve", "tensor_copy", out=QTg[1][64:128, :], in_=psT2[64:128, 0:512])
                for h in range(4):
                    hs = slice(h * 128, (h + 1) * 128)
                    K.mm(F[5][:, hs], kT[:, hs], qT[:, hs])
                I("dve", "tensor_tensor", out=STf[:], in0=F[5][:], in1=intraT[:, 0, :], op=ALU.mult)
                I("dve", "tensor_tensor", out=STb[:], in0=F[5][:], in1=intraT[:, 1, :], op=ALU.mult)
                for h in range(4):
                    hs = slice(h * 128, (h + 1) * 128)
                    K.mm(F[0][:, hs], kdf[:, h, :], vbf[:, h, :])
                for h in range(4):
                    hs = slice(h * 128, (h + 1) * 128)
                    K.mm(F[1][:, hs], STf[:, hs], vbf[:, h, :], start=True, stop=False)
                    K.mm(F[1][:, hs], STb[:, hs], vbf[:, h, :], start=False, stop=False)
                    K.mm(F[1][:, hs], qdTf[:, hs], Sf_bf[:, hs], start=False, stop=False)
                    K.mm(F[1][:, hs], qdTb[:, hs], SbAll[:, t, hs], start=False, stop=True)
                for h in range(4):
                    hs = slice(h * 128, (h + 1) * 128)
                    I("dve", "scalar_tensor_tensor", out=Sf[:, hs], in0=Sf[:, hs], scalar=cdec[:, h:h + 1], in1=F[0][:, hs], op0=ALU.mult, op1=ALU.add)
                I("dve", "tensor_copy", out=Sf_bf[:], in_=Sf[:])
                I("act", "activation", out=o_sb.v(o_sb.h[:].rearrange("p h d -> p (h d)")), in_=F[1][:], func=AF.Copy)
                I("dve", "tensor_reduce", out=st4[:, 0:4], in_=o_sb[:], axis=AX.X, op=ALU.add)
                I("dve", "tensor_tensor", out=osq[:], in0=o_sb[:], in1=o_sb[:], op=ALU.mult)
                I("dve", "tensor_reduce", out=st4[:, 4:8], in_=osq[:], axis=AX.X, op=ALU.add)
                I("dve", "tensor_scalar", out=st4[:, 0:8], in0=st4[:, 0:8], scalar1=1.0 / 128.0, scalar2=None, op0=ALU.mult)
                I("dve", "tensor_tensor", out=st4[:, 8:12], in0=st4[:, 0:4], in1=st4[:, 0:4], op=ALU.mult)
                I("dve", "tensor_tensor", out=st4[:, 8:12], in0=st4[:, 4:8], in1=st4[:, 8:12], op=ALU.subtract)
                I("act", "activation", out=st4[:, 12:16], in_=st4[:, 8:12], func=AF.Ln, scale=1.0, bias=EPS)
                I("act", "activation", out=st4[:, 12:16], in_=st4[:, 12:16], func=AF.Exp, scale=-0.5)
                I("dve", "tensor_tensor", out=o_sb[:], in0=o_sb[:], in1=st4.v(st4.h[:, 0:4].unsqueeze(2).to_broadcast([128, 4, 128])), op=ALU.subtract)
                I("dve", "tensor_tensor", out=o_sb[:], in0=o_sb[:], in1=st4.v(st4.h[:, 12:16].unsqueeze(2).to_broadcast([128, 4, 128])), op=ALU.mult)
                o2 = o_sb.v(o_sb.h[:].rearrange("p h d -> p (h d)"))
                I("dve", "tensor_tensor", out=o2, in0=o2, in1=gn_b[:], op=ALU.mult)
                I("dve", "tensor_tensor", out=mixr[:], in0=o2, in1=gate[:], op=ALU.mult)
                for k in range(4):
                    K.tr(psT[:, k * 128:(k + 1) * 128], mixr[:, k * 128:(k + 1) * 128], idb[:])
                I("act", "activation", out=mixT_r.v(mixT_r.h[:].rearrange("p a b -> p (a b)")), in_=psT[:, 0:512], func=AF.Copy)
                for g in range(2):
                    gs = slice(g * 64, (g + 1) * 64)
                    for kt in range(NKT):
                        Fq = F[2] if kt % 2 == 0 else F[5]
                        pt = PT[kt % 2]
                        K.mm(Fq[:], attKT[:, kt * 128:(kt + 1) * 128], QTg[g][:])
                        I("act", "activation", out=pt[:], in_=Fq[:], func=AF.Exp, scale=ATT_SCALE)
                        K.mm(F[3][:], attV[:, kt, :], pt[:], start=(kt == 0), stop=(kt == NKT - 1))
                        K.mm(F[4][:], ones[:], pt[:], start=(kt == 0), stop=(kt == NKT - 1))
                    I("dve", "reciprocal", out=rs[:], in_=F[4][:])
                    I("dve", "tensor_tensor", out=attT.v(attT.h[gs, :, :].rearrange("p a b -> p (a b)")), in0=F[3][gs, :], in1=rs[gs, :], op=ALU.mult)
                for half in range(2):
                    cs = slice(half * 512, (half + 1) * 512)
                    Fy = F[half]
                    for k in range(4):
                        K.mm(Fy[:], mixT_r[:, k, :], wo_r[:, k, cs], start=(k == 0), stop=False)
                    for pr in range(4):
                        K.mm(Fy[:], attT[:, pr, :], wo_a[:, pr, cs], start=False, stop=(pr == 3))
                    I("dve", "tensor_tensor", out=tmp[:, cs], in0=Fy[:], in1=V(g1_b.ap[:, cs], g1_b.deps), op=ALU.mult)
                    I("dve", "tensor_tensor", out=x1[:, cs], in0=tmp[:, cs], in1=xt[:, cs], op=ALU.add)
                K.dma("sp", x1s[t * 128:(t + 1) * 128, :], x1[:], "st1")
                norm_mod_T(x1, A2_b, sh2_b, tmp, hm, h2T)
                K.dma("sp", h2s[:, :, t * 128:(t + 1) * 128], h2T[:], "st2")
            K.barrier()

        es_ab.close()
        with ExitStack() as s3:
            W1 = K.sb("W1", [128, 8, 4096], BF16, s3)
            W2 = K.sb("W2", [128, 32, 1024], BF16, s3)
            hb = K.sb("hbC", [128, 8, 512], BF16, s3)
            act = K.sb("actC", [128, 32, 512], BF16, s3)
            rl = [K.sb("rlC%d" % i, [128, 512], F32, s3) for i in range(2)]
            x1 = K.sb("x1C", [128, 1024], F32, s3)
            x2 = K.sb("x2C", [128, 1024], F32, s3)
            tmp = K.sb("tmpC", [128, 1024], F32, s3)
            ot = K.sb("otC", [128, 1024], F32, s3)
            w1v = w1.rearrange("(k p) n -> p k n", p=128)
            w2v = w2.rearrange("(k p) n -> p k n", p=128)
            for j in range(4):
                K.dma("pool", W1[:, :, j * 1024:(j + 1) * 1024], w1v[:, :, j * 1024:(j + 1) * 1024], "w%d" % j)
            for j in range(4):
                K.dma("pool", W2[:, j * 8:(j + 1) * 8, :], w2v[:, j * 8:(j + 1) * 8, :], "v%d" % j)
            for blk in range(NC):
                K.dma("sp", hb[:], h2s[:, :, blk * 512:(blk + 1) * 512], "hl")
                for fc in range(32):
                    Fo = F[fc % 2]
                    for k in range(8):
                        K.mm(Fo[:], W1[:, k, fc * 128:(fc + 1) * 128], hb[:, k, :], start=(k == 0), stop=(k == 7))
                    r = rl[fc % 2]
                    I("act", "activation", out=r[:], in_=Fo[:], func=AF.Relu)
                    I("pool", "tensor_tensor", out=act[:, fc, :], in0=r[:], in1=r[:], op=ALU.mult)
                for tt in range(4):
                    t = blk * 4 + tt
                    K.dma("sp", x1[:], x1s[t * 128:(t + 1) * 128, :], "xl")
                    for half in range(2):
                        cs = slice(half * 512, (half + 1) * 512)
                        Fy = F[2 + half]
                        for fc in range(32):
                            K.mm(Fy[:], act[:, fc, tt * 128:(tt + 1) * 128], W2[:, fc, cs], start=(fc == 0), stop=(fc == 31))
                        I("dve", "tensor_tensor", out=tmp[:, cs], in0=Fy[:], in1=V(g2_b.ap[:, cs], g2_b.deps), op=ALU.mult)
                        I("dve", "tensor_tensor", out=x2[:, cs], in0=tmp[:, cs], in1=x1[:, cs], op=ALU.add)
                    row_rstd(x2[:], tmp[:], 1024.0)
                    I("dve", "scalar_tensor_tensor", out=ot[:], in0=x2[:], scalar=rst[:, 0:1], in1=gf_b[:], op0=ALU.mult, op1=ALU.mult)
                    K.dma("sp", out[t * 128:(t + 1) * 128, :], ot[:], "out")

        nw, cnt = K.emit()
        K.finish()
    return nc


def _tables():
    Lq = 4096
    t = np.arange(Lq, dtype=np.float32)
    fr = (np.float32(10000.0) ** (-(np.arange(64, dtype=np.float32) / np.float32(64)))).astype(np.float32)
    ang = (t[:, None] * fr[None, :]).astype(np.float32).astype(np.float64)
    rope_r = np.concatenate([np.cos(ang), np.sin(ang)], axis=1).astype(np.float32)
    af = (np.float32(10000.0) ** (-(np.arange(16, dtype=np.float32) / np.float32(16)))).astype(np.float32)
    row = np.repeat(np.arange(64, dtype=np.float32), 64)
    col = np.tile(np.arange(64, dtype=np.float32), 64)
    aang = np.concatenate([(row[:, None] * af[None, :]).astype(np.float32), (col[:, None] * af[None, :]).astype(np.float32)], axis=1).astype(np.float64)
    rope_a = np.concatenate([np.cos(aang), np.sin(aang)], axis=1).astype(np.float32)
    j = np.arange(128, dtype=np.float32)[:, None]
    i = np.arange(128, dtype=np.float32)[None, :]
    cm = np.zeros((128, 6, 128), np.float32)
    cm[:, 0, :] = np.maximum(i - j, 0.0)
    cm[:, 1, :] = (i >= j).astype(np.float32)
    cm[:, 2, :] = np.maximum(j - i, 0.0)
    cm[:, 3, :] = (j >= i).astype(np.float32)
    cm[:, 4, :] = i + 1.0 + 0.0 * j
    cm[:, 5, :] = 128.0 - i + 0.0 * j
    p = np.arange(128, dtype=np.float32)
    pcols = np.stack([p, 127.0 - p, 255.0 - p, 128.0 + p], axis=1).astype(np.float32)
    return rope_r, rope_a, cm, pcols, np.eye(128, dtype=np.float32)


def kernel(x, c, ctx, c_ctx, w_mod, b_mod, norm1_g, norm2_g, w_in, w_out, ret_log_rate, ret_gn_g,
           q_norm_g, k_norm_g, w_ff1, w_ff2, final_norm_g):
    f = lambda a: np.ascontiguousarray(np.asarray(a, dtype=np.float32))
    x = f(x); c = f(c); ctx = f(ctx); c_ctx = f(c_ctx)
    rope_r, rope_a, cm, pcols, ident = _tables()
    shared = {
        "w_mod": f(w_mod)[0], "b_mod": f(b_mod)[0], "norm1_g": f(norm1_g)[0], "norm2_g": f(norm2_g)[0],
        "w_in": f(w_in)[0], "w_out": f(w_out)[0], "rate": f(ret_log_rate)[0].reshape(8),
        "ret_gn_g": f(ret_gn_g)[0], "q_norm_g": f(q_norm_g)[0], "k_norm_g": f(k_norm_g)[0],
        "w_ff1": f(w_ff1)[0], "w_ff2": f(w_ff2)[0], "final_norm_g": f(final_norm_g),
        "rope_r": rope_r, "rope_a": rope_a, "cmat": cm, "pcols": pcols, "ident": ident,
    }
    n = x.shape[0]
    in_maps = []
    for b in range(n):
        m = dict(shared)
        m["x"] = x[b]
        m["ctx"] = ctx[b]
        m["cvec"] = np.ascontiguousarray(np.stack([c[b], c_ctx], axis=0))
        in_maps.append(m)
    nc = build_nc()
    res = run_bass_kernel_spmd(nc, in_maps, core_ids=list(range(n)))
    return np.stack([np.asarray(r["out"], dtype=np.float32) for r in res.results], axis=0)
```

```python
import numpy as np
import concourse.bass as bass
import concourse.mybir as mybir
from contextlib import ExitStack
from concourse.bass_utils import run_bass_kernel_spmd

F32 = mybir.dt.float32
BF16 = mybir.dt.bfloat16
AF = mybir.ActivationFunctionType
ALU = mybir.AluOpType
AX = mybir.AxisListType


class Dep:
    __slots__ = ("w", "readers", "name", "excl")

    def __init__(self, name="", excl=False):
        self.w = None
        self.readers = {}
        self.name = name
        self.excl = excl


class V:
    __slots__ = ("ap", "deps")

    def __init__(self, ap, deps):
        self.ap = ap
        self.deps = deps


class Tile:
    def __init__(self, handle, deps):
        self.h = handle
        self.deps = deps

    def __getitem__(self, idx):
        return V(self.h[idx], self.deps)

    def v(self, ap):
        return V(ap, self.deps)


class Op:
    __slots__ = ("eng", "fn", "waits", "marked", "tick", "dma_sem", "dma_val", "key")


class Kern:
    ENGS = ("pe", "act", "dve", "pool", "sp")

    def __init__(self, nc, es):
        self.nc = nc
        self.es = es
        self.ops = []
        self.eng_obj = {"pe": nc.tensor, "act": nc.scalar, "dve": nc.vector,
                        "pool": nc.gpsimd, "sp": nc.sync}
        self.eng_sem = {e: es.enter_context(nc.semaphore("sem_" + e)) for e in self.ENGS}
        self.dma_sems = {}
        self.dma_cnt = {}
        self.last_op = {e: None for e in self.ENGS}
        self.same_engine_sync = True

    def sb(self, name, shape, dt, es=None):
        h = (es or self.es).enter_context(self.nc.sbuf_tensor(name, list(shape), dt))
        return Tile(h, [Dep(name)])

    def ps(self, name, shape, dt):
        h = self.es.enter_context(self.nc.psum_tensor(name, list(shape), dt))
        return Tile(h, [Dep(name, excl=True)])

    def dsem(self, name):
        s = self.es.enter_context(self.nc.semaphore("dq_" + name))
        self.dma_sems[name] = s
        self.dma_cnt[name] = 0
        return name

    def _add_wait(self, op, prod):
        if prod is None or prod is op:
            return
        if prod.dma_sem is None:
            if prod.eng == op.eng and op.dma_sem is None:
                if op.eng in ("pe", "sp"):
                    return
                if not self.same_engine_sync:
                    return
            prod.marked = True
        op.waits.append(prod)

    def _record(self, eng, fn, reads, writes, dma_sem=None):
        op = Op()
        op.eng = eng
        op.fn = fn
        op.waits = []
        op.marked = False
        op.tick = None
        op.dma_sem = dma_sem
        op.dma_val = None
        if dma_sem is not None:
            self.dma_cnt[dma_sem] += 16
            op.dma_val = self.dma_cnt[dma_sem]
            op.key = ("dma", dma_sem)
        else:
            op.key = eng
        rd = []
        wd = []
        for v in reads:
            for d in v.deps:
                if d.excl:
                    if d not in wd:
                        wd.append(d)
                elif d not in rd:
                    rd.append(d)
        for v in writes:
            for d in v.deps:
                if d not in wd:
                    wd.append(d)
        for d in rd:
            self._add_wait(op, d.w)
        for d in wd:
            self._add_wait(op, d.w)
            for r in d.readers.values():
                self._add_wait(op, r)
        for d in rd:
            if d not in wd:
                d.readers[op.key] = op
        for d in wd:
            d.w = op
            d.readers = {}
        self.ops.append(op)
        if dma_sem is None:
            self.last_op[eng] = op
        return op

    def I(self, eng, method, *, r=(), w=(), **kw):
        reads = list(r)
        writes = list(w)
        args = {}
        for k, v in kw.items():
            if isinstance(v, V):
                args[k] = v.ap
                if k in ("out", "accum_out", "ap"):
                    writes.append(v)
                else:
                    reads.append(v)
            else:
                args[k] = v

        def fn(e, method=method, args=args):
            return getattr(e, method)(**args)

        return self._record(eng, fn, reads, writes)

    def mm(self, out, lhsT, rhs, start=True, stop=True):
        def fn(e, o=out.ap, l=lhsT.ap, r_=rhs.ap, s=start, t=stop):
            return e.matmul(o, lhsT=l, rhs=r_, start=s, stop=t)
        return self._record("pe", fn, [lhsT, rhs], [out])

    def tr(self, out, in_, ident):
        def fn(e, o=out.ap, i=in_.ap, d=ident.ap):
            return e.transpose(o, i, d)
        return self._record("pe", fn, [in_, ident], [out])

    def dma(self, queue, out, in_, sem, **dkw):
        reads = [in_] if isinstance(in_, V) else []
        writes = [out] if isinstance(out, V) else []
        o = out.ap if isinstance(out, V) else out
        i = in_.ap if isinstance(in_, V) else in_

        def fn(e, o=o, i=i, dkw=dkw):
            return e.dma_start(out=o, in_=i, **dkw)
        return self._record(queue, fn, reads, writes, dma_sem=sem)

    def barrier(self):
        lasts = []
        for e in self.ENGS:
            if self.last_op[e] is not None and self.last_op[e].dma_sem is None:
                lasts.append(self.last_op[e])
        dmas = {}
        for op in self.ops:
            if op.dma_sem is not None:
                dmas[op.dma_sem] = op
        for e in ("pe", "act", "dve", "pool", "sp"):
            op = Op()
            op.eng = e
            op.fn = None
            op.waits = []
            op.marked = False
            op.tick = None
            op.dma_sem = None
            op.dma_val = None
            op.key = e
            for p in lasts:
                if p.eng != e or e not in ("pe", "sp"):
                    p.marked = True
                    op.waits.append(p)
            for p in dmas.values():
                op.waits.append(p)
            self.ops.append(op)

    def emit(self):
        cnt = {e: 0 for e in self.ENGS}
        seen = {e: {} for e in self.ENGS}
        n_wait = 0
        for op in self.ops:
            e = self.eng_obj[op.eng]
            for p in op.waits:
                if p.dma_sem is not None:
                    ck = ("dma", p.dma_sem)
                    val = p.dma_val
                    sem = self.dma_sems[p.dma_sem]
                else:
                    ck = p.eng
                    val = p.tick
                    sem = self.eng_sem[p.eng]
                    assert val is not None, "waiting on unticked op"
                if seen[op.eng].get(ck, 0) >= val:
                    continue
                seen[op.eng][ck] = val
                e.wait_ge(sem, val)
                n_wait += 1
            if op.fn is None:
                continue
            ins = op.fn(e)
            if op.dma_sem is not None:
                ins.then_inc(self.dma_sems[op.dma_sem], 16)
            elif op.marked:
                cnt[op.eng] += 1
                op.tick = cnt[op.eng]
                ins.then_inc(self.eng_sem[op.eng], 1)
        return n_wait, cnt

    def finish(self, queue="sp"):
        e = self.eng_obj[queue]
        for name, c in self.dma_cnt.items():
            if c > 0:
                e.wait_ge(self.dma_sems[name], c)

L = 4096
NT = 32
D = 1024
EPS = 1e-6
RET_SCALE = 128 ** -0.5
ATT_SCALE = 64 ** -0.5
NKT = 34
NA = 32
NB = 32
NC = 8


def build_nc():
    nc = bass.Bass("TRN2", target_bir_lowering=False)

    def din(name, shape):
        return nc.dram_tensor(name, list(shape), F32, kind="ExternalInput").ap()

    x = din("x", [L, D])
    ctx = din("ctx", [256, D])
    cvec = din("cvec", [2, D])
    w_mod = din("w_mod", [D, 6144])
    b_mod = din("b_mod", [6144])
    n1g = din("norm1_g", [D])
    n2g = din("norm2_g", [D])
    w_in = din("w_in", [D, 2816])
    w_out = din("w_out", [D, D])
    rate = din("rate", [8])
    gng = din("ret_gn_g", [512])
    qng = din("q_norm_g", [64])
    kng = din("k_norm_g", [64])
    w1 = din("w_ff1", [D, 4096])
    w2 = din("w_ff2", [4096, D])
    fng = din("final_norm_g", [D])
    rope_r = din("rope_r", [L, 128])
    rope_a = din("rope_a", [L, 64])
    cmat = din("cmat", [128, 6, 128])
    pcols = din("pcols", [128, 4])
    ident = din("ident", [128, 128])
    out = nc.dram_tensor("out", [L, D], F32, kind="ExternalOutput").ap()
    x1s = nc.dram_tensor("x1s", [L, D], F32).ap()
    h2s = nc.dram_tensor("h2s", [128, 8, L], BF16).ap()

    with ExitStack() as es:
        K = Kern(nc, es)
        I = K.I
        idb = K.sb("idb", [128, 128], BF16)
        ones = K.sb("ones", [128, 128], BF16)
        g2_t = K.sb("g2_t", [128, 1024], F32)
        gf_b = K.sb("gf_b", [128, 1024], F32)
        ssq = K.sb("ssq", [128, 8], F32)
        rst = K.sb("rst", [128, 8], F32)
        es_ab = ExitStack()
        es.enter_context(es_ab)
        K.es_save = K.es
        K.es = es_ab
        modb = K.sb("modb", [128, 5120], F32)
        gn_b = K.sb("gn_b", [128, 512], F32)
        qg_b = K.sb("qg_b", [128, 64], F32)
        kg_b = K.sb("kg_b", [128, 64], F32)
        intraT = K.sb("intraT", [128, 2, 512], F32)
        QD = K.sb("QD", [128, 2, 512], F32)
        KD = K.sb("KD", [128, 2, 4], F32)
        cdec = K.sb("cdec", [128, 8], F32)
        wc = K.sb("wc", [128, 2, 2, 4], F32)
        SbAll = K.sb("SbAll", [128, NT, 512], BF16)
        attKT = K.sb("attKT", [128, NKT * 128], BF16)
        attV = K.sb("attV", [128, NKT, 128], BF16)
        Sf = K.sb("Sf", [128, 512], F32)
        Sf_bf = K.sb("Sf_bf", [128, 512], BF16)
        Sb = K.sb("Sb", [128, 512], F32)
        K.es = K.es_save
        es_c = ExitStack()
        es_ab.enter_context(es_c)
        cmodb = K.sb("cmodb", [128, 2048], F32, es_c)
        psT = K.ps("psT", [128, 1024], BF16)
        psT2 = K.ps("psT2", [128, 1024], BF16)
        F = [K.ps("F%d" % i, [128, 512], F32) for i in range(6)]
        for nm in ["x", "rp", "rp2", "w0", "w1", "w2", "w3", "w4", "v0", "v1", "v2", "v3", "st1", "st2", "out", "hl", "xl"] + ["cst%d" % i for i in range(2, 15)]:
            K.dsem(nm)

        def rstd_from(acc_v, n, out_v):
            I("act", "activation", out=out_v, in_=acc_v, func=AF.Ln, scale=1.0 / n, bias=EPS)
            I("act", "activation", out=out_v, in_=out_v, func=AF.Exp, scale=-0.5)

        def row_rstd(src_v, junk_v, n):
            I("dve", "memset", ap=ssq[:, 0:1], constant=0.0)
            I("act", "activation", out=junk_v, in_=src_v, func=AF.Square, accum_out=ssq[:, 0:1])
            rstd_from(ssq[:, 0:1], n, rst[:, 0:1])

        with ExitStack() as s0:
            idf = K.sb("idf", [128, 128], F32, s0)
            cc = K.sb("cc", [128, 16], F32, s0)
            cact = K.sb("cact", [128, 16], F32, s0)
            cb = K.sb("cb", [128, 16, 128], BF16, s0)
            wms = [K.sb("wms%d" % i, [128, 8, 512], BF16, s0) for i in range(2)]
            bmod_b = K.sb("bmod_b", [128, 6144], F32, s0)
            n1g_b = K.sb("n1g_b", [128, 1024], F32, s0)
            n2g_b = K.sb("n2g_b", [128, 1024], F32, s0)
            rate_b = K.sb("rate_b", [128, 8], F32, s0)
            lg = K.sb("lg", [128, 8], F32, s0)
            cm = K.sb("cm", [128, 6, 128], F32, s0)
            pc = K.sb("pc", [128, 4], F32, s0)
            tmpe = K.sb("tmpe", [128, 128], F32, s0)
            K.dma("sp", idf[:], ident[:, :], "cst2")
            K.dma("sp", cc[:, 0:8], cvec[0, :].rearrange("(k p) -> p k", p=128), "cst3", allow_slow_non_contiguous=True)
            K.dma("sp", cc[:, 8:16], cvec[1, :].rearrange("(k p) -> p k", p=128), "cst4", allow_slow_non_contiguous=True)
            K.dma("sp", bmod_b[:], b_mod.partition_broadcast(128), "cst5")
            K.dma("sp", n1g_b[:], n1g.partition_broadcast(128), "cst6")
            K.dma("sp", n2g_b[:], n2g.partition_broadcast(128), "cst7")
            K.dma("sp", gf_b[:], fng.partition_broadcast(128), "cst8")
            K.dma("sp", gn_b[:], gng.partition_broadcast(128), "cst9")
            K.dma("sp", qg_b[:], qng.partition_broadcast(128), "cst10")
            K.dma("sp", kg_b[:], kng.partition_broadcast(128), "cst11")
            K.dma("sp", rate_b[:], rate.partition_broadcast(128), "cst12")
            K.dma("sp", cm[:], cmat[:, :, :], "cst13")
            K.dma("sp", pc[:], pcols[:, :], "cst14")
            I("dve", "tensor_copy", out=idb[:], in_=idf[:])
            I("dve", "memset", ap=ones[:], constant=1.0)
            I("act", "activation", out=cact[:], in_=cc[:], func=AF.Silu)
            for j in range(16):
                I("dve", "tensor_copy", out=cb[:, j, :], in_=cact.v(cact.h[:, j:j + 1].to_broadcast([128, 128])))
            wmv = w_mod.rearrange("(k p) n -> p k n", p=128)
            for jb in range(12):
                ws = wms[jb % 2]
                K.dma("pool", ws[:], wmv[:, :, jb * 512:(jb + 1) * 512], "w%d" % (jb % 2))
                for k in range(8):
                    K.mm(F[0][:], cb[:, k, :], ws[:, k, :], start=(k == 0), stop=(k == 7))
                dst = modb[:, jb * 512:(jb + 1) * 512] if jb < 10 else g2_t[:, (jb - 10) * 512:(jb - 9) * 512]
                I("dve", "tensor_tensor", out=dst, in0=F[0][:], in1=bmod_b[:, jb * 512:(jb + 1) * 512], op=ALU.add)
                if jb < 4:
                    for k in range(8):
                        K.mm(F[1][:], cb[:, 8 + k, :], ws[:, k, :], start=(k == 0), stop=(k == 7))
                    I("dve", "tensor_tensor", out=cmodb[:, jb * 512:(jb + 1) * 512], in0=F[1][:], in1=bmod_b[:, jb * 512:(jb + 1) * 512], op=ALU.add)
            I("dve", "scalar_tensor_tensor", out=modb[:, 1024:2048], in0=modb[:, 1024:2048], scalar=1.0, in1=n1g_b[:], op0=ALU.add, op1=ALU.mult)
            I("dve", "scalar_tensor_tensor", out=modb[:, 4096:5120], in0=modb[:, 4096:5120], scalar=1.0, in1=n2g_b[:], op0=ALU.add, op1=ALU.mult)
            I("dve", "scalar_tensor_tensor", out=cmodb[:, 1024:2048], in0=cmodb[:, 1024:2048], scalar=1.0, in1=n1g_b[:], op0=ALU.add, op1=ALU.mult)
            I("act", "activation", out=lg[:], in_=rate_b[:], func=AF.Exp)
            I("act", "activation", out=lg[:], in_=lg[:], func=AF.Ln, scale=-1.0, bias=1.0)
            for d in range(2):
                for h in range(4):
                    c = d * 4 + h
                    sc = lg[:, c:c + 1]
                    I("act", "activation", out=tmpe[:], in_=cm[:, 2 * d, :], func=AF.Exp, scale=sc)
                    I("dve", "tensor_tensor", out=intraT[:, d, h * 128:(h + 1) * 128], in0=tmpe[:], in1=cm[:, 2 * d + 1, :], op=ALU.mult)
                    I("act", "activation", out=QD[:, d, h * 128:(h + 1) * 128], in_=cm[:, 4 + d, :], func=AF.Exp, scale=sc)
                    I("act", "activation", out=KD[:, d, h:h + 1], in_=pc[:, (1 if d == 0 else 0):(2 if d == 0 else 1)], func=AF.Exp, scale=sc)
                    for t in range(2):
                        col = (2 if t == 0 else 1) if d == 0 else (0 if t == 0 else 3)
                        I("act", "activation", out=wc[:, t, d, h:h + 1], in_=pc[:, col:col + 1], func=AF.Exp, scale=sc)
            I("act", "activation", out=cdec[:], in_=lg[:], func=AF.Exp, scale=128.0)
            I("dve", "tensor_scalar", out=KD[:], in0=KD[:], scalar1=RET_SCALE, scalar2=None, op0=ALU.mult)
            I("dve", "tensor_scalar", out=wc[:], in0=wc[:], scalar1=RET_SCALE, scalar2=None, op0=ALU.mult)
            K.barrier()

        sh1_b = modb[:, 0:1024]; A1_b = modb[:, 1024:2048]; g1_b = modb[:, 2048:3072]
        sh2_b = modb[:, 3072:4096]; A2_b = modb[:, 4096:5120]; g2_b = g2_t[:, :]
        csh1_b = cmodb[:, 0:1024]; cA1_b = cmodb[:, 1024:2048]

        def norm_mod_T(xt, A_v, sh_v, tmp, hm, hT):
            row_rstd(xt[:], tmp[:], 1024.0)
            I("dve", "scalar_tensor_tensor", out=tmp[:], in0=xt[:], scalar=rst[:, 0:1], in1=A_v, op0=ALU.mult, op1=ALU.mult)
            I("dve", "tensor_tensor", out=hm[:], in0=tmp[:], in1=sh_v, op=ALU.add)
            for k in range(8):
                K.tr(psT[:, k * 128:(k + 1) * 128], hm[:, k * 128:(k + 1) * 128], idb[:])
            I("act", "activation", out=hT.v(hT.h[:].rearrange("p a b -> p (a b)")), in_=psT[:], func=AF.Copy)

        def v3(t, ap):
            return t.v(ap)

        def rope(src, H, half, cos_v, sin_v, t1, t2, dst_lo, dst_hi):
            s_lo = V(src.ap[:, :, 0:half], src.deps)
            s_hi = V(src.ap[:, :, half:2 * half], src.deps)
            cb_ = V(cos_v.ap.unsqueeze(1).to_broadcast([128, H, half]), cos_v.deps)
            sb_ = V(sin_v.ap.unsqueeze(1).to_broadcast([128, H, half]), sin_v.deps)
            I("dve", "tensor_tensor", out=t1, in0=s_lo, in1=cb_, op=ALU.mult)
            I("dve", "tensor_tensor", out=t2, in0=s_hi, in1=sb_, op=ALU.mult)
            I("dve", "tensor_tensor", out=dst_lo, in0=t1, in1=t2, op=ALU.subtract)
            I("dve", "tensor_tensor", out=t1, in0=s_lo, in1=sb_, op=ALU.mult)
            I("dve", "tensor_tensor", out=t2, in0=s_hi, in1=cb_, op=ALU.mult)
            I("dve", "tensor_tensor", out=dst_hi, in0=t1, in1=t2, op=ALU.add)

        def head_rstd(src_v, H, n, sq, nh):
            I("act", "activation", out=sq, in_=src_v, func=AF.Square)
            I("dve", "tensor_reduce", out=ssq[:, 0:H], in_=sq, axis=AX.X, op=ALU.add)
            rstd_from(ssq[:, 0:H], float(n), rst[:, 0:H])

        with ExitStack() as s1:
            wA = K.sb("wA", [128, 8, 1280], BF16, s1)
            xt = K.sb("xtA", [128, 1024], F32, s1)
            tmp = K.sb("tmpA", [128, 1024], F32, s1)
            hm = K.sb("hmA", [128, 1024], BF16, s1)
            hT = K.sb("hTA", [128, 8, 128], BF16, s1)
            cr = K.sb("crA", [128, 128], F32, s1)
            ca = K.sb("caA", [128, 64], F32, s1)
            kr = K.sb("krA", [128, 4, 128], F32, s1)
            t1 = K.sb("t1A", [128, 4, 64], F32, s1)
            t2 = K.sb("t2A", [128, 4, 64], F32, s1)
            kdb = [K.sb("kdbA%d" % i, [128, 4, 128], BF16, s1) for i in range(2)]
            kdf = [K.sb("kdfA%d" % i, [128, 4, 128], BF16, s1) for i in range(2)]
            vbf = [K.sb("vbfA%d" % i, [128, 4, 128], BF16, s1) for i in range(2)]
            sqk = K.sb("sqkA", [128, 2, 64], F32, s1)
            kn = K.sb("knA", [128, 2, 64], F32, s1)
            kro = K.sb("kroA", [128, 2, 64], BF16, s1)
            wiv = w_in.rearrange("(k p) n -> p k n", p=128)
            K.dma("pool", wA[:, :, 0:1024], wiv[:, :, 512:1536], "w0")
            K.dma("pool", wA[:, :, 1024:1280], wiv[:, :, 2560:2816], "w1")

            def tileA(src_ap, is_ctx, idx, kt):
                K.dma("sp", xt[:], src_ap, "x")
                if not is_ctx:
                    K.dma("sp", cr[:], rope_r[idx * 128:(idx + 1) * 128, :], "rp")
                    K.dma("sp", ca[:], rope_a[idx * 128:(idx + 1) * 128, :], "rp2")
                norm_mod_T(xt, cA1_b if is_ctx else A1_b, csh1_b if is_ctx else sh1_b, tmp, hm, hT)
                for k in range(8):
                    K.mm(F[0][:], hT[:, k, :], wA[:, k, 0:512], start=(k == 0), stop=(k == 7))
                for k in range(8):
                    K.mm(F[1][:], hT[:, k, :], wA[:, k, 512:1024], start=(k == 0), stop=(k == 7))
                for k in range(8):
                    K.mm(F[2][:, 0:256], hT[:, k, :], wA[:, k, 1024:1280], start=(k == 0), stop=(k == 7))
                akv = F[2].v(F[2].h[:, 0:128].rearrange("p (h d) -> p h d", h=2))
                head_rstd(akv, 2, 64, sqk[:], 2)
                I("dve", "tensor_tensor", out=kn[:], in0=akv, in1=rst.v(rst.h[:, 0:2].unsqueeze(2).to_broadcast([128, 2, 64])), op=ALU.mult)
                I("dve", "tensor_tensor", out=kn[:], in0=kn[:], in1=kg_b.v(kg_b.h[:].unsqueeze(1).to_broadcast([128, 2, 64])), op=ALU.mult)
                if is_ctx:
                    I("dve", "tensor_copy", out=kro[:], in_=kn[:])
                else:
                    rope(kn[:], 2, 32, ca[:, 0:32], ca[:, 32:64], t1.v(t1.h[:, 0:2, 0:32]), t2.v(t2.h[:, 0:2, 0:32]),
                         kro[:, :, 0:32], kro[:, :, 32:64])
                K.tr(psT2[:, 0:128], kro.v(kro.h[:].rearrange("p h d -> p (h d)")), idb[:])
                I("act", "activation", out=attKT[:, kt * 128:(kt + 1) * 128], in_=psT2[:, 0:128], func=AF.Copy)
                I("dve", "tensor_copy", out=attV[:, kt, :], in_=F[2][:, 128:256])
                rk4 = F[0].v(F[0].h[:].rearrange("p (h d) -> p h d", h=4))
                if is_ctx:
                    I("dve", "tensor_copy", out=kr[:], in_=rk4)
                else:
                    rope(rk4, 4, 64, cr[:, 0:64], cr[:, 64:128], t1[:], t2[:], kr[:, :, 0:64], kr[:, :, 64:128])
                b = idx % 2 if is_ctx else 0
                I("act", "activation", out=vbf[b].v(vbf[b].h[:].rearrange("p h d -> p (h d)")), in_=F[1][:], func=AF.Copy)
                if is_ctx:
                    I("dve", "tensor_tensor", out=kdf[b][:], in0=kr[:], in1=wc.v(wc.h[:, idx, 0, :].unsqueeze(2).to_broadcast([128, 4, 128])), op=ALU.mult)
                    I("dve", "tensor_tensor", out=kdb[b][:], in0=kr[:], in1=wc.v(wc.h[:, idx, 1, :].unsqueeze(2).to_broadcast([128, 4, 128])), op=ALU.mult)
                else:
                    I("dve", "tensor_tensor", out=kdb[0][:], in0=kr[:], in1=KD.v(KD.h[:, 1, :].unsqueeze(2).to_broadcast([128, 4, 128])), op=ALU.mult)
                    for h in range(4):
                        K.mm(F[3][:, h * 128:(h + 1) * 128], kdb[0][:, h, :], vbf[0][:, h, :])
                    I("act", "activation", out=SbAll[:, idx, :], in_=Sb[:], func=AF.Copy)
                    for h in range(4):
                        I("dve", "scalar_tensor_tensor", out=Sb[:, h * 128:(h + 1) * 128], in0=Sb[:, h * 128:(h + 1) * 128],
                          scalar=cdec[:, 4 + h:5 + h], in1=F[3][:, h * 128:(h + 1) * 128], op0=ALU.mult, op1=ALU.add)

            tileA(ctx[0:128, :], True, 0, 0)
            tileA(ctx[128:256, :], True, 1, 1)
            for h in range(4):
                for t in range(2):
                    K.mm(F[3][:, h * 128:(h + 1) * 128], kdf[t][:, h, :], vbf[t][:, h, :], start=(t == 0), stop=(t == 1))
            for h in range(4):
                for t in range(2):
                    K.mm(F[4][:, h * 128:(h + 1) * 128], kdb[t][:, h, :], vbf[t][:, h, :], start=(t == 0), stop=(t == 1))
            I("dve", "tensor_copy", out=Sf[:], in_=F[3][:])
            I("dve", "tensor_copy", out=Sb[:], in_=F[4][:])
            I("dve", "tensor_copy", out=Sf_bf[:], in_=Sf[:])
            for t in range(NT - 1, NT - 1 - NA, -1):
                tileA(x[t * 128:(t + 1) * 128, :], False, t, 2 + t)
            K.barrier()

        es_c.close()
        with ExitStack() as s2:
            wi = K.sb("wi", [128, 8, 2560], BF16, s2)
            wo_r = K.sb("wo_r", [128, 4, 1024], BF16, s2)
            wo_a = K.sb("wo_a", [128, 4, 1024], BF16, s2)
            xt = K.sb("xtB", [128, 1024], F32, s2)
            tmp = K.sb("tmpB", [128, 1024], F32, s2)
            hm = K.sb("hmB", [128, 1024], BF16, s2)
            hT = K.sb("hTB", [128, 8, 128], BF16, s2)
            cr = K.sb("crB", [128, 128], F32, s2)
            ca = K.sb("caB", [128, 64], F32, s2)
            kr = K.sb("krB", [128, 4, 128], F32, s2)
            t1 = K.sb("t1B", [128, 8, 64], F32, s2)
            t2 = K.sb("t2B", [128, 8, 64], F32, s2)
            qr = K.sb("qrB", [128, 4, 128], BF16, s2)
            k_s = K.sb("k_sB", [128, 4, 128], BF16, s2)
            kdf = K.sb("kdfB", [128, 4, 128], BF16, s2)
            vbf = K.sb("vbfB", [128, 4, 128], BF16, s2)
            gate = K.sb("gateB", [128, 512], F32, s2)
            qro = K.sb("qroB", [128, 8, 64], BF16, s2)
            qrp = K.sb("qrpB", [128, 4, 2, 64], BF16, s2)
            qT = K.sb("qTB", [128, 512], BF16, s2)
            qdTf = K.sb("qdTfB", [128, 512], BF16, s2)
            qdTb = K.sb("qdTbB", [128, 512], BF16, s2)
            kT = K.sb("kTB", [128, 512], BF16, s2)
            QTg = [K.sb("QTg%dB" % g, [128, 512], BF16, s2) for g in range(2)]
            STf = K.sb("STfB", [128, 512], BF16, s2)
            STb = K.sb("STbB", [128, 512], BF16, s2)
            o_sb = K.sb("o_sbB", [128, 4, 128], F32, s2)
            st4 = K.sb("st4B", [128, 16], F32, s2)
            mixr = K.sb("mixrB", [128, 512], BF16, s2)
            mixT_r = K.sb("mixT_rB", [128, 4, 128], BF16, s2)
            attT = K.sb("attTB", [128, 4, 128], BF16, s2)
            PT = [K.sb("PT%dB" % i, [128, 512], BF16, s2) for i in range(3)]
            x1 = K.sb("x1B", [128, 1024], F32, s2)
            sq8 = kr.v(kr.h[:].rearrange("p h (a d) -> p (h a) d", a=2))
            osq = kr
            qn = o_sb.v(o_sb.h[:].rearrange("p h (a d) -> p (h a) d", a=2))
            rs = gate
            h2T = hT
            wiv = w_in.rearrange("(k p) n -> p k n", p=128)
            K.dma("pool", wi[:, :, 0:1280], wiv[:, :, 0:1280], "w0")
            K.dma("pool", wi[:, :, 1280:2560], wiv[:, :, 1280:2560], "w1")
            K.dma("pool", wo_r[:], w_out[0:512, :].rearrange("(k p) n -> p k n", p=128), "w2")
            for g in range(2):
                K.dma("pool", wo_a[g * 64:(g + 1) * 64, :, :],
                      w_out[512 + g * 256:512 + (g + 1) * 256, :].rearrange("(pr d) n -> d pr n", d=64), "w%d" % (3 + g))
            I("dve", "memset", ap=QTg[0][:], constant=0.0)
            I("dve", "memset", ap=QTg[1][:], constant=0.0)

            for t in range(NB):
                K.dma("sp", xt[:], x[t * 128:(t + 1) * 128, :], "x")
                K.dma("sp", cr[:], rope_r[t * 128:(t + 1) * 128, :], "rp")
                K.dma("sp", ca[:], rope_a[t * 128:(t + 1) * 128, :], "rp2")
                norm_mod_T(xt, A1_b, sh1_b, tmp, hm, hT)
                for j in range(5):
                    for k in range(8):
                        K.mm(F[j][:], hT[:, k, :], wi[:, k, j * 512:(j + 1) * 512], start=(k == 0), stop=(k == 7))
                rq4 = F[0].v(F[0].h[:].rearrange("p (h d) -> p h d", h=4))
                rk4 = F[1].v(F[1].h[:].rearrange("p (h d) -> p h d", h=4))
                rope(rq4, 4, 64, cr[:, 0:64], cr[:, 64:128], t1.v(t1.h[:, 0:4, :]), t2.v(t2.h[:, 0:4, :]), qr[:, :, 0:64], qr[:, :, 64:128])
                rope(rk4, 4, 64, cr[:, 0:64], cr[:, 64:128], t1.v(t1.h[:, 0:4, :]), t2.v(t2.h[:, 0:4, :]), kr[:, :, 0:64], kr[:, :, 64:128])
                I("dve", "tensor_scalar", out=k_s[:], in0=kr[:], scalar1=RET_SCALE, scalar2=None, op0=ALU.mult)
                I("dve", "tensor_tensor", out=kdf[:], in0=kr[:], in1=KD.v(KD.h[:, 0, :].unsqueeze(2).to_broadcast([128, 4, 128])), op=ALU.mult)
                I("act", "activation", out=vbf.v(vbf.h[:].rearrange("p h d -> p (h d)")), in_=F[2][:], func=AF.Copy)
                I("act", "activation", out=gate[:], in_=F[3][:], func=AF.Exp, scale=-1.0)
                I("act", "activation", out=gate[:], in_=gate[:], func=AF.Ln, scale=1.0, bias=1.0)
                I("act", "activation", out=gate[:], in_=gate[:], func=AF.Exp, scale=-1.0)
                I("dve", "tensor_tensor", out=gate[:], in0=F[3][:], in1=gate[:], op=ALU.mult)
                aq8 = F[4].v(F[4].h[:].rearrange("p (h d) -> p h d", h=8))
                head_rstd(aq8, 8, 64, sq8, 8)
                I("dve", "tensor_tensor", out=qn, in0=aq8, in1=rst.v(rst.h[:, 0:8].unsqueeze(2).to_broadcast([128, 8, 64])), op=ALU.mult)
                I("dve", "tensor_tensor", out=qn, in0=qn, in1=qg_b.v(qg_b.h[:].unsqueeze(1).to_broadcast([128, 8, 64])), op=ALU.mult)
                rope(qn, 8, 32, ca[:, 0:32], ca[:, 32:64], t1.v(t1.h[:, :, 0:32]), t2.v(t2.h[:, :, 0:32]), qro[:, :, 0:32], qro[:, :, 32:64])
                for e in range(2):
                    I("dve", "tensor_copy", out=qrp[:, :, e, :], in_=qro[:, e * 4:(e + 1) * 4, :])
                for h in range(4):
                    K.tr(psT[:, h * 128:(h + 1) * 128], qr[:, h, :], idb[:])
                for h in range(4):
                    K.tr(psT[:, 512 + h * 128:512 + (h + 1) * 128], k_s[:, h, :], idb[:])
                for pr in range(4):
                    K.tr(psT2[:, pr * 128:(pr + 1) * 128], qrp.v(qrp.h[:, pr, :, :].rearrange("p e d -> p (e d)")), idb[:])
                I("act", "activation", out=qT[:], in_=psT[:, 0:512], func=AF.Copy)
                I("dve", "tensor_tensor", out=qdTf[:], in0=psT[:, 0:512], in1=QD[:, 0, :], op=ALU.mult)
                I("dve", "tensor_tensor", out=qdTb[:], in0=psT[:, 0:512], in1=QD[:, 1, :], op=ALU.mult)
                I("act", "activation", out=kT[:], in_=psT[:, 512:1024], func=AF.Copy)
                I("dve", "tensor_copy", out=QTg[0][0:64, :], in_=psT2[0:64, 0:512])
                I("dve", "tensor_copy", out=QTg[1][64:128, :], in_=psT2[64:128, 0:512])
                for h in range(4):
                    hs = slice(h * 128, (h + 1) * 128)
                    K.mm(F[5][:, hs], kT[:, hs], qT[:, hs])
                I("dve", "tensor_tensor", out=STf[:], in0=F[5][:], in1=intraT[:, 0, :], op=ALU.mult)
                I("dve", "tensor_tensor", out=STb[:], in0=F[5][:], in1=intraT[:, 1, :], op=ALU.mult)
                for h in range(4):
                    hs = slice(h * 128, (h + 1) * 128)
                    K.mm(F[0][:, hs], kdf[:, h, :], vbf[:, h, :])
                for h in range(4):
                    hs = slice(h * 128, (h + 1) * 128)
                    K.mm(F[1][:, hs], STf[:, hs], vbf[:, h, :], start=True, stop=False)
                    K.mm(F[1][:, hs], STb[:, hs], vbf[:, h, :], start=False, stop=False)
                    K.mm(F[1][:, hs], qdTf[:, hs], Sf_bf[:, hs], start=False, stop=False)
                    K.mm(F[1][:, hs], qdTb[:, hs], SbAll[:, t, hs], start=False, stop=True)
                for h in range(4):
                    hs = slice(h * 128, (h + 1) * 128)
                    I("dve", "scalar_tensor_tensor", out=Sf[:, hs], in0=Sf[:, hs], scalar=cdec[:, h:h + 1], in1=F[0][:, hs], op0=ALU.mult, op1=ALU.add)
                I("dve", "tensor_copy", out=Sf_bf[:], in_=Sf[:])
                I("act", "activation", out=o_sb.v(o_sb.h[:].rearrange("p h d -> p (h d)")), in_=F[1][:], func=AF.Copy)
                I("dve", "tensor_reduce", out=st4[:, 0:4], in_=o_sb[:], axis=AX.X, op=ALU.add)
                I("dve", "tensor_tensor", out=osq[:], in0=o_sb[:], in1=o_sb[:], op=ALU.mult)
                I("dve", "tensor_reduce", out=st4[:, 4:8], in_=osq[:], axis=AX.X, op=ALU.add)
                I("dve", "tensor_scalar", out=st4[:, 0:8], in0=st4[:, 0:8], scalar1=1.0 / 128.0, scalar2=None, op0=ALU.mult)
                I("dve", "tensor_tensor", out=st4[:, 8:12], in0=st4[:, 0:4], in1=st4[:, 0:4], op=ALU.mult)
                I("dve", "tensor_tensor", out=st4[:, 8:12], in0=st4[:, 4:8], in1=st4[:, 8:12], op=ALU.subtract)
                I("act", "activation", out=st4[:, 12:16], in_=st4[:, 8:12], func=AF.Ln, scale=1.0, bias=EPS)
                I("act", "activation", out=st4[:, 12:16], in_=st4[:, 12:16], func=AF.Exp, scale=-0.5)
                I("dve", "tensor_tensor", out=o_sb[:], in0=o_sb[:], in1=st4.v(st4.h[:, 0:4].unsqueeze(2).to_broadcast([128, 4, 128])), op=ALU.subtract)
                I("dve", "tensor_tensor", out=o_sb[:], in0=o_sb[:], in1=st4.v(st4.h[:, 12:16].unsqueeze(2).to_broadcast([128, 4, 128])), op=ALU.mult)
                o2 = o_sb.v(o_sb.h[:].rearrange("p h d -> p (h d)"))
                I("dve", "tensor_tensor", out=o2, in0=o2, in1=gn_b[:], op=ALU.mult)
                I("dve", "tensor_tensor", out=mixr[:], in0=o2, in1=gate[:], op=ALU.mult)
                for k in range(4):
                    K.tr(psT[:, k * 128:(k + 1) * 128], mixr[:, k * 128:(k + 1) * 128], idb[:])
                I("act", "activation", out=mixT_r.v(mixT_r.h[:].rearrange("p a b -> p (a b)")), in_=psT[:, 0:512], func=AF.Copy)
                seq = [(g, kt) for g in range(2) for kt in range(NKT)]
                Fqs = [F[2], F[5]]
                accs = [(F[3], F[4]), (F[0], F[1])]

                def qk(i):
                    g, kt = seq[i]
                    K.mm(Fqs[i % 2][:], attKT[:, kt * 128:(kt + 1) * 128], QTg[g][:])

                qk(0)
                for i, (g, kt) in enumerate(seq):
                    if i + 1 < len(seq):
                        qk(i + 1)
                    pt = PT[i % 3]
                    Fpv, Fsm = accs[g]
                    I("act", "activation", out=pt[:], in_=Fqs[i % 2][:], func=AF.Exp, scale=ATT_SCALE)
                    K.mm(Fpv[:], attV[:, kt, :], pt[:], start=(kt == 0), stop=(kt == NKT - 1))
                    K.mm(Fsm[:], ones[:], pt[:], start=(kt == 0), stop=(kt == NKT - 1))
                    if kt == NKT - 1:
                        gs = slice(g * 64, (g + 1) * 64)
                        I("act", "activation", out=rs[gs, :], in_=Fsm[gs, :], func=AF.Ln)
                        I("act", "activation", out=rs[gs, :], in_=rs[gs, :], func=AF.Exp, scale=-1.0)
                        I("dve", "tensor_tensor", out=attT.v(attT.h[gs, :, :].rearrange("p a b -> p (a b)")), in0=Fpv[gs, :], in1=rs[gs, :], op=ALU.mult)
                for half in range(2):
                    cs = slice(half * 512, (half + 1) * 512)
                    Fy = F[half]
                    for k in range(4):
                        K.mm(Fy[:], mixT_r[:, k, :], wo_r[:, k, cs], start=(k == 0), stop=False)
                    for pr in range(4):
                        K.mm(Fy[:], attT[:, pr, :], wo_a[:, pr, cs], start=False, stop=(pr == 3))
                    I("dve", "tensor_tensor", out=tmp[:, cs], in0=Fy[:], in1=V(g1_b.ap[:, cs], g1_b.deps), op=ALU.mult)
                    I("dve", "tensor_tensor", out=x1[:, cs], in0=tmp[:, cs], in1=xt[:, cs], op=ALU.add)
                K.dma("sp", x1s[t * 128:(t + 1) * 128, :], x1[:], "st1")
                norm_mod_T(x1, A2_b, sh2_b, tmp, hm, h2T)
                K.dma("sp", h2s[:, :, t * 128:(t + 1) * 128], h2T[:], "st2")
            K.barrier()

        es_ab.close()
        with ExitStack() as s3:
            W1 = K.sb("W1", [128, 8, 4096], BF16, s3)
            W2 = K.sb("W2", [128, 32, 1024], BF16, s3)
            hb = K.sb("hbC", [128, 8, 512], BF16, s3)
            act = K.sb("actC", [128, 32, 512], BF16, s3)
            rl = [K.sb("rlC%d" % i, [128, 512], F32, s3) for i in range(2)]
            x1 = K.sb("x1C", [128, 1024], F32, s3)
            x2 = K.sb("x2C", [128, 1024], F32, s3)
            tmp = K.sb("tmpC", [128, 1024], F32, s3)
            ot = K.sb("otC", [128, 1024], F32, s3)
            w1v = w1.rearrange("(k p) n -> p k n", p=128)
            w2v = w2.rearrange("(k p) n -> p k n", p=128)
            for j in range(4):
                K.dma("pool", W1[:, :, j * 1024:(j + 1) * 1024], w1v[:, :, j * 1024:(j + 1) * 1024], "w%d" % j)
            for j in range(4):
                K.dma("pool", W2[:, j * 8:(j + 1) * 8, :], w2v[:, j * 8:(j + 1) * 8, :], "v%d" % j)
            for blk in range(NC):
                K.dma("sp", hb[:], h2s[:, :, blk * 512:(blk + 1) * 512], "hl")
                for fc in range(32):
                    Fo = F[fc % 2]
                    for k in range(8):
                        K.mm(Fo[:], W1[:, k, fc * 128:(fc + 1) * 128], hb[:, k, :], start=(k == 0), stop=(k == 7))
                    r = rl[fc % 2]
                    I("act", "activation", out=r[:], in_=Fo[:], func=AF.Relu)
                    I("pool", "tensor_tensor", out=act[:, fc, :], in0=r[:], in1=r[:], op=ALU.mult)
                for tt in range(4):
                    t = blk * 4 + tt
                    K.dma("sp", x1[:], x1s[t * 128:(t + 1) * 128, :], "xl")
                    for half in range(2):
                        cs = slice(half * 512, (half + 1) * 512)
                        Fy = F[2 + half]
                        for fc in range(32):
                            K.mm(Fy[:], act[:, fc, tt * 128:(tt + 1) * 128], W2[:, fc, cs], start=(fc == 0), stop=(fc == 31))
                        I("dve", "tensor_tensor", out=tmp[:, cs], in0=Fy[:], in1=V(g2_b.ap[:, cs], g2_b.deps), op=ALU.mult)
                        I("dve", "tensor_tensor", out=x2[:, cs], in0=tmp[:, cs], in1=x1[:, cs], op=ALU.add)
                    row_rstd(x2[:], tmp[:], 1024.0)
                    I("dve", "scalar_tensor_tensor", out=ot[:], in0=x2[:], scalar=rst[:, 0:1], in1=gf_b[:], op0=ALU.mult, op1=ALU.mult)
                    K.dma("sp", out[t * 128:(t + 1) * 128, :], ot[:], "out")

        nw, cnt = K.emit()
        K.finish()
    return nc


def _tables():
    Lq = 4096
    t = np.arange(Lq, dtype=np.float32)
    fr = (np.float32(10000.0) ** (-(np.arange(64, dtype=np.float32) / np.float32(64)))).astype(np.float32)
    ang = (t[:, None] * fr[None, :]).astype(np.float32).astype(np.float64)
    rope_r = np.concatenate([np.cos(ang), np.sin(ang)], axis=1).astype(np.float32)
    af = (np.float32(10000.0) ** (-(np.arange(16, dtype=np.float32) / np.float32(16)))).astype(np.float32)
    row = np.repeat(np.arange(64, dtype=np.float32), 64)
    col = np.tile(np.arange(64, dtype=np.float32), 64)
    aang = np.concatenate([(row[:, None] * af[None, :]).astype(np.float32), (col[:, None] * af[None, :]).astype(np.float32)], axis=1).astype(np.float64)
    rope_a = np.concatenate([np.cos(aang), np.sin(aang)], axis=1).astype(np.float32)
    j = np.arange(128, dtype=np.float32)[:, None]
    i = np.arange(128, dtype=np.float32)[None, :]
    cm = np.zeros((128, 6, 128), np.float32)
    cm[:, 0, :] = np.maximum(i - j, 0.0)
    cm[:, 1, :] = (i >= j).astype(np.float32)
    cm[:, 2, :] = np.maximum(j - i, 0.0)
    cm[:, 3, :] = (j >= i).astype(np.float32)
    cm[:, 4, :] = i + 1.0 + 0.0 * j
    cm[:, 5, :] = 128.0 - i + 0.0 * j
    p = np.arange(128, dtype=np.float32)
    pcols = np.stack([p, 127.0 - p, 255.0 - p, 128.0 + p], axis=1).astype(np.float32)
    return rope_r, rope_a, cm, pcols, np.eye(128, dtype=np.float32)


def kernel(x, c, ctx, c_ctx, w_mod, b_mod, norm1_g, norm2_g, w_in, w_out, ret_log_rate, ret_gn_g,
           q_norm_g, k_norm_g, w_ff1, w_ff2, final_norm_g):
    f = lambda a: np.ascontiguousarray(np.asarray(a, dtype=np.float32))
    x = f(x); c = f(c); ctx = f(ctx); c_ctx = f(c_ctx)
    rope_r, rope_a, cm, pcols, ident = _tables()
    shared = {
        "w_mod": f(w_mod)[0], "b_mod": f(b_mod)[0], "norm1_g": f(norm1_g)[0], "norm2_g": f(norm2_g)[0],
        "w_in": f(w_in)[0], "w_out": f(w_out)[0], "rate": f(ret_log_rate)[0].reshape(8),
        "ret_gn_g": f(ret_gn_g)[0], "q_norm_g": f(q_norm_g)[0], "k_norm_g": f(k_norm_g)[0],
        "w_ff1": f(w_ff1)[0], "w_ff2": f(w_ff2)[0], "final_norm_g": f(final_norm_g),
        "rope_r": rope_r, "rope_a": rope_a, "cmat": cm, "pcols": pcols, "ident": ident,
    }
    n = x.shape[0]
    in_maps = []
    for b in range(n):
        m = dict(shared)
        m["x"] = x[b]
        m["ctx"] = ctx[b]
        m["cvec"] = np.ascontiguousarray(np.stack([c[b], c_ctx], axis=0))
        in_maps.append(m)
    nc = build_nc()
    res = run_bass_kernel_spmd(nc, in_maps, core_ids=list(range(n)))
    return np.stack([np.asarray(r["out"], dtype=np.float32) for r in res.results], axis=0)
```

```python
import numpy as np
import concourse.bass as bass
import concourse.mybir as mybir
from contextlib import ExitStack
from concourse.bass_utils import run_bass_kernel_spmd

F32 = mybir.dt.float32
BF16 = mybir.dt.bfloat16
AF = mybir.ActivationFunctionType
ALU = mybir.AluOpType
AX = mybir.AxisListType


class Dep:
    __slots__ = ("w", "readers", "name", "excl")

    def __init__(self, name="", excl=False):
        self.w = None
        self.readers = {}
        self.name = name
        self.excl = excl


class V:
    __slots__ = ("ap", "deps")

    def __init__(self, ap, deps):
        self.ap = ap
        self.deps = deps


class Tile:
    def __init__(self, handle, deps):
        self.h = handle
        self.deps = deps

    def __getitem__(self, idx):
        return V(self.h[idx], self.deps)

    def v(self, ap):
        return V(ap, self.deps)


class SubTile:
    def __init__(self, ap, deps):
        self.h = ap
        self.deps = deps

    def __getitem__(self, idx):
        return V(self.h[idx], self.deps)

    def v(self, ap):
        return V(ap, self.deps)


class Op:
    __slots__ = ("eng", "fn", "waits", "marked", "tick", "dma_sem", "dma_val", "key")


class Kern:
    ENGS = ("pe", "act", "dve", "pool", "sp")

    def __init__(self, nc, es):
        self.nc = nc
        self.es = es
        self.ops = []
        self.eng_obj = {"pe": nc.tensor, "act": nc.scalar, "dve": nc.vector,
                        "pool": nc.gpsimd, "sp": nc.sync}
        self.eng_sem = {e: es.enter_context(nc.semaphore("sem_" + e)) for e in self.ENGS}
        self.dma_sems = {}
        self.dma_cnt = {}
        self.last_op = {e: None for e in self.ENGS}
        self.same_engine_sync = True

    def sb(self, name, shape, dt, es=None):
        h = (es or self.es).enter_context(self.nc.sbuf_tensor(name, list(shape), dt))
        return Tile(h, [Dep(name)])

    def ps(self, name, shape, dt):
        h = self.es.enter_context(self.nc.psum_tensor(name, list(shape), dt))
        return Tile(h, [Dep(name, excl=True)])

    def dsem(self, name):
        s = self.es.enter_context(self.nc.semaphore("dq_" + name))
        self.dma_sems[name] = s
        self.dma_cnt[name] = 0
        return name

    def _add_wait(self, op, prod):
        if prod is None or prod is op:
            return
        if prod.dma_sem is None:
            if prod.eng == op.eng and op.dma_sem is None:
                if op.eng in ("pe", "sp"):
                    return
                if not self.same_engine_sync:
                    return
            prod.marked = True
        op.waits.append(prod)

    def _record(self, eng, fn, reads, writes, dma_sem=None):
        op = Op()
        op.eng = eng
        op.fn = fn
        op.waits = []
        op.marked = False
        op.tick = None
        op.dma_sem = dma_sem
        op.dma_val = None
        if dma_sem is not None:
            self.dma_cnt[dma_sem] += 16
            op.dma_val = self.dma_cnt[dma_sem]
            op.key = ("dma", dma_sem)
        else:
            op.key = eng
        rd = []
        wd = []
        for v in reads:
            for d in v.deps:
                if d.excl:
                    if d not in wd:
                        wd.append(d)
                elif d not in rd:
                    rd.append(d)
        for v in writes:
            for d in v.deps:
                if d not in wd:
                    wd.append(d)
        for d in rd:
            self._add_wait(op, d.w)
        for d in wd:
            self._add_wait(op, d.w)
            for r in d.readers.values():
                self._add_wait(op, r)
        for d in rd:
            if d not in wd:
                d.readers[op.key] = op
        for d in wd:
            d.w = op
            d.readers = {}
        self.ops.append(op)
        if dma_sem is None:
            self.last_op[eng] = op
        return op

    def I(self, eng, method, *, r=(), w=(), **kw):
        reads = list(r)
        writes = list(w)
        args = {}
        for k, v in kw.items():
            if isinstance(v, V):
                args[k] = v.ap
                if k in ("out", "accum_out", "ap"):
                    writes.append(v)
                else:
                    reads.append(v)
            else:
                args[k] = v

        def fn(e, method=method, args=args):
            return getattr(e, method)(**args)

        return self._record(eng, fn, reads, writes)

    def mm(self, out, lhsT, rhs, start=True, stop=True):
        def fn(e, o=out.ap, l=lhsT.ap, r_=rhs.ap, s=start, t=stop):
            return e.matmul(o, lhsT=l, rhs=r_, start=s, stop=t)
        return self._record("pe", fn, [lhsT, rhs], [out])

    def tr(self, out, in_, ident):
        def fn(e, o=out.ap, i=in_.ap, d=ident.ap):
            return e.transpose(o, i, d)
        return self._record("pe", fn, [in_, ident], [out])

    def dma(self, queue, out, in_, sem, **dkw):
        reads = [in_] if isinstance(in_, V) else []
        writes = [out] if isinstance(out, V) else []
        o = out.ap if isinstance(out, V) else out
        i = in_.ap if isinstance(in_, V) else in_

        def fn(e, o=o, i=i, dkw=dkw):
            return e.dma_start(out=o, in_=i, **dkw)
        return self._record(queue, fn, reads, writes, dma_sem=sem)

    def barrier(self):
        lasts = []
        for e in self.ENGS:
            if self.last_op[e] is not None and self.last_op[e].dma_sem is None:
                lasts.append(self.last_op[e])
        dmas = {}
        for op in self.ops:
            if op.dma_sem is not None:
                dmas[op.dma_sem] = op
        for e in ("pe", "act", "dve", "pool", "sp"):
            op = Op()
            op.eng = e
            op.fn = None
            op.waits = []
            op.marked = False
            op.tick = None
            op.dma_sem = None
            op.dma_val = None
            op.key = e
            for p in lasts:
                if p.eng != e or e not in ("pe", "sp"):
                    p.marked = True
                    op.waits.append(p)
            for p in dmas.values():
                op.waits.append(p)
            self.ops.append(op)

    def emit(self):
        cnt = {e: 0 for e in self.ENGS}
        seen = {e: {} for e in self.ENGS}
        n_wait = 0
        for op in self.ops:
            e = self.eng_obj[op.eng]
            for p in op.waits:
                if p.dma_sem is not None:
                    ck = ("dma", p.dma_sem)
                    val = p.dma_val
                    sem = self.dma_sems[p.dma_sem]
                else:
                    ck = p.eng
                    val = p.tick
                    sem = self.eng_sem[p.eng]
                    assert val is not None, "waiting on unticked op"
                if seen[op.eng].get(ck, 0) >= val:
                    continue
                seen[op.eng][ck] = val
                e.wait_ge(sem, val)
                n_wait += 1
            if op.fn is None:
                continue
            ins = op.fn(e)
            if op.dma_sem is not None:
                ins.then_inc(self.dma_sems[op.dma_sem], 16)
            elif op.marked:
                cnt[op.eng] += 1
                op.tick = cnt[op.eng]
                ins.then_inc(self.eng_sem[op.eng], 1)
        return n_wait, cnt

    def finish(self, queue="sp"):
        e = self.eng_obj[queue]
        for name, c in self.dma_cnt.items():
            if c > 0:
                e.wait_ge(self.dma_sems[name], c)

L = 4096
NT = 32
D = 1024
EPS = 1e-6
RET_SCALE = 128 ** -0.5
ATT_SCALE = 64 ** -0.5
NKT = 34
NA = 32
NB = 32
NC = 8


def build_nc():
    nc = bass.Bass("TRN2", target_bir_lowering=False)

    def din(name, shape):
        return nc.dram_tensor(name, list(shape), F32, kind="ExternalInput").ap()

    x = din("x", [L, D])
    ctx = din("ctx", [256, D])
    cvec = din("cvec", [2, D])
    w_mod = din("w_mod", [D, 6144])
    b_mod = din("b_mod", [6144])
    n1g = din("norm1_g", [D])
    n2g = din("norm2_g", [D])
    w_in = din("w_in", [D, 2816])
    w_out = din("w_out", [D, D])
    rate = din("rate", [8])
    gng = din("ret_gn_g", [512])
    qng = din("q_norm_g", [64])
    kng = din("k_norm_g", [64])
    w1 = din("w_ff1", [D, 4096])
    w2 = din("w_ff2", [4096, D])
    fng = din("final_norm_g", [D])
    rope_r = din("rope_r", [L, 128])
    rope_a = din("rope_a", [L, 64])
    cmat = din("cmat", [128, 6, 128])
    pcols = din("pcols", [128, 4])
    ident = din("ident", [128, 128])
    out = nc.dram_tensor("out", [L, D], F32, kind="ExternalOutput").ap()
    x1s = nc.dram_tensor("x1s", [L, D], F32).ap()
    h2s = nc.dram_tensor("h2s", [128, 8, L], BF16).ap()

    with ExitStack() as es:
        K = Kern(nc, es)
        I = K.I
        idb = K.sb("idb", [128, 128], BF16)
        ones = K.sb("ones", [128, 128], BF16)
        g2_t = K.sb("g2_t", [128, 1024], F32)
        ssq = K.sb("ssq", [128, 8], F32)
        rst = K.sb("rst", [128, 8], F32)
        es_ab = ExitStack()
        es.enter_context(es_ab)
        K.es_save = K.es
        K.es = es_ab
        modb = K.sb("modb", [128, 5120], F32)
        gn_b = K.sb("gn_b", [128, 512], F32)
        qg_b = K.sb("qg_b", [128, 64], F32)
        kg_b = K.sb("kg_b", [128, 64], F32)
        intraT = K.sb("intraT", [128, 2, 512], F32)
        QD = K.sb("QD", [128, 2, 512], F32)
        KD = K.sb("KD", [128, 2, 4], F32)
        cdec = K.sb("cdec", [128, 8], F32)
        wc = K.sb("wc", [128, 2, 2, 4], F32)
        SbAll = K.sb("SbAll", [128, NT, 512], BF16)
        attKT = K.sb("attKT", [128, NKT * 128], BF16)
        attV = K.sb("attV", [128, NKT, 192], BF16)
        Sf = K.sb("Sf", [128, 512], F32)
        Sf_bf = K.sb("Sf_bf", [128, 512], BF16)
        K.es = K.es_save
        es_c = ExitStack()
        es_ab.enter_context(es_c)
        cmodb = K.sb("cmodb", [128, 2048], F32, es_c)
        Sb = K.sb("Sb", [128, 512], F32, es_c)
        G0 = K.ps("G0", [128, 512], F32)
        G1 = K.ps("G1", [128, 512], F32)
        acc0 = K.ps("acc0", [128, 512], F32)
        acc1 = K.ps("acc1", [128, 512], F32)
        QA = K.ps("QA", [128, 1024], F32)
        QB = K.ps("QB", [128, 1024], F32)
        QA.deps = [Dep("QA0", True), Dep("QA1", True)]
        QB.deps = [Dep("QB0", True), Dep("QB1", True)]
        psT = SubTile(G0.h[:].bitcast(BF16), G0.deps)
        psT2 = SubTile(G1.h[:].bitcast(BF16), G1.deps)
        F = [acc0, acc1,
             SubTile(QA.h[:, 0:512], [QA.deps[0]]), SubTile(QA.h[:, 512:1024], [QA.deps[1]]),
             SubTile(QB.h[:, 0:512], [QB.deps[0]]), SubTile(QB.h[:, 512:1024], [QB.deps[1]])]
        for nm in ["x0", "x1", "x2", "rpa0", "rpa1", "rpb0", "rpb1", "swp", "x", "rp", "rp2", "w0", "w1", "w2", "w3", "w4", "v0", "v1", "v2", "v3", "st1", "st2", "out", "hl", "xl"] + ["cst%d" % i for i in range(2, 15)]:
            K.dsem(nm)

        def rstd_from(acc_v, n, out_v):
            I("act", "activation", out=out_v, in_=acc_v, func=AF.Ln, scale=1.0 / n, bias=EPS)
            I("act", "activation", out=out_v, in_=out_v, func=AF.Exp, scale=-0.5)

        def row_rstd(src_v, junk_v, n):
            I("dve", "memset", ap=ssq[:, 0:1], constant=0.0)
            I("act", "activation", out=junk_v, in_=src_v, func=AF.Square, accum_out=ssq[:, 0:1])
            rstd_from(ssq[:, 0:1], n, rst[:, 0:1])

        with ExitStack() as s0:
            idf = K.sb("idf", [128, 128], F32, s0)
            cc = K.sb("cc", [16, 128], F32, s0)
            ccb = K.sb("ccb", [16, 128], BF16, s0)
            cact = K.sb("cact", [128, 16], BF16, s0)
            cb = K.sb("cb", [128, 16, 128], BF16, s0)
            wms = [K.sb("wms%d" % i, [128, 8, 512], BF16, s0) for i in range(2)]
            bmod_b = K.sb("bmod_b", [128, 6144], F32, s0)
            n1g_b = K.sb("n1g_b", [128, 1024], F32, s0)
            n2g_b = K.sb("n2g_b", [128, 1024], F32, s0)
            rate_b = K.sb("rate_b", [128, 8], F32, s0)
            lg = K.sb("lg", [128, 8], F32, s0)
            cm = K.sb("cm", [128, 6, 128], F32, s0)
            pc = K.sb("pc", [128, 4], F32, s0)
            tmpe = K.sb("tmpe", [128, 128], F32, s0)
            K.dma("sp", idf[:], ident[:, :], "cst2")
            K.dma("sp", cc[:], cvec.rearrange("r (k p) -> (r k) p", p=128), "cst3")
            K.dma("sp", bmod_b[:], b_mod.partition_broadcast(128), "cst5")
            K.dma("sp", n1g_b[:], n1g.partition_broadcast(128), "cst6")
            K.dma("sp", n2g_b[:], n2g.partition_broadcast(128), "cst7")
            K.dma("sp", gn_b[:], gng.partition_broadcast(128), "cst9")
            K.dma("sp", qg_b[:], qng.partition_broadcast(128), "cst10")
            K.dma("sp", kg_b[:], kng.partition_broadcast(128), "cst11")
            K.dma("sp", rate_b[:], rate.partition_broadcast(128), "cst12")
            K.dma("sp", cm[:], cmat[:, :, :], "cst13")
            K.dma("sp", pc[:], pcols[:, :], "cst14")
            I("dve", "tensor_copy", out=idb[:], in_=idf[:])
            I("dve", "memset", ap=ones[:], constant=1.0)
            I("act", "activation", out=ccb[:], in_=cc[:], func=AF.Silu)
            K.tr(psT2[:, 0:16], ccb[:], idb[0:16, 0:16])
            I("dve", "tensor_copy", out=cact[:], in_=psT2[:, 0:16])
            for j in range(16):
                I("dve", "tensor_copy", out=cb[:, j, :], in_=cact.v(cact.h[:, j:j + 1].to_broadcast([128, 128])))
            wmv = w_mod.rearrange("(k p) n -> p k n", p=128)
            for jb in range(12):
                ws = wms[jb % 2]
                K.dma("pool", ws[:], wmv[:, :, jb * 512:(jb + 1) * 512], "w%d" % (jb % 2))
                for k in range(8):
                    K.mm(F[0][:], cb[:, k, :], ws[:, k, :], start=(k == 0), stop=(k == 7))
                dst = modb[:, jb * 512:(jb + 1) * 512] if jb < 10 else g2_t[:, (jb - 10) * 512:(jb - 9) * 512]
                I("dve", "tensor_tensor", out=dst, in0=F[0][:], in1=bmod_b[:, jb * 512:(jb + 1) * 512], op=ALU.add)
                if jb < 4:
                    for k in range(8):
                        K.mm(F[1][:], cb[:, 8 + k, :], ws[:, k, :], start=(k == 0), stop=(k == 7))
                    I("dve", "tensor_tensor", out=cmodb[:, jb * 512:(jb + 1) * 512], in0=F[1][:], in1=bmod_b[:, jb * 512:(jb + 1) * 512], op=ALU.add)
            I("dve", "scalar_tensor_tensor", out=modb[:, 1024:2048], in0=modb[:, 1024:2048], scalar=1.0, in1=n1g_b[:], op0=ALU.add, op1=ALU.mult)
            I("dve", "scalar_tensor_tensor", out=modb[:, 4096:5120], in0=modb[:, 4096:5120], scalar=1.0, in1=n2g_b[:], op0=ALU.add, op1=ALU.mult)
            I("dve", "scalar_tensor_tensor", out=cmodb[:, 1024:2048], in0=cmodb[:, 1024:2048], scalar=1.0, in1=n1g_b[:], op0=ALU.add, op1=ALU.mult)
            I("act", "activation", out=lg[:], in_=rate_b[:], func=AF.Exp)
            I("act", "activation", out=lg[:], in_=lg[:], func=AF.Ln, scale=-1.0, bias=1.0)
            for d in range(2):
                for h in range(4):
                    c = d * 4 + h
                    sc = lg[:, c:c + 1]
                    I("act", "activation", out=tmpe[:], in_=cm[:, 2 * d, :], func=AF.Exp, scale=sc)
                    I("dve", "tensor_tensor", out=intraT[:, d, h * 128:(h + 1) * 128], in0=tmpe[:], in1=cm[:, 2 * d + 1, :], op=ALU.mult)
                    I("act", "activation", out=QD[:, d, h * 128:(h + 1) * 128], in_=cm[:, 4 + d, :], func=AF.Exp, scale=sc)
                    I("act", "activation", out=KD[:, d, h:h + 1], in_=pc[:, (1 if d == 0 else 0):(2 if d == 0 else 1)], func=AF.Exp, scale=sc)
                    for t in range(2):
                        col = (2 if t == 0 else 1) if d == 0 else (0 if t == 0 else 3)
                        I("act", "activation", out=wc[:, t, d, h:h + 1], in_=pc[:, col:col + 1], func=AF.Exp, scale=sc)
            I("act", "activation", out=cdec[:], in_=lg[:], func=AF.Exp, scale=128.0)
            I("dve", "tensor_scalar", out=KD[:], in0=KD[:], scalar1=RET_SCALE, scalar2=None, op0=ALU.mult)
            I("dve", "tensor_scalar", out=wc[:], in0=wc[:], scalar1=RET_SCALE, scalar2=None, op0=ALU.mult)
            K.barrier()

        sh1_b = modb[:, 0:1024]; A1_b = modb[:, 1024:2048]; g1_b = modb[:, 2048:3072]
        sh2_b = modb[:, 3072:4096]; A2_b = modb[:, 4096:5120]; g2_b = g2_t[:, :]
        csh1_b = cmodb[:, 0:1024]; cA1_b = cmodb[:, 1024:2048]

        def norm_mod_T(xt, A_v, sh_v, tmp, hm, hT):
            row_rstd(xt[:], tmp[:], 1024.0)
            I("dve", "scalar_tensor_tensor", out=tmp[:], in0=xt[:], scalar=rst[:, 0:1], in1=A_v, op0=ALU.mult, op1=ALU.mult)
            I("dve", "tensor_tensor", out=hm[:], in0=tmp[:], in1=sh_v, op=ALU.add)
            for k in range(8):
                K.tr(psT[:, k * 128:(k + 1) * 128], hm[:, k * 128:(k + 1) * 128], idb[:])
            I("act", "activation", out=hT.v(hT.h[:].rearrange("p a b -> p (a b)")), in_=psT[:], func=AF.Copy)

        def v3(t, ap):
            return t.v(ap)

        def rope(src, H, half, cos_v, sin_v, t1, t2, dst_lo, dst_hi):
            s_lo = V(src.ap[:, :, 0:half], src.deps)
            s_hi = V(src.ap[:, :, half:2 * half], src.deps)
            cb_ = V(cos_v.ap.unsqueeze(1).to_broadcast([128, H, half]), cos_v.deps)
            sb_ = V(sin_v.ap.unsqueeze(1).to_broadcast([128, H, half]), sin_v.deps)
            I("dve", "tensor_tensor", out=t1, in0=s_lo, in1=cb_, op=ALU.mult)
            I("dve", "tensor_tensor", out=t2, in0=s_hi, in1=sb_, op=ALU.mult)
            I("dve", "tensor_tensor", out=dst_lo, in0=t1, in1=t2, op=ALU.subtract)
            I("dve", "tensor_tensor", out=t1, in0=s_lo, in1=sb_, op=ALU.mult)
            I("dve", "tensor_tensor", out=t2, in0=s_hi, in1=cb_, op=ALU.mult)
            I("dve", "tensor_tensor", out=dst_hi, in0=t1, in1=t2, op=ALU.add)

        def head_rstd(src_v, H, n, sq, nh):
            I("act", "activation", out=sq, in_=src_v, func=AF.Square)
            I("dve", "tensor_reduce", out=ssq[:, 0:H], in_=sq, axis=AX.X, op=ALU.add)
            rstd_from(ssq[:, 0:H], float(n), rst[:, 0:H])

        with ExitStack() as s1:
            wA = K.sb("wA", [128, 8, 1280], BF16, s1)
            xt = K.sb("xtA", [128, 1024], F32, s1)
            tmp = K.sb("tmpA", [128, 1024], F32, s1)
            hm = K.sb("hmA", [128, 1024], BF16, s1)
            hT = K.sb("hTA", [128, 8, 128], BF16, s1)
            cr = K.sb("crA", [128, 128], F32, s1)
            ca = K.sb("caA", [128, 64], F32, s1)
            kr = K.sb("krA", [128, 4, 128], F32, s1)
            t1 = K.sb("t1A", [128, 4, 64], F32, s1)
            t2 = K.sb("t2A", [128, 4, 64], F32, s1)
            kdb = [K.sb("kdbA%d" % i, [128, 4, 128], BF16, s1) for i in range(2)]
            kdf = [K.sb("kdfA%d" % i, [128, 4, 128], BF16, s1) for i in range(2)]
            vbf = [K.sb("vbfA%d" % i, [128, 4, 128], BF16, s1) for i in range(2)]
            sqk = K.sb("sqkA", [128, 2, 64], F32, s1)
            kn = K.sb("knA", [128, 2, 64], F32, s1)
            kro = K.sb("kroA", [128, 2, 64], BF16, s1)
            wiv = w_in.rearrange("(k p) n -> p k n", p=128)
            K.dma("pool", wA[:, :, 0:1024], wiv[:, :, 512:1536], "w0")
            K.dma("pool", wA[:, :, 1024:1280], wiv[:, :, 2560:2816], "w1")
            I("dve", "memset", ap=attV[:, :, 64:128], constant=1.0)

            def tileA(src_ap, is_ctx, idx, kt):
                K.dma("sp", xt[:], src_ap, "x")
                if not is_ctx:
                    K.dma("sp", cr[:], rope_r[idx * 128:(idx + 1) * 128, :], "rp")
                    K.dma("sp", ca[:], rope_a[idx * 128:(idx + 1) * 128, :], "rp2")
                norm_mod_T(xt, cA1_b if is_ctx else A1_b, csh1_b if is_ctx else sh1_b, tmp, hm, hT)
                for k in range(8):
                    K.mm(F[0][:], hT[:, k, :], wA[:, k, 0:512], start=(k == 0), stop=(k == 7))
                for k in range(8):
                    K.mm(F[1][:], hT[:, k, :], wA[:, k, 512:1024], start=(k == 0), stop=(k == 7))
                for k in range(8):
                    K.mm(F[2][:, 0:256], hT[:, k, :], wA[:, k, 1024:1280], start=(k == 0), stop=(k == 7))
                akv = F[2].v(F[2].h[:, 0:128].rearrange("p (h d) -> p h d", h=2))
                head_rstd(akv, 2, 64, sqk[:], 2)
                I("dve", "tensor_tensor", out=kn[:], in0=akv, in1=rst.v(rst.h[:, 0:2].unsqueeze(2).to_broadcast([128, 2, 64])), op=ALU.mult)
                I("dve", "tensor_tensor", out=kn[:], in0=kn[:], in1=kg_b.v(kg_b.h[:].unsqueeze(1).to_broadcast([128, 2, 64])), op=ALU.mult)
                if is_ctx:
                    I("dve", "tensor_copy", out=kro[:], in_=kn[:])
                else:
                    rope(kn[:], 2, 32, ca[:, 0:32], ca[:, 32:64], t1.v(t1.h[:, 0:2, 0:32]), t2.v(t2.h[:, 0:2, 0:32]),
                         kro[:, :, 0:32], kro[:, :, 32:64])
                K.tr(psT2[:, 0:128], kro.v(kro.h[:].rearrange("p h d -> p (h d)")), idb[:])
                I("act", "activation", out=attKT[:, kt * 128:(kt + 1) * 128], in_=psT2[:, 0:128], func=AF.Copy)
                I("dve", "tensor_copy", out=attV[:, kt, 0:64], in_=F[2][:, 128:192])
                I("dve", "tensor_copy", out=attV[:, kt, 128:192], in_=F[2][:, 192:256])
                rk4 = F[0].v(F[0].h[:].rearrange("p (h d) -> p h d", h=4))
                if is_ctx:
                    I("dve", "tensor_copy", out=kr[:], in_=rk4)
                else:
                    rope(rk4, 4, 64, cr[:, 0:64], cr[:, 64:128], t1[:], t2[:], kr[:, :, 0:64], kr[:, :, 64:128])
                b = idx % 2 if is_ctx else 0
                I("act", "activation", out=vbf[b].v(vbf[b].h[:].rearrange("p h d -> p (h d)")), in_=F[1][:], func=AF.Copy)
                if is_ctx:
                    I("dve", "tensor_tensor", out=kdf[b][:], in0=kr[:], in1=wc.v(wc.h[:, idx, 0, :].unsqueeze(2).to_broadcast([128, 4, 128])), op=ALU.mult)
                    I("dve", "tensor_tensor", out=kdb[b][:], in0=kr[:], in1=wc.v(wc.h[:, idx, 1, :].unsqueeze(2).to_broadcast([128, 4, 128])), op=ALU.mult)
                else:
                    I("dve", "tensor_tensor", out=kdb[0][:], in0=kr[:], in1=KD.v(KD.h[:, 1, :].unsqueeze(2).to_broadcast([128, 4, 128])), op=ALU.mult)
                    for h in range(4):
                        K.mm(F[3][:, h * 128:(h + 1) * 128], kdb[0][:, h, :], vbf[0][:, h, :])
                    I("act", "activation", out=SbAll[:, idx, :], in_=Sb[:], func=AF.Copy)
                    for h in range(4):
                        I("dve", "scalar_tensor_tensor", out=Sb[:, h * 128:(h + 1) * 128], in0=Sb[:, h * 128:(h + 1) * 128],
                          scalar=cdec[:, 4 + h:5 + h], in1=F[3][:, h * 128:(h + 1) * 128], op0=ALU.mult, op1=ALU.add)

            tileA(ctx[0:128, :], True, 0, 0)
            tileA(ctx[128:256, :], True, 1, 1)
            for h in range(4):
                for t in range(2):
                    K.mm(F[3][:, h * 128:(h + 1) * 128], kdf[t][:, h, :], vbf[t][:, h, :], start=(t == 0), stop=(t == 1))
            for h in range(4):
                for t in range(2):
                    K.mm(F[4][:, h * 128:(h + 1) * 128], kdb[t][:, h, :], vbf[t][:, h, :], start=(t == 0), stop=(t == 1))
            I("dve", "tensor_copy", out=Sf[:], in_=F[3][:])
            I("dve", "tensor_copy", out=Sb[:], in_=F[4][:])
            I("dve", "tensor_copy", out=Sf_bf[:], in_=Sf[:])
            for t in range(NT - 1, NT - 1 - NA, -1):
                tileA(x[t * 128:(t + 1) * 128, :], False, t, 2 + t)
            K.barrier()

        es_c.close()
        with ExitStack() as s2:
            wi = K.sb("wi", [128, 8, 2560], BF16, s2)
            wo_r = K.sb("wo_r", [128, 4, 1024], BF16, s2)
            wo_a = K.sb("wo_a", [128, 4, 1024], BF16, s2)
            xts = [K.sb("xtB%d" % i, [128, 1024], F32, s2) for i in range(3)]
            crs = [K.sb("crB%d" % i, [128, 128], F32, s2) for i in range(2)]
            cas = [K.sb("caB%d" % i, [128, 64], F32, s2) for i in range(2)]
            tmp = K.sb("tmpB", [128, 1024], F32, s2)
            hm = K.sb("hmB", [128, 1024], BF16, s2)
            hT = K.sb("hTB", [128, 8, 128], BF16, s2)
            kr = K.sb("krB", [128, 4, 128], F32, s2)
            t1 = K.sb("t1B", [128, 4, 64], F32, s2)
            t2 = K.sb("t2B", [128, 4, 64], F32, s2)
            qr = K.sb("qrB", [128, 4, 128], BF16, s2)
            k_s = K.sb("k_sB", [128, 4, 128], BF16, s2)
            kdf = K.sb("kdfB", [128, 4, 128], BF16, s2)
            vbf = K.sb("vbfB", [128, 4, 128], BF16, s2)
            gate = K.sb("gateB", [128, 512], F32, s2)
            qro = K.sb("qroB", [128, 8, 64], BF16, s2)
            qrp = K.sb("qrpB", [128, 4, 2, 64], BF16, s2)
            qT = K.sb("qTB", [128, 512], BF16, s2)
            qdTf = K.sb("qdTfB", [128, 512], BF16, s2)
            qdTb = K.sb("qdTbB", [128, 512], BF16, s2)
            kT = K.sb("kTB", [128, 512], BF16, s2)
            QTgs = [[K.sb("QTg%d_%dB" % (g, i), [128, 512], BF16, s2) for g in range(2)] for i in range(2)]
            STf = K.sb("STfB", [128, 512], BF16, s2)
            STb = K.sb("STbB", [128, 512], BF16, s2)
            o_sb = K.sb("o_sbB", [128, 4, 128], F32, s2)
            st4 = K.sb("st4B", [128, 16], F32, s2)
            mixr = K.sb("mixrB", [128, 512], BF16, s2)
            mixT_rs = [K.sb("mixT_rB%d" % i, [128, 4, 128], BF16, s2) for i in range(2)]
            attT = K.sb("attTB", [128, 4, 128], BF16, s2)
            PT = [K.sb("PT%dB" % i, [128, 1024], BF16, s2) for i in range(2)]
            pvraw = K.sb("pvrawB", [128, 512], F32, s2)
            rsw = K.sb("rswB", [128, 512], F32, s2)
            rs = K.sb("rsB", [128, 512], F32, s2)
            sq8 = kr.v(kr.h[:].rearrange("p h (a d) -> p (h a) d", a=2))
            osq = kr
            qn = o_sb.v(o_sb.h[:].rearrange("p h (a d) -> p (h a) d", a=2))
            wiv = w_in.rearrange("(k p) n -> p k n", p=128)
            K.dma("pool", wi[:, :, 0:1280], wiv[:, :, 0:1280], "w0")
            K.dma("pool", wi[:, :, 1280:2560], wiv[:, :, 1280:2560], "w1")
            K.dma("pool", wo_r[:], w_out[0:512, :].rearrange("(k p) n -> p k n", p=128), "w2")
            for g in range(2):
                K.dma("pool", wo_a[g * 64:(g + 1) * 64, :, :],
                      w_out[512 + g * 256:512 + (g + 1) * 256, :].rearrange("(pr d) n -> d pr n", d=64), "w%d" % (3 + g))
            for i in range(2):
                for g in range(2):
                    I("dve", "memset", ap=QTgs[i][g][:], constant=0.0)
            G = [G0, G1]
            Gb = [psT, psT2]

            def load_tile(t):
                b = t % 2
                K.dma("sp", xts[t % 3][:], x[t * 128:(t + 1) * 128, :], "x%d" % (t % 3))
                K.dma("sp", crs[b][:], rope_r[t * 128:(t + 1) * 128, :], "rpa%d" % b)
                K.dma("sp", cas[b][:], rope_a[t * 128:(t + 1) * 128, :], "rpb%d" % b)

            def proj(j, Gt):
                for k in range(8):
                    K.mm(Gt[:], hT[:, k, :], wi[:, k, j * 512:(j + 1) * 512], start=(k == 0), stop=(k == 7))

            def prologue(t):
                b = t % 2
                xt, cr, ca = xts[t % 3], crs[b], cas[b]
                QTg = QTgs[b]
                S = []

                def step(f):
                    S.append(f)
                    return f

                def gap():
                    S.append(None)

                @step
                def _():
                    I("dve", "memset", ap=ssq[:, 0:1], constant=0.0)
                    I("act", "activation", out=tmp[:], in_=xt[:], func=AF.Square, accum_out=ssq[:, 0:1])

                @step
                def _():
                    rstd_from(ssq[:, 0:1], 1024.0, rst[:, 0:1])
                gap()

                @step
                def _():
                    I("dve", "scalar_tensor_tensor", out=tmp[:], in0=xt[:], scalar=rst[:, 0:1], in1=A1_b, op0=ALU.mult, op1=ALU.mult)
                    I("dve", "tensor_tensor", out=hm[:], in0=tmp[:], in1=sh1_b, op=ALU.add)
                gap()

                @step
                def _():
                    for k in range(8):
                        K.tr(psT[:, k * 128:(k + 1) * 128], hm[:, k * 128:(k + 1) * 128], idb[:])

                @step
                def _():
                    I("dve", "tensor_copy", out=hT.v(hT.h[:].rearrange("p a b -> p (a b)")), in_=psT[:])

                @step
                def _():
                    proj(0, G1)

                @step
                def _():
                    proj(1, G0)
                    rq4 = G1.v(G1.h[:].rearrange("p (h d) -> p h d", h=4))
                    rope(rq4, 4, 64, cr[:, 0:64], cr[:, 64:128], t1[:], t2[:], qr[:, :, 0:64], qr[:, :, 64:128])

                @step
                def _():
                    rk4 = G0.v(G0.h[:].rearrange("p (h d) -> p h d", h=4))
                    rope(rk4, 4, 64, cr[:, 0:64], cr[:, 64:128], t1[:], t2[:], kr[:, :, 0:64], kr[:, :, 64:128])
                    I("dve", "tensor_scalar", out=k_s[:], in0=kr[:], scalar1=RET_SCALE, scalar2=None, op0=ALU.mult)
                    I("dve", "tensor_tensor", out=kdf[:], in0=kr[:], in1=KD.v(KD.h[:, 0, :].unsqueeze(2).to_broadcast([128, 4, 128])), op=ALU.mult)

                @step
                def _():
                    proj(2, G1)
                gap()

                @step
                def _():
                    proj(3, G0)
                    I("dve", "tensor_copy", out=vbf.v(vbf.h[:].rearrange("p h d -> p (h d)")), in_=G1[:])
                gap()

                @step
                def _():
                    I("act", "activation", out=gate[:], in_=G0[:], func=AF.Exp, scale=-1.0)

                @step
                def _():
                    I("act", "activation", out=gate[:], in_=gate[:], func=AF.Ln, scale=1.0, bias=1.0)
                    I("act", "activation", out=gate[:], in_=gate[:], func=AF.Exp, scale=-1.0)
                    proj(4, G1)
                gap()

                @step
                def _():
                    I("dve", "tensor_tensor", out=gate[:], in0=G0[:], in1=gate[:], op=ALU.mult)
                    aq8 = G1.v(G1.h[:].rearrange("p (h d) -> p h d", h=8))
                    I("act", "activation", out=sq8, in_=aq8, func=AF.Square)

                @step
                def _():
                    I("dve", "tensor_reduce", out=ssq[:, 0:8], in_=sq8, axis=AX.X, op=ALU.add)
                gap()

                @step
                def _():
                    rstd_from(ssq[:, 0:8], 64.0, rst[:, 0:8])
                gap()

                @step
                def _():
                    aq8 = G1.v(G1.h[:].rearrange("p (h d) -> p h d", h=8))
                    I("dve", "tensor_tensor", out=qn, in0=aq8, in1=rst.v(rst.h[:, 0:8].unsqueeze(2).to_broadcast([128, 8, 64])), op=ALU.mult)
                    I("dve", "tensor_tensor", out=qn, in0=qn, in1=qg_b.v(qg_b.h[:].unsqueeze(1).to_broadcast([128, 8, 64])), op=ALU.mult)

                @step
                def _():
                    t1v = t1.v(t1.h[:].rearrange("p h (a d) -> p (h a) d", a=2))
                    t2v = t2.v(t2.h[:].rearrange("p h (a d) -> p (h a) d", a=2))
                    rope(qn, 8, 32, ca[:, 0:32], ca[:, 32:64], t1v, t2v, qro[:, :, 0:32], qro[:, :, 32:64])
                    for e in range(2):
                        I("dve", "tensor_copy", out=qrp[:, :, e, :], in_=qro[:, e * 4:(e + 1) * 4, :])

                @step
                def _():
                    for h in range(4):
                        K.tr(psT[:, h * 128:(h + 1) * 128], qr[:, h, :], idb[:])
                    for h in range(4):
                        K.tr(psT[:, 512 + h * 128:512 + (h + 1) * 128], k_s[:, h, :], idb[:])
                gap()

                @step
                def _():
                    for pr in range(4):
                        K.tr(psT2[:, pr * 128:(pr + 1) * 128], qrp.v(qrp.h[:, pr, :, :].rearrange("p e d -> p (e d)")), idb[:])
                    I("dve", "tensor_copy", out=qT[:], in_=psT[:, 0:512])
                    I("dve", "tensor_tensor", out=qdTf[:], in0=psT[:, 0:512], in1=QD[:, 0, :], op=ALU.mult)

                @step
                def _():
                    I("dve", "tensor_tensor", out=qdTb[:], in0=psT[:, 0:512], in1=QD[:, 1, :], op=ALU.mult)
                    I("dve", "tensor_copy", out=kT[:], in_=psT[:, 512:1024])

                @step
                def _():
                    I("dve", "tensor_copy", out=QTg[0][0:64, :], in_=psT2[0:64, 0:512])
                    I("dve", "tensor_copy", out=QTg[1][64:128, :], in_=psT2[64:128, 0:512])
                gap()

                @step
                def _():
                    for h in range(4):
                        hs = slice(h * 128, (h + 1) * 128)
                        K.mm(G0[:, hs], kT[:, hs], qT[:, hs])

                @step
                def _():
                    for h in range(4):
                        hs = slice(h * 128, (h + 1) * 128)
                        K.mm(G1[:, hs], kdf[:, h, :], vbf[:, h, :])
                    I("dve", "tensor_tensor", out=STf[:], in0=G0[:], in1=intraT[:, 0, :], op=ALU.mult)
                    I("dve", "tensor_tensor", out=STb[:], in0=G0[:], in1=intraT[:, 1, :], op=ALU.mult)
                gap()

                @step
                def _():
                    for h in range(4):
                        hs = slice(h * 128, (h + 1) * 128)
                        K.mm(G0[:, hs], STf[:, hs], vbf[:, h, :], start=True, stop=False)
                        K.mm(G0[:, hs], STb[:, hs], vbf[:, h, :], start=False, stop=False)
                        K.mm(G0[:, hs], qdTf[:, hs], Sf_bf[:, hs], start=False, stop=False)
                        K.mm(G0[:, hs], qdTb[:, hs], SbAll[:, t, hs], start=False, stop=True)

                @step
                def _():
                    for h in range(4):
                        hs = slice(h * 128, (h + 1) * 128)
                        I("dve", "scalar_tensor_tensor", out=Sf[:, hs], in0=Sf[:, hs], scalar=cdec[:, h:h + 1], in1=G1[:, hs], op0=ALU.mult, op1=ALU.add)
                    I("dve", "tensor_copy", out=Sf_bf[:], in_=Sf[:])

                @step
                def _():
                    I("dve", "tensor_copy", out=o_sb.v(o_sb.h[:].rearrange("p h d -> p (h d)")), in_=G0[:])
                    I("dve", "tensor_reduce", out=st4[:, 0:4], in_=o_sb[:], axis=AX.X, op=ALU.add)
                    I("dve", "tensor_scalar", out=st4[:, 0:4], in0=st4[:, 0:4], scalar1=1.0 / 128.0, scalar2=None, op0=ALU.mult)
                    I("dve", "tensor_tensor", out=o_sb[:], in0=o_sb[:], in1=st4.v(st4.h[:, 0:4].unsqueeze(2).to_broadcast([128, 4, 128])), op=ALU.subtract)

                @step
                def _():
                    I("dve", "tensor_tensor", out=osq[:], in0=o_sb[:], in1=o_sb[:], op=ALU.mult)
                    I("dve", "tensor_reduce", out=st4[:, 4:8], in_=osq[:], axis=AX.X, op=ALU.add)
                gap()

                @step
                def _():
                    I("act", "activation", out=st4[:, 12:16], in_=st4[:, 4:8], func=AF.Ln, scale=1.0 / 128.0, bias=EPS)
                    I("act", "activation", out=st4[:, 12:16], in_=st4[:, 12:16], func=AF.Exp, scale=-0.5)
                gap()

                @step
                def _():
                    I("dve", "tensor_tensor", out=o_sb[:], in0=o_sb[:], in1=st4.v(st4.h[:, 12:16].unsqueeze(2).to_broadcast([128, 4, 128])), op=ALU.mult)
                    o2 = o_sb.v(o_sb.h[:].rearrange("p h d -> p (h d)"))
                    I("dve", "tensor_tensor", out=o2, in0=o2, in1=gn_b[:], op=ALU.mult)
                    I("dve", "tensor_tensor", out=mixr[:], in0=o2, in1=gate[:], op=ALU.mult)
                gap()

                @step
                def _():
                    for k in range(4):
                        K.tr(psT2[:, k * 128:(k + 1) * 128], mixr[:, k * 128:(k + 1) * 128], idb[:])

                @step
                def _():
                    I("dve", "tensor_copy", out=mixT_rs[b].v(mixT_rs[b].h[:].rearrange("p a b -> p (a b)")), in_=psT2[:, 0:512])
                return S

            def attention(t):
                QTg = QTgs[t % 2]
                Qs = [QA, QB]

                def qk(kt):
                    Q = Qs[kt % 2]
                    K.mm(Q[:, 0:512], attKT[:, kt * 128:(kt + 1) * 128], QTg[0][:])
                    K.mm(Q[:, 512:1024], attKT[:, kt * 128:(kt + 1) * 128], QTg[1][:])

                qk(0)
                for kt in range(NKT):
                    if kt + 1 < NKT:
                        qk(kt + 1)
                    pt = PT[kt % 2]
                    I("act", "activation", out=pt[:], in_=Qs[kt % 2][:], func=AF.Exp, scale=ATT_SCALE)
                    K.mm(acc0[:], attV[:, kt, 0:128], pt[:, 0:512], start=(kt == 0), stop=(kt == NKT - 1))
                    K.mm(acc1[:], attV[:, kt, 64:192], pt[:, 512:1024], start=(kt == 0), stop=(kt == NKT - 1))
                    yield
                I("act", "activation", out=rsw[64:128, :], in_=acc0[64:128, :], func=AF.Ln)
                I("act", "activation", out=rsw[0:64, :], in_=acc1[0:64, :], func=AF.Ln)
                I("act", "activation", out=rsw[:], in_=rsw[:], func=AF.Exp, scale=-1.0)
                I("dve", "tensor_copy", out=pvraw[0:64, :], in_=acc0[0:64, :])
                I("dve", "tensor_copy", out=pvraw[64:128, :], in_=acc1[64:128, :])
                K.dma("sp", rs[0:64, :], rsw[64:128, :], "swp")
                K.dma("sp", rs[64:128, :], rsw[0:64, :], "swp")
                yield

            def epilogue(t):
                b = t % 2
                xt = xts[t % 3]
                S = []

                def step(f):
                    S.append(f)
                    return f

                def gap():
                    S.append(None)

                @step
                def _():
                    I("dve", "tensor_tensor", out=attT.v(attT.h[:].rearrange("p a b -> p (a b)")), in0=pvraw[:], in1=rs[:], op=ALU.mult)

                @step
                def _():
                    for half in range(2):
                        cs = slice(half * 512, (half + 1) * 512)
                        Gy = G[half]
                        for k in range(4):
                            K.mm(Gy[:], mixT_rs[b][:, k, :], wo_r[:, k, cs], start=(k == 0), stop=False)
                        for pr in range(4):
                            K.mm(Gy[:], attT[:, pr, :], wo_a[:, pr, cs], start=False, stop=(pr == 3))

                @step
                def _():
                    for half in range(2):
                        cs = slice(half * 512, (half + 1) * 512)
                        I("dve", "tensor_tensor", out=tmp[:, cs], in0=G[half][:], in1=V(g1_b.ap[:, cs], g1_b.deps), op=ALU.mult)
                        I("dve", "tensor_tensor", out=xt[:, cs], in0=tmp[:, cs], in1=xt[:, cs], op=ALU.add)
                gap()

                @step
                def _():
                    K.dma("sp", x1s[t * 128:(t + 1) * 128, :], xt[:], "st1")
                    I("dve", "memset", ap=ssq[:, 0:1], constant=0.0)
                    I("act", "activation", out=tmp[:], in_=xt[:], func=AF.Square, accum_out=ssq[:, 0:1])

                @step
                def _():
                    rstd_from(ssq[:, 0:1], 1024.0, rst[:, 0:1])
                gap()

                @step
                def _():
                    I("dve", "scalar_tensor_tensor", out=tmp[:], in0=xt[:], scalar=rst[:, 0:1], in1=A2_b, op0=ALU.mult, op1=ALU.mult)
                    I("dve", "tensor_tensor", out=hm[:], in0=tmp[:], in1=sh2_b, op=ALU.add)
                gap()

                @step
                def _():
                    for k in range(8):
                        K.tr(psT[:, k * 128:(k + 1) * 128], hm[:, k * 128:(k + 1) * 128], idb[:])

                @step
                def _():
                    I("dve", "tensor_copy", out=hT.v(hT.h[:].rearrange("p a b -> p (a b)")), in_=psT[:])
                    K.dma("sp", h2s[:, :, t * 128:(t + 1) * 128], hT[:], "st2")
                return S

            def run(steps):
                for f in steps:
                    if f is not None:
                        f()

            load_tile(0)
            if NB > 1:
                load_tile(1)
            if NB > 0:
                run(prologue(0))
            for t in range(NB):
                side = (epilogue(t - 1) if t > 0 else []) + (prologue(t + 1) if t + 1 < NB else [])
                n = len(side)
                done = 0
                it = 0
                for _ in attention(t):
                    it += 1
                    upto = min(n, (it * n + NKT - 1) // NKT)
                    run(side[done:upto])
                    done = upto
                run(side[done:])
                if t + 2 < NB:
                    load_tile(t + 2)
            if NB > 0:
                run(epilogue(NB - 1))
            K.barrier()

        es_ab.close()
        with ExitStack() as s3:
            W1 = K.sb("W1", [128, 8, 4096], BF16, s3)
            W2 = K.sb("W2", [128, 32, 1024], BF16, s3)
            hb = K.sb("hbC", [128, 8, 512], BF16, s3)
            act = K.sb("actC", [128, 32, 512], BF16, s3)
            rl = [K.sb("rlC%d" % i, [128, 512], F32, s3) for i in range(2)]
            x1 = K.sb("x1C", [128, 1024], F32, s3)
            x2 = K.sb("x2C", [128, 1024], F32, s3)
            tmp = K.sb("tmpC", [128, 1024], F32, s3)
            ot = K.sb("otC", [128, 1024], F32, s3)
            gf_b = K.sb("gf_b", [128, 1024], F32, s3)
            K.dma("sp", gf_b[:], fng.partition_broadcast(128), "cst8")
            w1v = w1.rearrange("(k p) n -> p k n", p=128)
            w2v = w2.rearrange("(k p) n -> p k n", p=128)
            for j in range(4):
                K.dma("pool", W1[:, :, j * 1024:(j + 1) * 1024], w1v[:, :, j * 1024:(j + 1) * 1024], "w%d" % j)
            for j in range(4):
                K.dma("pool", W2[:, j * 8:(j + 1) * 8, :], w2v[:, j * 8:(j + 1) * 8, :], "v%d" % j)
            for blk in range(NC):
                K.dma("sp", hb[:], h2s[:, :, blk * 512:(blk + 1) * 512], "hl")
                for fc in range(32):
                    Fo = F[fc % 2]
                    for k in range(8):
                        K.mm(Fo[:], W1[:, k, fc * 128:(fc + 1) * 128], hb[:, k, :], start=(k == 0), stop=(k == 7))
                    r = rl[fc % 2]
                    I("act", "activation", out=r[:], in_=Fo[:], func=AF.Relu)
                    I("pool", "tensor_tensor", out=act[:, fc, :], in0=r[:], in1=r[:], op=ALU.mult)
                for tt in range(4):
                    t = blk * 4 + tt
                    K.dma("sp", x1[:], x1s[t * 128:(t + 1) * 128, :], "xl")
                    for half in range(2):
                        cs = slice(half * 512, (half + 1) * 512)
                        Fy = F[2 + half]
                        for fc in range(32):
                            K.mm(Fy[:], act[:, fc, tt * 128:(tt + 1) * 128], W2[:, fc, cs], start=(fc == 0), stop=(fc == 31))
                        I("dve", "tensor_tensor", out=tmp[:, cs], in0=Fy[:], in1=V(g2_b.ap[:, cs], g2_b.deps), op=ALU.mult)
                        I("dve", "tensor_tensor", out=x2[:, cs], in0=tmp[:, cs], in1=x1[:, cs], op=ALU.add)
                    row_rstd(x2[:], tmp[:], 1024.0)
                    I("dve", "scalar_tensor_tensor", out=ot[:], in0=x2[:], scalar=rst[:, 0:1], in1=gf_b[:], op0=ALU.mult, op1=ALU.mult)
                    K.dma("sp", out[t * 128:(t + 1) * 128, :], ot[:], "out")

        nw, cnt = K.emit()
        K.finish()
    return nc


def _tables():
    Lq = 4096
    t = np.arange(Lq, dtype=np.float32)
    fr = (np.float32(10000.0) ** (-(np.arange(64, dtype=np.float32) / np.float32(64)))).astype(np.float32)
    ang = (t[:, None] * fr[None, :]).astype(np.float32).astype(np.float64)
    rope_r = np.concatenate([np.cos(ang), np.sin(ang)], axis=1).astype(np.float32)
    af = (np.float32(10000.0) ** (-(np.arange(16, dtype=np.float32) / np.float32(16)))).astype(np.float32)
    row = np.repeat(np.arange(64, dtype=np.float32), 64)
    col = np.tile(np.arange(64, dtype=np.float32), 64)
    aang = np.concatenate([(row[:, None] * af[None, :]).astype(np.float32), (col[:, None] * af[None, :]).astype(np.float32)], axis=1).astype(np.float64)
    rope_a = np.concatenate([np.cos(aang), np.sin(aang)], axis=1).astype(np.float32)
    j = np.arange(128, dtype=np.float32)[:, None]
    i = np.arange(128, dtype=np.float32)[None, :]
    cm = np.zeros((128, 6, 128), np.float32)
    cm[:, 0, :] = np.maximum(i - j, 0.0)
    cm[:, 1, :] = (i >= j).astype(np.float32)
    cm[:, 2, :] = np.maximum(j - i, 0.0)
    cm[:, 3, :] = (j >= i).astype(np.float32)
    cm[:, 4, :] = i + 1.0 + 0.0 * j
    cm[:, 5, :] = 128.0 - i + 0.0 * j
    p = np.arange(128, dtype=np.float32)
    pcols = np.stack([p, 127.0 - p, 255.0 - p, 128.0 + p], axis=1).astype(np.float32)
    return rope_r, rope_a, cm, pcols, np.eye(128, dtype=np.float32)


def kernel(x, c, ctx, c_ctx, w_mod, b_mod, norm1_g, norm2_g, w_in, w_out, ret_log_rate, ret_gn_g,
           q_norm_g, k_norm_g, w_ff1, w_ff2, final_norm_g):
    f = lambda a: np.ascontiguousarray(np.asarray(a, dtype=np.float32))
    x = f(x); c = f(c); ctx = f(ctx); c_ctx = f(c_ctx)
    rope_r, rope_a, cm, pcols, ident = _tables()
    shared = {
        "w_mod": f(w_mod)[0], "b_mod": f(b_mod)[0], "norm1_g": f(norm1_g)[0], "norm2_g": f(norm2_g)[0],
        "w_in": f(w_in)[0], "w_out": f(w_out)[0], "rate": f(ret_log_rate)[0].reshape(8),
        "ret_gn_g": f(ret_gn_g)[0], "q_norm_g": f(q_norm_g)[0], "k_norm_g": f(k_norm_g)[0],
        "w_ff1": f(w_ff1)[0], "w_ff2": f(w_ff2)[0], "final_norm_g": f(final_norm_g),
        "rope_r": rope_r, "rope_a": rope_a, "cmat": cm, "pcols": pcols, "ident": ident,
    }
    n = x.shape[0]
    in_maps = []
    for b in range(n):
        m = dict(shared)
        m["x"] = x[b]
        m["ctx"] = ctx[b]
        m["cvec"] = np.ascontiguousarray(np.stack([c[b], c_ctx], axis=0))
        in_maps.append(m)
    nc = build_nc()
    res = run_bass_kernel_spmd(nc, in_maps, core_ids=list(range(n)))
    return np.stack([np.asarray(r["out"], dtype=np.float32) for r in res.results], axis=0)
```

```python
import numpy as np
import concourse.bass as bass
import concourse.mybir as mybir
from contextlib import ExitStack
from concourse.bass_utils import run_bass_kernel_spmd

F32 = mybir.dt.float32
BF16 = mybir.dt.bfloat16
AF = mybir.ActivationFunctionType
ALU = mybir.AluOpType
AX = mybir.AxisListType


class Dep:
    __slots__ = ("w", "readers", "name", "excl")

    def __init__(self, name="", excl=False):
        self.w = None
        self.readers = {}
        self.name = name
        self.excl = excl


class V:
    __slots__ = ("ap", "deps")

    def __init__(self, ap, deps):
        self.ap = ap
        self.deps = deps


class Tile:
    def __init__(self, handle, deps):
        self.h = handle
        self.deps = deps

    def __getitem__(self, idx):
        return V(self.h[idx], self.deps)

    def v(self, ap):
        return V(ap, self.deps)


class SubTile:
    def __init__(self, ap, deps):
        self.h = ap
        self.deps = deps

    def __getitem__(self, idx):
        return V(self.h[idx], self.deps)

    def v(self, ap):
        return V(ap, self.deps)


class Op:
    __slots__ = ("eng", "fn", "waits", "marked", "tick", "dma_sem", "dma_val", "key")


class Kern:
    ENGS = ("pe", "act", "dve", "pool", "sp")

    def __init__(self, nc, es):
        self.nc = nc
        self.es = es
        self.ops = []
        self.eng_obj = {"pe": nc.tensor, "act": nc.scalar, "dve": nc.vector,
                        "pool": nc.gpsimd, "sp": nc.sync}
        self.eng_sem = {e: es.enter_context(nc.semaphore("sem_" + e)) for e in self.ENGS}
        self.dma_sems = {}
        self.dma_cnt = {}
        self.last_op = {e: None for e in self.ENGS}
        self.same_engine_sync = True

    def sb(self, name, shape, dt, es=None):
        h = (es or self.es).enter_context(self.nc.sbuf_tensor(name, list(shape), dt))
        return Tile(h, [Dep(name)])

    def ps(self, name, shape, dt):
        h = self.es.enter_context(self.nc.psum_tensor(name, list(shape), dt))
        return Tile(h, [Dep(name, excl=True)])

    def dsem(self, name):
        s = self.es.enter_context(self.nc.semaphore("dq_" + name))
        self.dma_sems[name] = s
        self.dma_cnt[name] = 0
        return name

    def _add_wait(self, op, prod):
        if prod is None or prod is op:
            return
        if prod.dma_sem is None:
            if prod.eng == op.eng and op.dma_sem is None:
                if op.eng in ("pe", "sp"):
                    return
                if not self.same_engine_sync:
                    return
            prod.marked = True
        op.waits.append(prod)

    def _record(self, eng, fn, reads, writes, dma_sem=None):
        op = Op()
        op.eng = eng
        op.fn = fn
        op.waits = []
        op.marked = False
        op.tick = None
        op.dma_sem = dma_sem
        op.dma_val = None
        if dma_sem is not None:
            self.dma_cnt[dma_sem] += 16
            op.dma_val = self.dma_cnt[dma_sem]
            op.key = ("dma", dma_sem)
        else:
            op.key = eng
        rd = []
        wd = []
        for v in reads:
            for d in v.deps:
                if d.excl:
                    if d not in wd:
                        wd.append(d)
                elif d not in rd:
                    rd.append(d)
        for v in writes:
            for d in v.deps:
                if d not in wd:
                    wd.append(d)
        for d in rd:
            self._add_wait(op, d.w)
        for d in wd:
            self._add_wait(op, d.w)
            for r in d.readers.values():
                self._add_wait(op, r)
        for d in rd:
            if d not in wd:
                d.readers[op.key] = op
        for d in wd:
            d.w = op
            d.readers = {}
        self.ops.append(op)
        if dma_sem is None:
            self.last_op[eng] = op
        return op

    def I(self, eng, method, *, r=(), w=(), **kw):
        reads = list(r)
        writes = list(w)
        args = {}
        for k, v in kw.items():
            if isinstance(v, V):
                args[k] = v.ap
                if k in ("out", "accum_out", "ap"):
                    writes.append(v)
                else:
                    reads.append(v)
            else:
                args[k] = v

        def fn(e, method=method, args=args):
            return getattr(e, method)(**args)

        return self._record(eng, fn, reads, writes)

    def mm(self, out, lhsT, rhs, start=True, stop=True):
        def fn(e, o=out.ap, l=lhsT.ap, r_=rhs.ap, s=start, t=stop):
            return e.matmul(o, lhsT=l, rhs=r_, start=s, stop=t)
        return self._record("pe", fn, [lhsT, rhs], [out])

    def tr(self, out, in_, ident):
        def fn(e, o=out.ap, i=in_.ap, d=ident.ap):
            return e.transpose(o, i, d)
        return self._record("pe", fn, [in_, ident], [out])

    def dma(self, queue, out, in_, sem, **dkw):
        reads = [in_] if isinstance(in_, V) else []
        writes = [out] if isinstance(out, V) else []
        o = out.ap if isinstance(out, V) else out
        i = in_.ap if isinstance(in_, V) else in_

        def fn(e, o=o, i=i, dkw=dkw):
            return e.dma_start(out=o, in_=i, **dkw)
        return self._record(queue, fn, reads, writes, dma_sem=sem)

    def barrier(self):
        lasts = []
        for e in self.ENGS:
            if self.last_op[e] is not None and self.last_op[e].dma_sem is None:
                lasts.append(self.last_op[e])
        dmas = {}
        for op in self.ops:
            if op.dma_sem is not None:
                dmas[op.dma_sem] = op
        for e in ("pe", "act", "dve", "pool", "sp"):
            op = Op()
            op.eng = e
            op.fn = None
            op.waits = []
            op.marked = False
            op.tick = None
            op.dma_sem = None
            op.dma_val = None
            op.key = e
            for p in lasts:
                if p.eng != e or e not in ("pe", "sp"):
                    p.marked = True
                    op.waits.append(p)
            for p in dmas.values():
                op.waits.append(p)
            self.ops.append(op)

    def emit(self):
        cnt = {e: 0 for e in self.ENGS}
        seen = {e: {} for e in self.ENGS}
        n_wait = 0
        for op in self.ops:
            e = self.eng_obj[op.eng]
            for p in op.waits:
                if p.dma_sem is not None:
                    ck = ("dma", p.dma_sem)
                    val = p.dma_val
                    sem = self.dma_sems[p.dma_sem]
                else:
                    ck = p.eng
                    val = p.tick
                    sem = self.eng_sem[p.eng]
                    assert val is not None, "waiting on unticked op"
                if seen[op.eng].get(ck, 0) >= val:
                    continue
                seen[op.eng][ck] = val
                e.wait_ge(sem, val)
                n_wait += 1
            if op.fn is None:
                continue
            ins = op.fn(e)
            if op.dma_sem is not None:
                ins.then_inc(self.dma_sems[op.dma_sem], 16)
            elif op.marked:
                cnt[op.eng] += 1
                op.tick = cnt[op.eng]
                ins.then_inc(self.eng_sem[op.eng], 1)
        return n_wait, cnt

    def finish(self, queue="sp"):
        e = self.eng_obj[queue]
        for name, c in self.dma_cnt.items():
            if c > 0:
                e.wait_ge(self.dma_sems[name], c)

L = 4096
NT = 32
D = 1024
EPS = 1e-6
RET_SCALE = 128 ** -0.5
ATT_SCALE = 64 ** -0.5
NKT = 34
NA = 32
NB = 32
NC = 8


def build_nc():
    nc = bass.Bass("TRN2", target_bir_lowering=False)

    def din(name, shape):
        return nc.dram_tensor(name, list(shape), F32, kind="ExternalInput").ap()

    x = din("x", [L, D])
    ctx = din("ctx", [256, D])
    cvec = din("cvec", [2, D])
    w_mod = din("w_mod", [D, 6144])
    b_mod = din("b_mod", [6144])
    n1g = din("norm1_g", [D])
    n2g = din("norm2_g", [D])
    w_in = din("w_in", [D, 2816])
    w_out = din("w_out", [D, D])
    rate = din("rate", [8])
    gng = din("ret_gn_g", [512])
    qng = din("q_norm_g", [64])
    kng = din("k_norm_g", [64])
    w1 = din("w_ff1", [D, 4096])
    w2 = din("w_ff2", [4096, D])
    fng = din("final_norm_g", [D])
    rope_r = din("rope_r", [L, 128])
    rope_a = din("rope_a", [L, 64])
    cmat = din("cmat", [128, 6, 128])
    pcols = din("pcols", [128, 4])
    ident = din("ident", [128, 128])
    out = nc.dram_tensor("out", [L, D], F32, kind="ExternalOutput").ap()
    x1s = nc.dram_tensor("x1s", [L, D], F32).ap()
    h2s = nc.dram_tensor("h2s", [128, 8, L], BF16).ap()

    with ExitStack() as es:
        K = Kern(nc, es)
        I = K.I
        idb = K.sb("idb", [128, 128], BF16)
        ones = K.sb("ones", [128, 128], BF16)
        g2_t = K.sb("g2_t", [128, 1024], F32)
        ssq = K.sb("ssq", [128, 8], F32)
        rst = K.sb("rst", [128, 8], F32)
        es_ab = ExitStack()
        es.enter_context(es_ab)
        K.es_save = K.es
        K.es = es_ab
        modb = K.sb("modb", [128, 5120], F32)
        gn_b = K.sb("gn_b", [128, 512], F32)
        qg_b = K.sb("qg_b", [128, 64], F32)
        kg_b = K.sb("kg_b", [128, 64], F32)
        intraT = K.sb("intraT", [128, 2, 512], F32)
        QD = K.sb("QD", [128, 2, 512], F32)
        KD = K.sb("KD", [128, 2, 4], F32)
        cdec = K.sb("cdec", [128, 8], F32)
        wc = K.sb("wc", [128, 2, 2, 4], F32)
        SbAll = K.sb("SbAll", [128, NT, 512], BF16)
        attKT = K.sb("attKT", [128, NKT * 128], BF16)
        attV = K.sb("attV", [128, NKT, 192], BF16)
        Sf = K.sb("Sf", [128, 512], F32)
        Sf_bf = K.sb("Sf_bf", [128, 512], BF16)
        wi = K.sb("wi", [128, 8, 2560], BF16)
        K.es = K.es_save
        es_c = ExitStack()
        es_ab.enter_context(es_c)
        cmodb = K.sb("cmodb", [128, 2048], F32, es_c)
        Sb = K.sb("Sb", [128, 512], F32, es_c)
        G0 = K.ps("G0", [128, 512], F32)
        G1 = K.ps("G1", [128, 512], F32)
        acc0 = K.ps("acc0", [128, 512], F32)
        acc1 = K.ps("acc1", [128, 512], F32)
        QA = K.ps("QA", [128, 1024], F32)
        QB = K.ps("QB", [128, 1024], F32)
        QA.deps = [Dep("QA0", True), Dep("QA1", True)]
        QB.deps = [Dep("QB0", True), Dep("QB1", True)]
        psT = SubTile(G0.h[:].bitcast(BF16), G0.deps)
        psT2 = SubTile(G1.h[:].bitcast(BF16), G1.deps)
        F = [acc0, acc1,
             SubTile(QA.h[:, 0:512], [QA.deps[0]]), SubTile(QA.h[:, 512:1024], [QA.deps[1]]),
             SubTile(QB.h[:, 0:512], [QB.deps[0]]), SubTile(QB.h[:, 512:1024], [QB.deps[1]])]
        for nm in ["ra0", "ra1", "ra2", "rb0", "rb1", "rb2", "wi0", "wi1", "x0", "x1", "x2", "rpa0", "rpa1", "rpb0", "rpb1", "swp", "x", "rp", "rp2", "w0", "w1", "w2", "w3", "w4", "v0", "v1", "v2", "v3", "st1", "st2", "out", "hl", "xl"] + ["cst%d" % i for i in range(2, 15)]:
            K.dsem(nm)

        def rstd_from(acc_v, n, out_v):
            I("act", "activation", out=out_v, in_=acc_v, func=AF.Ln, scale=1.0 / n, bias=EPS)
            I("act", "activation", out=out_v, in_=out_v, func=AF.Exp, scale=-0.5)

        def row_rstd(src_v, junk_v, n):
            I("dve", "memset", ap=ssq[:, 0:1], constant=0.0)
            I("act", "activation", out=junk_v, in_=src_v, func=AF.Square, accum_out=ssq[:, 0:1])
            rstd_from(ssq[:, 0:1], n, rst[:, 0:1])

        with ExitStack() as s0:
            idf = K.sb("idf", [128, 128], F32, s0)
            cc = K.sb("cc", [16, 128], F32, s0)
            ccb = K.sb("ccb", [16, 128], BF16, s0)
            cact = K.sb("cact", [128, 16], BF16, s0)
            cb = K.sb("cb", [128, 16, 128], BF16, s0)
            wms = [K.sb("wms%d" % i, [128, 8, 512], BF16, s0) for i in range(2)]
            bmod_b = K.sb("bmod_b", [128, 6144], F32, s0)
            n1g_b = K.sb("n1g_b", [128, 1024], F32, s0)
            n2g_b = K.sb("n2g_b", [128, 1024], F32, s0)
            rate_b = K.sb("rate_b", [128, 8], F32, s0)
            lg = K.sb("lg", [128, 8], F32, s0)
            cm = K.sb("cm", [128, 6, 128], F32, s0)
            pc = K.sb("pc", [128, 4], F32, s0)
            tmpe = K.sb("tmpe", [128, 128], F32, s0)
            K.dma("sp", idf[:], ident[:, :], "cst2")
            K.dma("sp", cc[:], cvec.rearrange("r (k p) -> (r k) p", p=128), "cst3")
            K.dma("sp", bmod_b[:], b_mod.partition_broadcast(128), "cst5")
            K.dma("sp", n1g_b[:], n1g.partition_broadcast(128), "cst6")
            K.dma("sp", n2g_b[:], n2g.partition_broadcast(128), "cst7")
            K.dma("sp", gn_b[:], gng.partition_broadcast(128), "cst9")
            K.dma("sp", qg_b[:], qng.partition_broadcast(128), "cst10")
            K.dma("sp", kg_b[:], kng.partition_broadcast(128), "cst11")
            K.dma("sp", rate_b[:], rate.partition_broadcast(128), "cst12")
            K.dma("sp", cm[:], cmat[:, :, :], "cst13")
            K.dma("sp", pc[:], pcols[:, :], "cst14")
            I("dve", "tensor_copy", out=idb[:], in_=idf[:])
            I("dve", "memset", ap=ones[:], constant=1.0)
            I("act", "activation", out=ccb[:], in_=cc[:], func=AF.Silu)
            K.tr(psT2[:, 0:16], ccb[:], idb[0:16, 0:16])
            I("dve", "tensor_copy", out=cact[:], in_=psT2[:, 0:16])
            for j in range(16):
                I("dve", "tensor_copy", out=cb[:, j, :], in_=cact.v(cact.h[:, j:j + 1].to_broadcast([128, 128])))
            wmv = w_mod.rearrange("(k p) n -> p k n", p=128)
            for jb in range(12):
                ws = wms[jb % 2]
                K.dma("pool", ws[:], wmv[:, :, jb * 512:(jb + 1) * 512], "w%d" % (jb % 2))
                for k in range(8):
                    K.mm(F[0][:], cb[:, k, :], ws[:, k, :], start=(k == 0), stop=(k == 7))
                dst = modb[:, jb * 512:(jb + 1) * 512] if jb < 10 else g2_t[:, (jb - 10) * 512:(jb - 9) * 512]
                I("dve", "tensor_tensor", out=dst, in0=F[0][:], in1=bmod_b[:, jb * 512:(jb + 1) * 512], op=ALU.add)
                if jb < 4:
                    for k in range(8):
                        K.mm(F[1][:], cb[:, 8 + k, :], ws[:, k, :], start=(k == 0), stop=(k == 7))
                    I("dve", "tensor_tensor", out=cmodb[:, jb * 512:(jb + 1) * 512], in0=F[1][:], in1=bmod_b[:, jb * 512:(jb + 1) * 512], op=ALU.add)
            I("dve", "scalar_tensor_tensor", out=modb[:, 1024:2048], in0=modb[:, 1024:2048], scalar=1.0, in1=n1g_b[:], op0=ALU.add, op1=ALU.mult)
            I("dve", "scalar_tensor_tensor", out=modb[:, 4096:5120], in0=modb[:, 4096:5120], scalar=1.0, in1=n2g_b[:], op0=ALU.add, op1=ALU.mult)
            I("dve", "scalar_tensor_tensor", out=cmodb[:, 1024:2048], in0=cmodb[:, 1024:2048], scalar=1.0, in1=n1g_b[:], op0=ALU.add, op1=ALU.mult)
            I("act", "activation", out=lg[:], in_=rate_b[:], func=AF.Exp)
            I("act", "activation", out=lg[:], in_=lg[:], func=AF.Ln, scale=-1.0, bias=1.0)
            for d in range(2):
                for h in range(4):
                    c = d * 4 + h
                    sc = lg[:, c:c + 1]
                    I("act", "activation", out=tmpe[:], in_=cm[:, 2 * d, :], func=AF.Exp, scale=sc)
                    I("dve", "tensor_tensor", out=intraT[:, d, h * 128:(h + 1) * 128], in0=tmpe[:], in1=cm[:, 2 * d + 1, :], op=ALU.mult)
                    I("act", "activation", out=QD[:, d, h * 128:(h + 1) * 128], in_=cm[:, 4 + d, :], func=AF.Exp, scale=sc)
                    I("act", "activation", out=KD[:, d, h:h + 1], in_=pc[:, (1 if d == 0 else 0):(2 if d == 0 else 1)], func=AF.Exp, scale=sc)
                    for t in range(2):
                        col = (2 if t == 0 else 1) if d == 0 else (0 if t == 0 else 3)
                        I("act", "activation", out=wc[:, t, d, h:h + 1], in_=pc[:, col:col + 1], func=AF.Exp, scale=sc)
            I("act", "activation", out=cdec[:], in_=lg[:], func=AF.Exp, scale=128.0)
            I("dve", "tensor_scalar", out=KD[:], in0=KD[:], scalar1=RET_SCALE, scalar2=None, op0=ALU.mult)
            I("dve", "tensor_scalar", out=wc[:], in0=wc[:], scalar1=RET_SCALE, scalar2=None, op0=ALU.mult)
            K.barrier()

        sh1_b = modb[:, 0:1024]; A1_b = modb[:, 1024:2048]; g1_b = modb[:, 2048:3072]
        sh2_b = modb[:, 3072:4096]; A2_b = modb[:, 4096:5120]; g2_b = g2_t[:, :]
        csh1_b = cmodb[:, 0:1024]; cA1_b = cmodb[:, 1024:2048]

        def norm_mod_T(xt, A_v, sh_v, tmp, hm, hT):
            row_rstd(xt[:], tmp[:], 1024.0)
            I("dve", "scalar_tensor_tensor", out=tmp[:], in0=xt[:], scalar=rst[:, 0:1], in1=A_v, op0=ALU.mult, op1=ALU.mult)
            I("dve", "tensor_tensor", out=hm[:], in0=tmp[:], in1=sh_v, op=ALU.add)
            for k in range(8):
                K.tr(psT[:, k * 128:(k + 1) * 128], hm[:, k * 128:(k + 1) * 128], idb[:])
            I("act", "activation", out=hT.v(hT.h[:].rearrange("p a b -> p (a b)")), in_=psT[:], func=AF.Copy)

        def v3(t, ap):
            return t.v(ap)

        def rope(src, H, half, cos_v, sin_v, t1, t2, dst_lo, dst_hi):
            s_lo = V(src.ap[:, :, 0:half], src.deps)
            s_hi = V(src.ap[:, :, half:2 * half], src.deps)
            cb_ = V(cos_v.ap.unsqueeze(1).to_broadcast([128, H, half]), cos_v.deps)
            sb_ = V(sin_v.ap.unsqueeze(1).to_broadcast([128, H, half]), sin_v.deps)
            I("dve", "tensor_tensor", out=t1, in0=s_lo, in1=cb_, op=ALU.mult)
            I("dve", "tensor_tensor", out=t2, in0=s_hi, in1=sb_, op=ALU.mult)
            I("dve", "tensor_tensor", out=dst_lo, in0=t1, in1=t2, op=ALU.subtract)
            I("dve", "tensor_tensor", out=t1, in0=s_lo, in1=sb_, op=ALU.mult)
            I("dve", "tensor_tensor", out=t2, in0=s_hi, in1=cb_, op=ALU.mult)
            I("dve", "tensor_tensor", out=dst_hi, in0=t1, in1=t2, op=ALU.add)

        def head_rstd(src_v, H, n, sq, nh):
            I("act", "activation", out=sq, in_=src_v, func=AF.Square)
            I("dve", "tensor_reduce", out=ssq[:, 0:H], in_=sq, axis=AX.X, op=ALU.add)
            rstd_from(ssq[:, 0:H], float(n), rst[:, 0:H])

        with ExitStack() as s1:
            wA = K.sb("wA", [128, 8, 1280], BF16, s1)
            xtl = [K.sb("xtA%d" % i, [128, 1024], F32, s1) for i in range(3)]
            tmp = K.sb("tmpA", [128, 1024], F32, s1)
            hml = [K.sb("hmA%d" % i, [128, 1024], BF16, s1) for i in range(2)]
            hTl = [K.sb("hTA%d" % i, [128, 8, 128], BF16, s1) for i in range(2)]
            crl = [K.sb("crA%d" % i, [128, 128], F32, s1) for i in range(3)]
            cal = [K.sb("caA%d" % i, [128, 64], F32, s1) for i in range(3)]
            kr = K.sb("krA", [128, 4, 128], F32, s1)
            t1 = K.sb("t1A", [128, 4, 64], F32, s1)
            t2 = K.sb("t2A", [128, 4, 64], F32, s1)
            kdb = [K.sb("kdbA%d" % i, [128, 4, 128], BF16, s1) for i in range(2)]
            kdf = [K.sb("kdfA%d" % i, [128, 4, 128], BF16, s1) for i in range(2)]
            vbf = [K.sb("vbfA%d" % i, [128, 4, 128], BF16, s1) for i in range(2)]
            sqk = K.sb("sqkA", [128, 2, 64], F32, s1)
            kn = K.sb("knA", [128, 2, 64], F32, s1)
            kro = K.sb("kroA", [128, 2, 64], BF16, s1)
            ssq2 = K.sb("ssq2A", [128, 2], F32, s1)
            rst2 = K.sb("rst2A", [128, 2], F32, s1)
            wiv = w_in.rearrange("(k p) n -> p k n", p=128)
            K.dma("pool", wA[:, :, 0:1024], wiv[:, :, 512:1536], "w0")
            K.dma("pool", wA[:, :, 1024:1280], wiv[:, :, 2560:2816], "w1")
            I("dve", "memset", ap=attV[:, :, 64:128], constant=1.0)
            tiles = [(ctx[0:128, :], True, 0, 0), (ctx[128:256, :], True, 1, 1)]
            for t in range(NT - 1, NT - 1 - NA, -1):
                tiles.append((x[t * 128:(t + 1) * 128, :], False, t, 2 + t))

            def loadA(i):
                src_ap, is_ctx, idx, kt = tiles[i]
                K.dma("sp", xtl[i % 3][:], src_ap, "x%d" % (i % 3))
                if not is_ctx:
                    K.dma("sp", crl[i % 3][:], rope_r[idx * 128:(idx + 1) * 128, :], "rpa%d" % (i % 2) if False else "ra%d" % (i % 3))
                    K.dma("sp", cal[i % 3][:], rope_a[idx * 128:(idx + 1) * 128, :], "rb%d" % (i % 3))

            def stage1(i):
                src_ap, is_ctx, idx, kt = tiles[i]
                xt, hm, hT = xtl[i % 3], hml[i % 2], hTl[i % 2]
                P = F[0:3] if i % 2 == 0 else F[3:6]
                norm_mod_T(xt, cA1_b if is_ctx else A1_b, csh1_b if is_ctx else sh1_b, tmp, hm, hT)
                for k in range(8):
                    K.mm(P[0][:], hT[:, k, :], wA[:, k, 0:512], start=(k == 0), stop=(k == 7))
                for k in range(8):
                    K.mm(P[1][:], hT[:, k, :], wA[:, k, 512:1024], start=(k == 0), stop=(k == 7))
                for k in range(8):
                    K.mm(P[2][:, 0:256], hT[:, k, :], wA[:, k, 1024:1280], start=(k == 0), stop=(k == 7))

            def stage2(i):
                src_ap, is_ctx, idx, kt = tiles[i]
                cr, ca = crl[i % 3], cal[i % 3]
                P = F[0:3] if i % 2 == 0 else F[3:6]
                akv = P[2].v(P[2].h[:, 0:128].rearrange("p (h d) -> p h d", h=2))
                I("act", "activation", out=sqk[:], in_=akv, func=AF.Square)
                I("dve", "tensor_reduce", out=ssq2[:], in_=sqk[:], axis=AX.X, op=ALU.add)
                rstd_from(ssq2[:], 64.0, rst2[:])
                I("dve", "tensor_tensor", out=kn[:], in0=akv, in1=rst2.v(rst2.h[:, 0:2].unsqueeze(2).to_broadcast([128, 2, 64])), op=ALU.mult)
                I("dve", "tensor_tensor", out=kn[:], in0=kn[:], in1=kg_b.v(kg_b.h[:].unsqueeze(1).to_broadcast([128, 2, 64])), op=ALU.mult)
                if is_ctx:
                    I("dve", "tensor_copy", out=kro[:], in_=kn[:])
                else:
                    rope(kn[:], 2, 32, ca[:, 0:32], ca[:, 32:64], t1.v(t1.h[:, 0:2, 0:32]), t2.v(t2.h[:, 0:2, 0:32]),
                         kro[:, :, 0:32], kro[:, :, 32:64])
                K.tr(psT2[:, 0:128], kro.v(kro.h[:].rearrange("p h d -> p (h d)")), idb[:])
                I("act", "activation", out=attKT[:, kt * 128:(kt + 1) * 128], in_=psT2[:, 0:128], func=AF.Copy)
                I("dve", "tensor_copy", out=attV[:, kt, 0:64], in_=P[2][:, 128:192])
                I("dve", "tensor_copy", out=attV[:, kt, 128:192], in_=P[2][:, 192:256])
                rk4 = P[0].v(P[0].h[:].rearrange("p (h d) -> p h d", h=4))
                if is_ctx:
                    I("dve", "tensor_copy", out=kr[:], in_=rk4)
                else:
                    rope(rk4, 4, 64, cr[:, 0:64], cr[:, 64:128], t1[:], t2[:], kr[:, :, 0:64], kr[:, :, 64:128])
                b = idx % 2 if is_ctx else 0
                I("act", "activation", out=vbf[b].v(vbf[b].h[:].rearrange("p h d -> p (h d)")), in_=P[1][:], func=AF.Copy)
                if is_ctx:
                    I("dve", "tensor_tensor", out=kdf[b][:], in0=kr[:], in1=wc.v(wc.h[:, idx, 0, :].unsqueeze(2).to_broadcast([128, 4, 128])), op=ALU.mult)
                    I("dve", "tensor_tensor", out=kdb[b][:], in0=kr[:], in1=wc.v(wc.h[:, idx, 1, :].unsqueeze(2).to_broadcast([128, 4, 128])), op=ALU.mult)
                else:
                    I("dve", "tensor_tensor", out=kdb[0][:], in0=kr[:], in1=KD.v(KD.h[:, 1, :].unsqueeze(2).to_broadcast([128, 4, 128])), op=ALU.mult)
                    for h in range(4):
                        K.mm(G1[:, h * 128:(h + 1) * 128], kdb[0][:, h, :], vbf[0][:, h, :])
                    I("act", "activation", out=SbAll[:, idx, :], in_=Sb[:], func=AF.Copy)
                    for h in range(4):
                        I("dve", "scalar_tensor_tensor", out=Sb[:, h * 128:(h + 1) * 128], in0=Sb[:, h * 128:(h + 1) * 128],
                          scalar=cdec[:, 4 + h:5 + h], in1=G1[:, h * 128:(h + 1) * 128], op0=ALU.mult, op1=ALU.add)

            def ctx_states():
                for h in range(4):
                    for t in range(2):
                        K.mm(G1[:, h * 128:(h + 1) * 128], kdf[t][:, h, :], vbf[t][:, h, :], start=(t == 0), stop=(t == 1))
                I("dve", "tensor_copy", out=Sf[:], in_=G1[:])
                for h in range(4):
                    for t in range(2):
                        K.mm(G1[:, h * 128:(h + 1) * 128], kdb[t][:, h, :], vbf[t][:, h, :], start=(t == 0), stop=(t == 1))
                I("dve", "tensor_copy", out=Sb[:], in_=G1[:])
                I("dve", "tensor_copy", out=Sf_bf[:], in_=Sf[:])

            n_tiles = len(tiles)
            loadA(0)
            loadA(1)
            stage1(0)
            stage2(0)
            loadA(2) if n_tiles > 2 else None
            stage1(1)
            stage2(1)
            ctx_states()
            K.dma("pool", wi[:, :, 0:1280], wiv[:, :, 0:1280], "wi0")
            K.dma("pool", wi[:, :, 1280:2560], wiv[:, :, 1280:2560], "wi1")
            if n_tiles > 3:
                loadA(3)
            if n_tiles > 2:
                stage1(2)
            for i in range(2, n_tiles):
                if i + 2 < n_tiles:
                    loadA(i + 2)
                if i + 1 < n_tiles:
                    stage1(i + 1)
                stage2(i)
            K.barrier()

        es_c.close()
        with ExitStack() as s2:
            wo_r = K.sb("wo_r", [128, 4, 1024], BF16, s2)
            wo_a = K.sb("wo_a", [128, 4, 1024], BF16, s2)
            xts = [K.sb("xtB%d" % i, [128, 1024], F32, s2) for i in range(3)]
            crs = [K.sb("crB%d" % i, [128, 128], F32, s2) for i in range(2)]
            cas = [K.sb("caB%d" % i, [128, 64], F32, s2) for i in range(2)]
            tmp = K.sb("tmpB", [128, 1024], F32, s2)
            hm = K.sb("hmB", [128, 1024], BF16, s2)
            hT = K.sb("hTB", [128, 8, 128], BF16, s2)
            kr = K.sb("krB", [128, 4, 128], F32, s2)
            t1 = K.sb("t1B", [128, 4, 64], F32, s2)
            t2 = K.sb("t2B", [128, 4, 64], F32, s2)
            qr = K.sb("qrB", [128, 4, 128], BF16, s2)
            k_s = K.sb("k_sB", [128, 4, 128], BF16, s2)
            kdf = K.sb("kdfB", [128, 4, 128], BF16, s2)
            vbf = K.sb("vbfB", [128, 4, 128], BF16, s2)
            gate = K.sb("gateB", [128, 512], F32, s2)
            qro = K.sb("qroB", [128, 8, 64], BF16, s2)
            qrp = K.sb("qrpB", [128, 4, 2, 64], BF16, s2)
            qT = K.sb("qTB", [128, 512], BF16, s2)
            qdTf = K.sb("qdTfB", [128, 512], BF16, s2)
            qdTb = K.sb("qdTbB", [128, 512], BF16, s2)
            kT = K.sb("kTB", [128, 512], BF16, s2)
            QTgs = [[K.sb("QTg%d_%dB" % (g, i), [128, 512], BF16, s2) for g in range(2)] for i in range(2)]
            STf = K.sb("STfB", [128, 512], BF16, s2)
            STb = K.sb("STbB", [128, 512], BF16, s2)
            o_sb = K.sb("o_sbB", [128, 4, 128], F32, s2)
            st4 = K.sb("st4B", [128, 16], F32, s2)
            mixr = K.sb("mixrB", [128, 512], BF16, s2)
            mixT_rs = [K.sb("mixT_rB%d" % i, [128, 4, 128], BF16, s2) for i in range(2)]
            attT = K.sb("attTB", [128, 4, 128], BF16, s2)
            PT = [K.sb("PT%dB" % i, [128, 1024], BF16, s2) for i in range(2)]
            pvraw = K.sb("pvrawB", [128, 512], F32, s2)
            rsw = K.sb("rswB", [128, 512], F32, s2)
            rs = K.sb("rsB", [128, 512], F32, s2)
            sq8 = kr.v(kr.h[:].rearrange("p h (a d) -> p (h a) d", a=2))
            osq = kr
            qn = o_sb.v(o_sb.h[:].rearrange("p h (a d) -> p (h a) d", a=2))
            wiv = w_in.rearrange("(k p) n -> p k n", p=128)
            K.dma("pool", wo_r[:], w_out[0:512, :].rearrange("(k p) n -> p k n", p=128), "w2")
            for g in range(2):
                K.dma("pool", wo_a[g * 64:(g + 1) * 64, :, :],
                      w_out[512 + g * 256:512 + (g + 1) * 256, :].rearrange("(pr d) n -> d pr n", d=64), "w%d" % (3 + g))
            for i in range(2):
                for g in range(2):
                    I("dve", "memset", ap=QTgs[i][g][:], constant=0.0)
            G = [G0, G1]
            Gb = [psT, psT2]

            def load_tile(t):
                b = t % 2
                K.dma("sp", xts[t % 3][:], x[t * 128:(t + 1) * 128, :], "x%d" % (t % 3))
                K.dma("sp", crs[b][:], rope_r[t * 128:(t + 1) * 128, :], "rpa%d" % b)
                K.dma("sp", cas[b][:], rope_a[t * 128:(t + 1) * 128, :], "rpb%d" % b)

            def proj(j, Gt):
                for k in range(8):
                    K.mm(Gt[:], hT[:, k, :], wi[:, k, j * 512:(j + 1) * 512], start=(k == 0), stop=(k == 7))

            def prologue(t):
                b = t % 2
                xt, cr, ca = xts[t % 3], crs[b], cas[b]
                QTg = QTgs[b]
                S = []

                def step(f):
                    S.append(f)
                    return f

                def gap():
                    S.append(None)

                @step
                def _():
                    I("dve", "memset", ap=ssq[:, 0:1], constant=0.0)
                    I("act", "activation", out=tmp[:], in_=xt[:], func=AF.Square, accum_out=ssq[:, 0:1])

                @step
                def _():
                    rstd_from(ssq[:, 0:1], 1024.0, rst[:, 0:1])
                gap()

                @step
                def _():
                    I("dve", "scalar_tensor_tensor", out=tmp[:], in0=xt[:], scalar=rst[:, 0:1], in1=A1_b, op0=ALU.mult, op1=ALU.mult)
                    I("dve", "tensor_tensor", out=hm[:], in0=tmp[:], in1=sh1_b, op=ALU.add)
                gap()

                @step
                def _():
                    for k in range(8):
                        K.tr(psT[:, k * 128:(k + 1) * 128], hm[:, k * 128:(k + 1) * 128], idb[:])

                @step
                def _():
                    I("dve", "tensor_copy", out=hT.v(hT.h[:].rearrange("p a b -> p (a b)")), in_=psT[:])

                @step
                def _():
                    proj(0, G1)

                @step
                def _():
                    proj(1, G0)
                    rq4 = G1.v(G1.h[:].rearrange("p (h d) -> p h d", h=4))
                    rope(rq4, 4, 64, cr[:, 0:64], cr[:, 64:128], t1[:], t2[:], qr[:, :, 0:64], qr[:, :, 64:128])

                @step
                def _():
                    rk4 = G0.v(G0.h[:].rearrange("p (h d) -> p h d", h=4))
                    rope(rk4, 4, 64, cr[:, 0:64], cr[:, 64:128], t1[:], t2[:], kr[:, :, 0:64], kr[:, :, 64:128])
                    I("dve", "tensor_scalar", out=k_s[:], in0=kr[:], scalar1=RET_SCALE, scalar2=None, op0=ALU.mult)
                    I("dve", "tensor_tensor", out=kdf[:], in0=kr[:], in1=KD.v(KD.h[:, 0, :].unsqueeze(2).to_broadcast([128, 4, 128])), op=ALU.mult)

                @step
                def _():
                    proj(2, G1)
                gap()

                @step
                def _():
                    proj(3, G0)
                    I("dve", "tensor_copy", out=vbf.v(vbf.h[:].rearrange("p h d -> p (h d)")), in_=G1[:])
                gap()

                @step
                def _():
                    I("act", "activation", out=gate[:], in_=G0[:], func=AF.Exp, scale=-1.0)

                @step
                def _():
                    I("act", "activation", out=gate[:], in_=gate[:], func=AF.Ln, scale=1.0, bias=1.0)
                    I("act", "activation", out=gate[:], in_=gate[:], func=AF.Exp, scale=-1.0)
                    proj(4, G1)
                gap()

                @step
                def _():
                    I("dve", "tensor_tensor", out=gate[:], in0=G0[:], in1=gate[:], op=ALU.mult)
                    aq8 = G1.v(G1.h[:].rearrange("p (h d) -> p h d", h=8))
                    I("act", "activation", out=sq8, in_=aq8, func=AF.Square)

                @step
                def _():
                    I("dve", "tensor_reduce", out=ssq[:, 0:8], in_=sq8, axis=AX.X, op=ALU.add)
                gap()

                @step
                def _():
                    rstd_from(ssq[:, 0:8], 64.0, rst[:, 0:8])
                gap()

                @step
                def _():
                    aq8 = G1.v(G1.h[:].rearrange("p (h d) -> p h d", h=8))
                    I("dve", "tensor_tensor", out=qn, in0=aq8, in1=rst.v(rst.h[:, 0:8].unsqueeze(2).to_broadcast([128, 8, 64])), op=ALU.mult)
                    I("dve", "tensor_tensor", out=qn, in0=qn, in1=qg_b.v(qg_b.h[:].unsqueeze(1).to_broadcast([128, 8, 64])), op=ALU.mult)

                @step
                def _():
                    t1v = t1.v(t1.h[:].rearrange("p h (a d) -> p (h a) d", a=2))
                    t2v = t2.v(t2.h[:].rearrange("p h (a d) -> p (h a) d", a=2))
                    rope(qn, 8, 32, ca[:, 0:32], ca[:, 32:64], t1v, t2v, qro[:, :, 0:32], qro[:, :, 32:64])
                    for e in range(2):
                        I("dve", "tensor_copy", out=qrp[:, :, e, :], in_=qro[:, e * 4:(e + 1) * 4, :])

                @step
                def _():
                    for h in range(4):
                        K.tr(psT[:, h * 128:(h + 1) * 128], qr[:, h, :], idb[:])
                    for h in range(4):
                        K.tr(psT[:, 512 + h * 128:512 + (h + 1) * 128], k_s[:, h, :], idb[:])
                gap()

                @step
                def _():
                    for pr in range(4):
                        K.tr(psT2[:, pr * 128:(pr + 1) * 128], qrp.v(qrp.h[:, pr, :, :].rearrange("p e d -> p (e d)")), idb[:])
                    I("dve", "tensor_copy", out=qT[:], in_=psT[:, 0:512])
                    I("dve", "tensor_tensor", out=qdTf[:], in0=psT[:, 0:512], in1=QD[:, 0, :], op=ALU.mult)

                @step
                def _():
                    I("dve", "tensor_tensor", out=qdTb[:], in0=psT[:, 0:512], in1=QD[:, 1, :], op=ALU.mult)
                    I("dve", "tensor_copy", out=kT[:], in_=psT[:, 512:1024])

                @step
                def _():
                    I("dve", "tensor_copy", out=QTg[0][0:64, :], in_=psT2[0:64, 0:512])
                    I("dve", "tensor_copy", out=QTg[1][64:128, :], in_=psT2[64:128, 0:512])
                gap()

                @step
                def _():
                    for h in range(4):
                        hs = slice(h * 128, (h + 1) * 128)
                        K.mm(G0[:, hs], kT[:, hs], qT[:, hs])

                @step
                def _():
                    for h in range(4):
                        hs = slice(h * 128, (h + 1) * 128)
                        K.mm(G1[:, hs], kdf[:, h, :], vbf[:, h, :])
                    I("dve", "tensor_tensor", out=STf[:], in0=G0[:], in1=intraT[:, 0, :], op=ALU.mult)
                    I("dve", "tensor_tensor", out=STb[:], in0=G0[:], in1=intraT[:, 1, :], op=ALU.mult)
                gap()

                @step
                def _():
                    for h in range(4):
                        hs = slice(h * 128, (h + 1) * 128)
                        K.mm(G0[:, hs], STf[:, hs], vbf[:, h, :], start=True, stop=False)
                        K.mm(G0[:, hs], STb[:, hs], vbf[:, h, :], start=False, stop=False)
                        K.mm(G0[:, hs], qdTf[:, hs], Sf_bf[:, hs], start=False, stop=False)
                        K.mm(G0[:, hs], qdTb[:, hs], SbAll[:, t, hs], start=False, stop=True)

                @step
                def _():
                    for h in range(4):
                        hs = slice(h * 128, (h + 1) * 128)
                        I("dve", "scalar_tensor_tensor", out=Sf[:, hs], in0=Sf[:, hs], scalar=cdec[:, h:h + 1], in1=G1[:, hs], op0=ALU.mult, op1=ALU.add)
                    I("dve", "tensor_copy", out=Sf_bf[:], in_=Sf[:])

                @step
                def _():
                    I("dve", "tensor_copy", out=o_sb.v(o_sb.h[:].rearrange("p h d -> p (h d)")), in_=G0[:])
                    I("dve", "tensor_reduce", out=st4[:, 0:4], in_=o_sb[:], axis=AX.X, op=ALU.add)
                    I("dve", "tensor_scalar", out=st4[:, 0:4], in0=st4[:, 0:4], scalar1=1.0 / 128.0, scalar2=None, op0=ALU.mult)
                    I("dve", "tensor_tensor", out=o_sb[:], in0=o_sb[:], in1=st4.v(st4.h[:, 0:4].unsqueeze(2).to_broadcast([128, 4, 128])), op=ALU.subtract)

                @step
                def _():
                    I("dve", "tensor_tensor", out=osq[:], in0=o_sb[:], in1=o_sb[:], op=ALU.mult)
                    I("dve", "tensor_reduce", out=st4[:, 4:8], in_=osq[:], axis=AX.X, op=ALU.add)
                gap()

                @step
                def _():
                    I("act", "activation", out=st4[:, 12:16], in_=st4[:, 4:8], func=AF.Ln, scale=1.0 / 128.0, bias=EPS)
                    I("act", "activation", out=st4[:, 12:16], in_=st4[:, 12:16], func=AF.Exp, scale=-0.5)
                gap()

                @step
                def _():
                    I("dve", "tensor_tensor", out=o_sb[:], in0=o_sb[:], in1=st4.v(st4.h[:, 12:16].unsqueeze(2).to_broadcast([128, 4, 128])), op=ALU.mult)
                    o2 = o_sb.v(o_sb.h[:].rearrange("p h d -> p (h d)"))
                    I("dve", "tensor_tensor", out=o2, in0=o2, in1=gn_b[:], op=ALU.mult)
                    I("dve", "tensor_tensor", out=mixr[:], in0=o2, in1=gate[:], op=ALU.mult)
                gap()

                @step
                def _():
                    for k in range(4):
                        K.tr(psT2[:, k * 128:(k + 1) * 128], mixr[:, k * 128:(k + 1) * 128], idb[:])

                @step
                def _():
                    I("dve", "tensor_copy", out=mixT_rs[b].v(mixT_rs[b].h[:].rearrange("p a b -> p (a b)")), in_=psT2[:, 0:512])
                return S

            def attention(t):
                QTg = QTgs[t % 2]
                Qs = [QA, QB]

                def qk(kt):
                    Q = Qs[kt % 2]
                    K.mm(Q[:, 0:512], attKT[:, kt * 128:(kt + 1) * 128], QTg[0][:])
                    K.mm(Q[:, 512:1024], attKT[:, kt * 128:(kt + 1) * 128], QTg[1][:])

                qk(0)
                for kt in range(NKT):
                    if kt + 1 < NKT:
                        qk(kt + 1)
                    pt = PT[kt % 2]
                    I("act", "activation", out=pt[:], in_=Qs[kt % 2][:], func=AF.Exp, scale=ATT_SCALE)
                    K.mm(acc0[:], attV[:, kt, 0:128], pt[:, 0:512], start=(kt == 0), stop=(kt == NKT - 1))
                    K.mm(acc1[:], attV[:, kt, 64:192], pt[:, 512:1024], start=(kt == 0), stop=(kt == NKT - 1))
                    yield
                I("act", "activation", out=rsw[64:128, :], in_=acc0[64:128, :], func=AF.Ln)
                I("act", "activation", out=rsw[0:64, :], in_=acc1[0:64, :], func=AF.Ln)
                I("act", "activation", out=rsw[:], in_=rsw[:], func=AF.Exp, scale=-1.0)
                I("dve", "tensor_copy", out=pvraw[0:64, :], in_=acc0[0:64, :])
                I("dve", "tensor_copy", out=pvraw[64:128, :], in_=acc1[64:128, :])
                K.dma("sp", rs[0:64, :], rsw[64:128, :], "swp")
                K.dma("sp", rs[64:128, :], rsw[0:64, :], "swp")
                yield

            def epilogue(t):
                b = t % 2
                xt = xts[t % 3]
                S = []

                def step(f):
                    S.append(f)
                    return f

                def gap():
                    S.append(None)

                @step
                def _():
                    I("dve", "tensor_tensor", out=attT.v(attT.h[:].rearrange("p a b -> p (a b)")), in0=pvraw[:], in1=rs[:], op=ALU.mult)

                @step
                def _():
                    for half in range(2):
                        cs = slice(half * 512, (half + 1) * 512)
                        Gy = G[half]
                        for k in range(4):
                            K.mm(Gy[:], mixT_rs[b][:, k, :], wo_r[:, k, cs], start=(k == 0), stop=False)
                        for pr in range(4):
                            K.mm(Gy[:], attT[:, pr, :], wo_a[:, pr, cs], start=False, stop=(pr == 3))

                @step
                def _():
                    for half in range(2):
                        cs = slice(half * 512, (half + 1) * 512)
                        I("dve", "tensor_tensor", out=tmp[:, cs], in0=G[half][:], in1=V(g1_b.ap[:, cs], g1_b.deps), op=ALU.mult)
                        I("dve", "tensor_tensor", out=xt[:, cs], in0=tmp[:, cs], in1=xt[:, cs], op=ALU.add)
                gap()

                @step
                def _():
                    K.dma("sp", x1s[t * 128:(t + 1) * 128, :], xt[:], "st1")
                    I("dve", "memset", ap=ssq[:, 0:1], constant=0.0)
                    I("act", "activation", out=tmp[:], in_=xt[:], func=AF.Square, accum_out=ssq[:, 0:1])

                @step
                def _():
                    rstd_from(ssq[:, 0:1], 1024.0, rst[:, 0:1])
                gap()

                @step
                def _():
                    I("dve", "scalar_tensor_tensor", out=tmp[:], in0=xt[:], scalar=rst[:, 0:1], in1=A2_b, op0=ALU.mult, op1=ALU.mult)
                    I("dve", "tensor_tensor", out=hm[:], in0=tmp[:], in1=sh2_b, op=ALU.add)
                gap()

                @step
                def _():
                    for k in range(8):
                        K.tr(psT[:, k * 128:(k + 1) * 128], hm[:, k * 128:(k + 1) * 128], idb[:])

                @step
                def _():
                    I("dve", "tensor_copy", out=hT.v(hT.h[:].rearrange("p a b -> p (a b)")), in_=psT[:])
                    K.dma("sp", h2s[:, :, t * 128:(t + 1) * 128], hT[:], "st2")
                return S

            def run(steps):
                for f in steps:
                    if f is not None:
                        f()

            load_tile(0)
            if NB > 1:
                load_tile(1)
            if NB > 0:
                run(prologue(0))
            for t in range(NB):
                side = (epilogue(t - 1) if t > 0 else []) + (prologue(t + 1) if t + 1 < NB else [])
                n = len(side)
                done = 0
                it = 0
                for _ in attention(t):
                    it += 1
                    upto = min(n, (it * n + NKT - 1) // NKT)
                    run(side[done:upto])
                    done = upto
                run(side[done:])
                if t + 2 < NB:
                    load_tile(t + 2)
            if NB > 0:
                run(epilogue(NB - 1))
            K.barrier()

        es_ab.close()
        with ExitStack() as s3:
            W1c = [K.sb("W1c%d" % j, [128, 8, 1024], BF16, s3) for j in range(4)]
            W2c = [K.sb("W2c%d" % j, [128, 8, 1024], BF16, s3) for j in range(4)]
            hb = K.sb("hbC", [128, 8, 512], BF16, s3)
            act = K.sb("actC", [128, 32, 512], BF16, s3)
            rl = [K.sb("rlC%d" % i, [128, 512], F32, s3) for i in range(2)]
            x1 = K.sb("x1C", [128, 1024], F32, s3)
            x2 = K.sb("x2C", [128, 1024], F32, s3)
            tmp = K.sb("tmpC", [128, 1024], F32, s3)
            ot = K.sb("otC", [128, 1024], F32, s3)
            gf_b = K.sb("gf_b", [128, 1024], F32, s3)
            K.dma("sp", gf_b[:], fng.partition_broadcast(128), "cst8")
            w1v = w1.rearrange("(k p) n -> p k n", p=128)
            w2v = w2.rearrange("(k p) n -> p k n", p=128)
            for j in range(4):
                K.dma("pool", W1c[j][:], w1v[:, :, j * 1024:(j + 1) * 1024], "w%d" % j)
            for j in range(4):
                K.dma("pool", W2c[j][:], w2v[:, j * 8:(j + 1) * 8, :], "v%d" % j)
            for blk in range(NC):
                K.dma("sp", hb[:], h2s[:, :, blk * 512:(blk + 1) * 512], "hl")
                for fc in range(32):
                    Fo = F[fc % 2]
                    for k in range(8):
                        K.mm(Fo[:], W1c[fc // 8][:, k, (fc % 8) * 128:(fc % 8 + 1) * 128], hb[:, k, :], start=(k == 0), stop=(k == 7))
                    r = rl[fc % 2]
                    I("act", "activation", out=r[:], in_=Fo[:], func=AF.Relu)
                    I("pool", "tensor_tensor", out=act[:, fc, :], in0=r[:], in1=r[:], op=ALU.mult)
                for tt in range(4):
                    t = blk * 4 + tt
                    K.dma("sp", x1[:], x1s[t * 128:(t + 1) * 128, :], "xl")
                    for half in range(2):
                        cs = slice(half * 512, (half + 1) * 512)
                        Fy = F[2 + half]
                        for fc in range(32):
                            K.mm(Fy[:], act[:, fc, tt * 128:(tt + 1) * 128], W2c[fc // 8][:, fc % 8, cs], start=(fc == 0), stop=(fc == 31))
                        I("dve", "tensor_tensor", out=tmp[:, cs], in0=Fy[:], in1=V(g2_b.ap[:, cs], g2_b.deps), op=ALU.mult)
                        I("dve", "tensor_tensor", out=x2[:, cs], in0=tmp[:, cs], in1=x1[:, cs], op=ALU.add)
                    row_rstd(x2[:], tmp[:], 1024.0)
                    I("dve", "scalar_tensor_tensor", out=ot[:], in0=x2[:], scalar=rst[:, 0:1], in1=gf_b[:], op0=ALU.mult, op1=ALU.mult)
                    K.dma("sp", out[t * 128:(t + 1) * 128, :], ot[:], "out")

        nw, cnt = K.emit()
        K.finish()
    return nc


def _tables():
    Lq = 4096
    t = np.arange(Lq, dtype=np.float32)
    fr = (np.float32(10000.0) ** (-(np.arange(64, dtype=np.float32) / np.float32(64)))).astype(np.float32)
    ang = (t[:, None] * fr[None, :]).astype(np.float32).astype(np.float64)
    rope_r = np.concatenate([np.cos(ang), np.sin(ang)], axis=1).astype(np.float32)
    af = (np.float32(10000.0) ** (-(np.arange(16, dtype=np.float32) / np.float32(16)))).astype(np.float32)
    row = np.repeat(np.arange(64, dtype=np.float32), 64)
    col = np.tile(np.arange(64, dtype=np.float32), 64)
    aang = np.concatenate([(row[:, None] * af[None, :]).astype(np.float32), (col[:, None] * af[None, :]).astype(np.float32)], axis=1).astype(np.float64)
    rope_a = np.concatenate([np.cos(aang), np.sin(aang)], axis=1).astype(np.float32)
    j = np.arange(128, dtype=np.float32)[:, None]
    i = np.arange(128, dtype=np.float32)[None, :]
    cm = np.zeros((128, 6, 128), np.float32)
    cm[:, 0, :] = np.maximum(i - j, 0.0)
    cm[:, 1, :] = (i >= j).astype(np.float32)
    cm[:, 2, :] = np.maximum(j - i, 0.0)
    cm[:, 3, :] = (j >= i).astype(np.float32)
    cm[:, 4, :] = i + 1.0 + 0.0 * j
    cm[:, 5, :] = 128.0 - i + 0.0 * j
    p = np.arange(128, dtype=np.float32)
    pcols = np.stack([p, 127.0 - p, 255.0 - p, 128.0 + p], axis=1).astype(np.float32)
    return rope_r, rope_a, cm, pcols, np.eye(128, dtype=np.float32)


def kernel(x, c, ctx, c_ctx, w_mod, b_mod, norm1_g, norm2_g, w_in, w_out, ret_log_rate, ret_gn_g,
           q_norm_g, k_norm_g, w_ff1, w_ff2, final_norm_g):
    f = lambda a: np.ascontiguousarray(np.asarray(a, dtype=np.float32))
    x = f(x); c = f(c); ctx = f(ctx); c_ctx = f(c_ctx)
    rope_r, rope_a, cm, pcols, ident = _tables()
    shared = {
        "w_mod": f(w_mod)[0], "b_mod": f(b_mod)[0], "norm1_g": f(norm1_g)[0], "norm2_g": f(norm2_g)[0],
        "w_in": f(w_in)[0], "w_out": f(w_out)[0], "rate": f(ret_log_rate)[0].reshape(8),
        "ret_gn_g": f(ret_gn_g)[0], "q_norm_g": f(q_norm_g)[0], "k_norm_g": f(k_norm_g)[0],
        "w_ff1": f(w_ff1)[0], "w_ff2": f(w_ff2)[0], "final_norm_g": f(final_norm_g),
        "rope_r": rope_r, "rope_a": rope_a, "cmat": cm, "pcols": pcols, "ident": ident,
    }
    n = x.shape[0]
    in_maps = []
    for b in range(n):
        m = dict(shared)
        m["x"] = x[b]
        m["ctx"] = ctx[b]
        m["cvec"] = np.ascontiguousarray(np.stack([c[b], c_ctx], axis=0))
        in_maps.append(m)
    nc = build_nc()
    res = run_bass_kernel_spmd(nc, in_maps, core_ids=list(range(n)))
    return np.stack([np.asarray(r["out"], dtype=np.float32) for r in res.results], axis=0)
```

```python
import numpy as np
import concourse.bass as bass
import concourse.mybir as mybir
from contextlib import ExitStack
from concourse.bass_utils import run_bass_kernel_spmd

F32 = mybir.dt.float32
BF16 = mybir.dt.bfloat16
AF = mybir.ActivationFunctionType
ALU = mybir.AluOpType
AX = mybir.AxisListType


class Dep:
    __slots__ = ("w", "readers", "name", "excl")

    def __init__(self, name="", excl=False):
        self.w = None
        self.readers = {}
        self.name = name
        self.excl = excl


class V:
    __slots__ = ("ap", "deps")

    def __init__(self, ap, deps):
        self.ap = ap
        self.deps = deps


class Tile:
    def __init__(self, handle, deps):
        self.h = handle
        self.deps = deps

    def __getitem__(self, idx):
        return V(self.h[idx], self.deps)

    def v(self, ap):
        return V(ap, self.deps)


class SubTile:
    def __init__(self, ap, deps):
        self.h = ap
        self.deps = deps

    def __getitem__(self, idx):
        return V(self.h[idx], self.deps)

    def v(self, ap):
        return V(ap, self.deps)


class Op:
    __slots__ = ("eng", "fn", "waits", "marked", "tick", "dma_sem", "dma_val", "key")


class Kern:
    ENGS = ("pe", "act", "dve", "pool", "sp")

    def __init__(self, nc, es):
        self.nc = nc
        self.es = es
        self.ops = []
        self.eng_obj = {"pe": nc.tensor, "act": nc.scalar, "dve": nc.vector,
                        "pool": nc.gpsimd, "sp": nc.sync}
        self.eng_sem = {e: es.enter_context(nc.semaphore("sem_" + e)) for e in self.ENGS}
        self.dma_sems = {}
        self.dma_cnt = {}
        self.last_op = {e: None for e in self.ENGS}
        self.same_engine_sync = True

    def sb(self, name, shape, dt, es=None):
        h = (es or self.es).enter_context(self.nc.sbuf_tensor(name, list(shape), dt))
        return Tile(h, [Dep(name)])

    def ps(self, name, shape, dt):
        h = self.es.enter_context(self.nc.psum_tensor(name, list(shape), dt))
        return Tile(h, [Dep(name, excl=True)])

    def dsem(self, name):
        s = self.es.enter_context(self.nc.semaphore("dq_" + name))
        self.dma_sems[name] = s
        self.dma_cnt[name] = 0
        return name

    def _add_wait(self, op, prod):
        if prod is None or prod is op:
            return
        if prod.dma_sem is None:
            if prod.eng == op.eng and op.dma_sem is None:
                if op.eng in ("pe", "sp"):
                    return
                if not self.same_engine_sync:
                    return
            prod.marked = True
        op.waits.append(prod)

    def _record(self, eng, fn, reads, writes, dma_sem=None):
        op = Op()
        op.eng = eng
        op.fn = fn
        op.waits = []
        op.marked = False
        op.tick = None
        op.dma_sem = dma_sem
        op.dma_val = None
        if dma_sem is not None:
            self.dma_cnt[dma_sem] += 16
            op.dma_val = self.dma_cnt[dma_sem]
            op.key = ("dma", dma_sem)
        else:
            op.key = eng
        rd = []
        wd = []
        for v in reads:
            for d in v.deps:
                if d.excl:
                    if d not in wd:
                        wd.append(d)
                elif d not in rd:
                    rd.append(d)
        for v in writes:
            for d in v.deps:
                if d not in wd:
                    wd.append(d)
        for d in rd:
            self._add_wait(op, d.w)
        for d in wd:
            self._add_wait(op, d.w)
            for r in d.readers.values():
                self._add_wait(op, r)
        for d in rd:
            if d not in wd:
                d.readers[op.key] = op
        for d in wd:
            d.w = op
            d.readers = {}
        self.ops.append(op)
        if dma_sem is None:
            self.last_op[eng] = op
        return op

    def I(self, eng, method, *, r=(), w=(), **kw):
        reads = list(r)
        writes = list(w)
        args = {}
        for k, v in kw.items():
            if isinstance(v, V):
                args[k] = v.ap
                if k in ("out", "accum_out", "ap"):
                    writes.append(v)
                else:
                    reads.append(v)
            else:
                args[k] = v

        def fn(e, method=method, args=args):
            return getattr(e, method)(**args)

        return self._record(eng, fn, reads, writes)

    def mm(self, out, lhsT, rhs, start=True, stop=True):
        def fn(e, o=out.ap, l=lhsT.ap, r_=rhs.ap, s=start, t=stop):
            return e.matmul(o, lhsT=l, rhs=r_, start=s, stop=t)
        return self._record("pe", fn, [lhsT, rhs], [out])

    def tr(self, out, in_, ident):
        def fn(e, o=out.ap, i=in_.ap, d=ident.ap):
            return e.transpose(o, i, d)
        return self._record("pe", fn, [in_, ident], [out])

    def dma(self, queue, out, in_, sem, **dkw):
        reads = [in_] if isinstance(in_, V) else []
        writes = [out] if isinstance(out, V) else []
        o = out.ap if isinstance(out, V) else out
        i = in_.ap if isinstance(in_, V) else in_

        def fn(e, o=o, i=i, dkw=dkw):
            return e.dma_start(out=o, in_=i, **dkw)
        return self._record(queue, fn, reads, writes, dma_sem=sem)

    def barrier(self):
        lasts = []
        for e in self.ENGS:
            if self.last_op[e] is not None and self.last_op[e].dma_sem is None:
                lasts.append(self.last_op[e])
        dmas = {}
        for op in self.ops:
            if op.dma_sem is not None:
                dmas[op.dma_sem] = op
        for e in ("pe", "act", "dve", "pool", "sp"):
            op = Op()
            op.eng = e
            op.fn = None
            op.waits = []
            op.marked = False
            op.tick = None
            op.dma_sem = None
            op.dma_val = None
            op.key = e
            for p in lasts:
                if p.eng != e or e not in ("pe", "sp"):
                    p.marked = True
                    op.waits.append(p)
            for p in dmas.values():
                op.waits.append(p)
            self.ops.append(op)

    def emit(self):
        cnt = {e: 0 for e in self.ENGS}
        seen = {e: {} for e in self.ENGS}
        n_wait = 0
        for op in self.ops:
            e = self.eng_obj[op.eng]
            for p in op.waits:
                if p.dma_sem is not None:
                    ck = ("dma", p.dma_sem)
                    val = p.dma_val
                    sem = self.dma_sems[p.dma_sem]
                else:
                    ck = p.eng
                    val = p.tick
                    sem = self.eng_sem[p.eng]
                    assert val is not None, "waiting on unticked op"
                if seen[op.eng].get(ck, 0) >= val:
                    continue
                seen[op.eng][ck] = val
                e.wait_ge(sem, val)
                n_wait += 1
            if op.fn is None:
                continue
            ins = op.fn(e)
            if op.dma_sem is not None:
                ins.then_inc(self.dma_sems[op.dma_sem], 16)
            elif op.marked:
                cnt[op.eng] += 1
                op.tick = cnt[op.eng]
                ins.then_inc(self.eng_sem[op.eng], 1)
        return n_wait, cnt

    def finish(self, queue="sp"):
        e = self.eng_obj[queue]
        for name, c in self.dma_cnt.items():
            if c > 0:
                e.wait_ge(self.dma_sems[name], c)

L = 4096
NT = 32
D = 1024
EPS = 1e-6
RET_SCALE = 128 ** -0.5
ATT_SCALE = 64 ** -0.5
NKT = 34
NA = 32
NB = 32
NC = 8


def build_nc():
    nc = bass.Bass("TRN2", target_bir_lowering=False)

    def din(name, shape):
        return nc.dram_tensor(name, list(shape), F32, kind="ExternalInput").ap()

    x = din("x", [L, D])
    ctx = din("ctx", [256, D])
    cvec = din("cvec", [2, D])
    w_mod = din("w_mod", [D, 6144])
    b_mod = din("b_mod", [6144])
    n1g = din("norm1_g", [D])
    n2g = din("norm2_g", [D])
    w_in = din("w_in", [D, 2816])
    w_out = din("w_out", [D, D])
    rate = din("rate", [8])
    gng = din("ret_gn_g", [512])
    qng = din("q_norm_g", [64])
    kng = din("k_norm_g", [64])
    w1 = din("w_ff1", [D, 4096])
    w2 = din("w_ff2", [4096, D])
    fng = din("final_norm_g", [D])
    rope_r = din("rope_r", [L, 128])
    rope_a = din("rope_a", [L, 64])
    cmat = din("cmat", [128, 6, 128])
    pcols = din("pcols", [128, 4])
    ident = din("ident", [128, 128])
    out = nc.dram_tensor("out", [L, D], F32, kind="ExternalOutput").ap()
    x1s = nc.dram_tensor("x1s", [L, D], F32).ap()
    h2s = nc.dram_tensor("h2s", [128, 8, L], BF16).ap()

    with ExitStack() as es:
        K = Kern(nc, es)
        I = K.I
        idb = K.sb("idb", [128, 128], BF16)
        ones = K.sb("ones", [128, 128], BF16)
        g2_t = K.sb("g2_t", [128, 1024], F32)
        ssq = K.sb("ssq", [128, 8], F32)
        rst = K.sb("rst", [128, 8], F32)
        es_ab = ExitStack()
        es.enter_context(es_ab)
        K.es_save = K.es
        K.es = es_ab
        modb = K.sb("modb", [128, 5120], F32)
        gn_b = K.sb("gn_b", [128, 512], F32)
        qg_b = K.sb("qg_b", [128, 64], F32)
        kg_b = K.sb("kg_b", [128, 64], F32)
        intraT = K.sb("intraT", [128, 2, 512], F32)
        QD = K.sb("QD", [128, 2, 512], F32)
        KD = K.sb("KD", [128, 2, 4], F32)
        cdec = K.sb("cdec", [128, 8], F32)
        wc = K.sb("wc", [128, 2, 2, 4], F32)
        SbAll = K.sb("SbAll", [128, NT, 512], BF16)
        attKT = K.sb("attKT", [128, NKT * 128], BF16)
        attV = K.sb("attV", [128, NKT, 192], BF16)
        Sf = K.sb("Sf", [128, 512], F32)
        Sf_bf = K.sb("Sf_bf", [128, 512], BF16)
        wi = K.sb("wi", [128, 8, 2560], BF16)
        K.es = K.es_save
        es_c = ExitStack()
        es_ab.enter_context(es_c)
        cmodb = K.sb("cmodb", [128, 2048], F32, es_c)
        Sb = K.sb("Sb", [128, 512], F32, es_c)
        G0 = K.ps("G0", [128, 512], F32)
        G1 = K.ps("G1", [128, 512], F32)
        acc0 = K.ps("acc0", [128, 512], F32)
        acc1 = K.ps("acc1", [128, 512], F32)
        QA = K.ps("QA", [128, 1024], F32)
        QB = K.ps("QB", [128, 1024], F32)
        QA.deps = [Dep("QA0", True), Dep("QA1", True)]
        QB.deps = [Dep("QB0", True), Dep("QB1", True)]
        psT = SubTile(G0.h[:].bitcast(BF16), G0.deps)
        psT2 = SubTile(G1.h[:].bitcast(BF16), G1.deps)
        F = [acc0, acc1,
             SubTile(QA.h[:, 0:512], [QA.deps[0]]), SubTile(QA.h[:, 512:1024], [QA.deps[1]]),
             SubTile(QB.h[:, 0:512], [QB.deps[0]]), SubTile(QB.h[:, 512:1024], [QB.deps[1]])]
        for nm in ["ra0", "ra1", "ra2", "rb0", "rb1", "rb2", "wi0", "wi1", "x0", "x1", "x2", "rpa0", "rpa1", "rpb0", "rpb1", "swp", "x", "rp", "rp2", "w0", "w1", "w2", "w3", "w4", "v0", "v1", "v2", "v3", "st1", "st2", "out", "hl", "xl"] + ["cst%d" % i for i in range(2, 15)]:
            K.dsem(nm)

        def rstd_from(acc_v, n, out_v):
            I("act", "activation", out=out_v, in_=acc_v, func=AF.Ln, scale=1.0 / n, bias=EPS)
            I("act", "activation", out=out_v, in_=out_v, func=AF.Exp, scale=-0.5)

        def row_rstd(src_v, junk_v, n):
            I("dve", "memset", ap=ssq[:, 0:1], constant=0.0)
            I("act", "activation", out=junk_v, in_=src_v, func=AF.Square, accum_out=ssq[:, 0:1])
            rstd_from(ssq[:, 0:1], n, rst[:, 0:1])

        with ExitStack() as s0:
            idf = K.sb("idf", [128, 128], F32, s0)
            cc = K.sb("cc", [16, 128], F32, s0)
            ccb = K.sb("ccb", [128, 128], BF16, s0)
            cact = K.sb("cact", [128, 16], BF16, s0)
            cb = K.sb("cb", [128, 16, 128], BF16, s0)
            wms = [K.sb("wms%d" % i, [128, 8, 512], BF16, s0) for i in range(2)]
            bmod_b = K.sb("bmod_b", [128, 6144], F32, s0)
            n1g_b = K.sb("n1g_b", [128, 1024], F32, s0)
            n2g_b = K.sb("n2g_b", [128, 1024], F32, s0)
            rate_b = K.sb("rate_b", [128, 8], F32, s0)
            lg = K.sb("lg", [128, 8], F32, s0)
            cm = K.sb("cm", [128, 6, 128], F32, s0)
            pc = K.sb("pc", [128, 4], F32, s0)
            tmpe = K.sb("tmpe", [128, 128], F32, s0)
            K.dma("sp", idf[:], ident[:, :], "cst2")
            K.dma("sp", cc[:], cvec.rearrange("r (k p) -> (r k) p", p=128), "cst3")
            K.dma("sp", bmod_b[:], b_mod.partition_broadcast(128), "cst5")
            K.dma("sp", n1g_b[:], n1g.partition_broadcast(128), "cst6")
            K.dma("sp", n2g_b[:], n2g.partition_broadcast(128), "cst7")
            K.dma("sp", gn_b[:], gng.partition_broadcast(128), "cst9")
            K.dma("sp", qg_b[:], qng.partition_broadcast(128), "cst10")
            K.dma("sp", kg_b[:], kng.partition_broadcast(128), "cst11")
            K.dma("sp", rate_b[:], rate.partition_broadcast(128), "cst12")
            K.dma("sp", cm[:], cmat[:, :, :], "cst13")
            K.dma("sp", pc[:], pcols[:, :], "cst14")
            I("dve", "tensor_copy", out=idb[:], in_=idf[:])
            I("dve", "memset", ap=ones[:], constant=1.0)
            I("dve", "memset", ap=ccb[:], constant=0.0)
            I("act", "activation", out=ccb[0:16, :], in_=cc[:], func=AF.Silu)
            K.tr(psT2[:, 0:128], ccb[:], idb[:])
            I("dve", "tensor_copy", out=cact[:], in_=psT2[:, 0:16])
            for j in range(16):
                I("dve", "tensor_copy", out=cb[:, j, :], in_=cact.v(cact.h[:, j:j + 1].to_broadcast([128, 128])))
            wmv = w_mod.rearrange("(k p) n -> p k n", p=128)
            for jb in range(12):
                ws = wms[jb % 2]
                K.dma("pool", ws[:], wmv[:, :, jb * 512:(jb + 1) * 512], "w%d" % (jb % 2))
                for k in range(8):
                    K.mm(F[0][:], cb[:, k, :], ws[:, k, :], start=(k == 0), stop=(k == 7))
                dst = modb[:, jb * 512:(jb + 1) * 512] if jb < 10 else g2_t[:, (jb - 10) * 512:(jb - 9) * 512]
                I("dve", "tensor_tensor", out=dst, in0=F[0][:], in1=bmod_b[:, jb * 512:(jb + 1) * 512], op=ALU.add)
                if jb < 4:
                    for k in range(8):
                        K.mm(F[1][:], cb[:, 8 + k, :], ws[:, k, :], start=(k == 0), stop=(k == 7))
                    I("dve", "tensor_tensor", out=cmodb[:, jb * 512:(jb + 1) * 512], in0=F[1][:], in1=bmod_b[:, jb * 512:(jb + 1) * 512], op=ALU.add)
            I("dve", "scalar_tensor_tensor", out=modb[:, 1024:2048], in0=modb[:, 1024:2048], scalar=1.0, in1=n1g_b[:], op0=ALU.add, op1=ALU.mult)
            I("dve", "scalar_tensor_tensor", out=modb[:, 4096:5120], in0=modb[:, 4096:5120], scalar=1.0, in1=n2g_b[:], op0=ALU.add, op1=ALU.mult)
            I("dve", "scalar_tensor_tensor", out=cmodb[:, 1024:2048], in0=cmodb[:, 1024:2048], scalar=1.0, in1=n1g_b[:], op0=ALU.add, op1=ALU.mult)
            I("act", "activation", out=lg[:], in_=rate_b[:], func=AF.Exp)
            I("act", "activation", out=lg[:], in_=lg[:], func=AF.Ln, scale=-1.0, bias=1.0)
            for d in range(2):
                for h in range(4):
                    c = d * 4 + h
                    sc = lg[:, c:c + 1]
                    I("act", "activation", out=tmpe[:], in_=cm[:, 2 * d, :], func=AF.Exp, scale=sc)
                    I("dve", "tensor_tensor", out=intraT[:, d, h * 128:(h + 1) * 128], in0=tmpe[:], in1=cm[:, 2 * d + 1, :], op=ALU.mult)
                    I("act", "activation", out=QD[:, d, h * 128:(h + 1) * 128], in_=cm[:, 4 + d, :], func=AF.Exp, scale=sc)
                    I("act", "activation", out=KD[:, d, h:h + 1], in_=pc[:, (1 if d == 0 else 0):(2 if d == 0 else 1)], func=AF.Exp, scale=sc)
                    for t in range(2):
                        col = (2 if t == 0 else 1) if d == 0 else (0 if t == 0 else 3)
                        I("act", "activation", out=wc[:, t, d, h:h + 1], in_=pc[:, col:col + 1], func=AF.Exp, scale=sc)
            I("act", "activation", out=cdec[:], in_=lg[:], func=AF.Exp, scale=128.0)
            I("dve", "tensor_scalar", out=KD[:], in0=KD[:], scalar1=RET_SCALE, scalar2=None, op0=ALU.mult)
            I("dve", "tensor_scalar", out=wc[:], in0=wc[:], scalar1=RET_SCALE, scalar2=None, op0=ALU.mult)
            K.barrier()

        sh1_b = modb[:, 0:1024]; A1_b = modb[:, 1024:2048]; g1_b = modb[:, 2048:3072]
        sh2_b = modb[:, 3072:4096]; A2_b = modb[:, 4096:5120]; g2_b = g2_t[:, :]
        csh1_b = cmodb[:, 0:1024]; cA1_b = cmodb[:, 1024:2048]

        def norm_mod_T(xt, A_v, sh_v, tmp, hm, hT):
            row_rstd(xt[:], tmp[:], 1024.0)
            I("dve", "scalar_tensor_tensor", out=tmp[:], in0=xt[:], scalar=rst[:, 0:1], in1=A_v, op0=ALU.mult, op1=ALU.mult)
            I("dve", "tensor_tensor", out=hm[:], in0=tmp[:], in1=sh_v, op=ALU.add)
            for k in range(8):
                K.tr(psT[:, k * 128:(k + 1) * 128], hm[:, k * 128:(k + 1) * 128], idb[:])
            I("act", "activation", out=hT.v(hT.h[:].rearrange("p a b -> p (a b)")), in_=psT[:], func=AF.Copy)

        def v3(t, ap):
            return t.v(ap)

        def rope(src, H, half, cos_v, sin_v, t1, t2, dst_lo, dst_hi):
            s_lo = V(src.ap[:, :, 0:half], src.deps)
            s_hi = V(src.ap[:, :, half:2 * half], src.deps)
            cb_ = V(cos_v.ap.unsqueeze(1).to_broadcast([128, H, half]), cos_v.deps)
            sb_ = V(sin_v.ap.unsqueeze(1).to_broadcast([128, H, half]), sin_v.deps)
            I("dve", "tensor_tensor", out=t1, in0=s_lo, in1=cb_, op=ALU.mult)
            I("dve", "tensor_tensor", out=t2, in0=s_hi, in1=sb_, op=ALU.mult)
            I("dve", "tensor_tensor", out=dst_lo, in0=t1, in1=t2, op=ALU.subtract)
            I("dve", "tensor_tensor", out=t1, in0=s_lo, in1=sb_, op=ALU.mult)
            I("dve", "tensor_tensor", out=t2, in0=s_hi, in1=cb_, op=ALU.mult)
            I("dve", "tensor_tensor", out=dst_hi, in0=t1, in1=t2, op=ALU.add)

        def head_rstd(src_v, H, n, sq, nh):
            I("act", "activation", out=sq, in_=src_v, func=AF.Square)
            I("dve", "tensor_reduce", out=ssq[:, 0:H], in_=sq, axis=AX.X, op=ALU.add)
            rstd_from(ssq[:, 0:H], float(n), rst[:, 0:H])

        with ExitStack() as s1:
            wA = K.sb("wA", [128, 8, 1280], BF16, s1)
            xtl = [K.sb("xtA%d" % i, [128, 1024], F32, s1) for i in range(3)]
            tmp = K.sb("tmpA", [128, 1024], F32, s1)
            hml = [K.sb("hmA%d" % i, [128, 1024], BF16, s1) for i in range(2)]
            hTl = [K.sb("hTA%d" % i, [128, 8, 128], BF16, s1) for i in range(2)]
            crl = [K.sb("crA%d" % i, [128, 128], F32, s1) for i in range(3)]
            cal = [K.sb("caA%d" % i, [128, 64], F32, s1) for i in range(3)]
            kr = K.sb("krA", [128, 4, 128], F32, s1)
            t1 = K.sb("t1A", [128, 4, 64], F32, s1)
            t2 = K.sb("t2A", [128, 4, 64], F32, s1)
            kdb = [K.sb("kdbA%d" % i, [128, 4, 128], BF16, s1) for i in range(2)]
            kdf = [K.sb("kdfA%d" % i, [128, 4, 128], BF16, s1) for i in range(2)]
            vbf = [K.sb("vbfA%d" % i, [128, 4, 128], BF16, s1) for i in range(2)]
            sqk = K.sb("sqkA", [128, 2, 64], F32, s1)
            kn = K.sb("knA", [128, 2, 64], F32, s1)
            kro = K.sb("kroA", [128, 2, 64], BF16, s1)
            ssq2 = K.sb("ssq2A", [128, 2], F32, s1)
            rst2 = K.sb("rst2A", [128, 2], F32, s1)
            wiv = w_in.rearrange("(k p) n -> p k n", p=128)
            K.dma("pool", wA[:, :, 0:1024], wiv[:, :, 512:1536], "w0")
            K.dma("pool", wA[:, :, 1024:1280], wiv[:, :, 2560:2816], "w1")
            I("dve", "memset", ap=attV[:, :, 64:128], constant=1.0)
            tiles = [(ctx[0:128, :], True, 0, 0), (ctx[128:256, :], True, 1, 1)]
            for t in range(NT - 1, NT - 1 - NA, -1):
                tiles.append((x[t * 128:(t + 1) * 128, :], False, t, 2 + t))

            def loadA(i):
                src_ap, is_ctx, idx, kt = tiles[i]
                K.dma("sp", xtl[i % 3][:], src_ap, "x%d" % (i % 3))
                if not is_ctx:
                    K.dma("sp", crl[i % 3][:], rope_r[idx * 128:(idx + 1) * 128, :], "rpa%d" % (i % 2) if False else "ra%d" % (i % 3))
                    K.dma("sp", cal[i % 3][:], rope_a[idx * 128:(idx + 1) * 128, :], "rb%d" % (i % 3))

            def stage1(i):
                src_ap, is_ctx, idx, kt = tiles[i]
                xt, hm, hT = xtl[i % 3], hml[i % 2], hTl[i % 2]
                P = F[0:3] if i % 2 == 0 else F[3:6]
                norm_mod_T(xt, cA1_b if is_ctx else A1_b, csh1_b if is_ctx else sh1_b, tmp, hm, hT)
                for k in range(8):
                    K.mm(P[0][:], hT[:, k, :], wA[:, k, 0:512], start=(k == 0), stop=(k == 7))
                for k in range(8):
                    K.mm(P[1][:], hT[:, k, :], wA[:, k, 512:1024], start=(k == 0), stop=(k == 7))
                for k in range(8):
                    K.mm(P[2][:, 0:256], hT[:, k, :], wA[:, k, 1024:1280], start=(k == 0), stop=(k == 7))

            def stage2(i):
                src_ap, is_ctx, idx, kt = tiles[i]
                cr, ca = crl[i % 3], cal[i % 3]
                P = F[0:3] if i % 2 == 0 else F[3:6]
                akv = P[2].v(P[2].h[:, 0:128].rearrange("p (h d) -> p h d", h=2))
                rk4 = P[0].v(P[0].h[:].rearrange("p (h d) -> p h d", h=4))
                b = idx % 2 if is_ctx else 0
                I("act", "activation", out=sqk[:], in_=akv, func=AF.Square)
                I("act", "activation", out=vbf[b].v(vbf[b].h[:].rearrange("p h d -> p (h d)")), in_=P[1][:], func=AF.Copy)
                if is_ctx:
                    I("dve", "tensor_copy", out=kr[:], in_=rk4)
                else:
                    rope(rk4, 4, 64, cr[:, 0:64], cr[:, 64:128], t1[:], t2[:], kr[:, :, 0:64], kr[:, :, 64:128])
                I("dve", "tensor_reduce", out=ssq2[:], in_=sqk[:], axis=AX.X, op=ALU.add)
                rstd_from(ssq2[:], 64.0, rst2[:])
                if is_ctx:
                    I("dve", "tensor_tensor", out=kdf[b][:], in0=kr[:], in1=wc.v(wc.h[:, idx, 0, :].unsqueeze(2).to_broadcast([128, 4, 128])), op=ALU.mult)
                    I("dve", "tensor_tensor", out=kdb[b][:], in0=kr[:], in1=wc.v(wc.h[:, idx, 1, :].unsqueeze(2).to_broadcast([128, 4, 128])), op=ALU.mult)
                else:
                    I("dve", "tensor_tensor", out=kdb[0][:], in0=kr[:], in1=KD.v(KD.h[:, 1, :].unsqueeze(2).to_broadcast([128, 4, 128])), op=ALU.mult)
                    for h in range(4):
                        K.mm(G1[:, h * 128:(h + 1) * 128], kdb[0][:, h, :], vbf[0][:, h, :])
                    I("act", "activation", out=SbAll[:, idx, :], in_=Sb[:], func=AF.Copy)
                I("dve", "tensor_copy", out=attV[:, kt, 0:64], in_=P[2][:, 128:192])
                I("dve", "tensor_copy", out=attV[:, kt, 128:192], in_=P[2][:, 192:256])
                I("dve", "tensor_tensor", out=kn[:], in0=akv, in1=rst2.v(rst2.h[:, 0:2].unsqueeze(2).to_broadcast([128, 2, 64])), op=ALU.mult)
                I("dve", "tensor_tensor", out=kn[:], in0=kn[:], in1=kg_b.v(kg_b.h[:].unsqueeze(1).to_broadcast([128, 2, 64])), op=ALU.mult)
                if is_ctx:
                    I("dve", "tensor_copy", out=kro[:], in_=kn[:])
                else:
                    rope(kn[:], 2, 32, ca[:, 0:32], ca[:, 32:64], t1.v(t1.h[:, 0:2, 0:32]), t2.v(t2.h[:, 0:2, 0:32]),
                         kro[:, :, 0:32], kro[:, :, 32:64])
                    for h in range(4):
                        I("dve", "scalar_tensor_tensor", out=Sb[:, h * 128:(h + 1) * 128], in0=Sb[:, h * 128:(h + 1) * 128],
                          scalar=cdec[:, 4 + h:5 + h], in1=G1[:, h * 128:(h + 1) * 128], op0=ALU.mult, op1=ALU.add)
                K.tr(psT2[:, 0:128], kro.v(kro.h[:].rearrange("p h d -> p (h d)")), idb[:])
                I("act", "activation", out=attKT[:, kt * 128:(kt + 1) * 128], in_=psT2[:, 0:128], func=AF.Copy)

            def ctx_states():
                for h in range(4):
                    for t in range(2):
                        K.mm(G1[:, h * 128:(h + 1) * 128], kdf[t][:, h, :], vbf[t][:, h, :], start=(t == 0), stop=(t == 1))
                I("dve", "tensor_copy", out=Sf[:], in_=G1[:])
                for h in range(4):
                    for t in range(2):
                        K.mm(G1[:, h * 128:(h + 1) * 128], kdb[t][:, h, :], vbf[t][:, h, :], start=(t == 0), stop=(t == 1))
                I("dve", "tensor_copy", out=Sb[:], in_=G1[:])
                I("dve", "tensor_copy", out=Sf_bf[:], in_=Sf[:])

            n_tiles = len(tiles)
            loadA(0)
            loadA(1)
            stage1(0)
            stage2(0)
            loadA(2) if n_tiles > 2 else None
            stage1(1)
            stage2(1)
            ctx_states()
            K.dma("pool", wi[:, :, 0:1280], wiv[:, :, 0:1280], "wi0")
            K.dma("pool", wi[:, :, 1280:2560], wiv[:, :, 1280:2560], "wi1")
            if n_tiles > 3:
                loadA(3)
            if n_tiles > 2:
                stage1(2)
            for i in range(2, n_tiles):
                if i + 2 < n_tiles:
                    loadA(i + 2)
                if i + 1 < n_tiles:
                    stage1(i + 1)
                stage2(i)
            K.barrier()

        es_c.close()
        with ExitStack() as s2:
            wo_r = K.sb("wo_r", [128, 4, 1024], BF16, s2)
            wo_a = K.sb("wo_a", [128, 4, 1024], BF16, s2)
            xts = [K.sb("xtB%d" % i, [128, 1024], F32, s2) for i in range(3)]
            crs = [K.sb("crB%d" % i, [128, 128], F32, s2) for i in range(2)]
            cas = [K.sb("caB%d" % i, [128, 64], F32, s2) for i in range(2)]
            tmp = K.sb("tmpB", [128, 1024], F32, s2)
            hm = K.sb("hmB", [128, 1024], BF16, s2)
            hT = K.sb("hTB", [128, 8, 128], BF16, s2)
            kr = K.sb("krB", [128, 4, 128], F32, s2)
            t1 = K.sb("t1B", [128, 4, 64], F32, s2)
            t2 = K.sb("t2B", [128, 4, 64], F32, s2)
            qr = K.sb("qrB", [128, 4, 128], BF16, s2)
            k_s = K.sb("k_sB", [128, 4, 128], BF16, s2)
            kdf = K.sb("kdfB", [128, 4, 128], BF16, s2)
            vbf = K.sb("vbfB", [128, 4, 128], BF16, s2)
            gate = K.sb("gateB", [128, 512], F32, s2)
            qro = K.sb("qroB", [128, 8, 64], BF16, s2)
            qrp = K.sb("qrpB", [128, 4, 2, 64], BF16, s2)
            qT = K.sb("qTB", [128, 512], BF16, s2)
            qdTf = K.sb("qdTfB", [128, 512], BF16, s2)
            qdTb = K.sb("qdTbB", [128, 512], BF16, s2)
            kT = K.sb("kTB", [128, 512], BF16, s2)
            QTgs = [[K.sb("QTg%d_%dB" % (g, i), [128, 512], BF16, s2) for g in range(2)] for i in range(2)]
            STf = K.sb("STfB", [128, 512], BF16, s2)
            STb = K.sb("STbB", [128, 512], BF16, s2)
            o_sb = K.sb("o_sbB", [128, 4, 128], F32, s2)
            st4 = K.sb("st4B", [128, 16], F32, s2)
            mixr = K.sb("mixrB", [128, 512], BF16, s2)
            mixT_rs = [K.sb("mixT_rB%d" % i, [128, 4, 128], BF16, s2) for i in range(2)]
            attT = K.sb("attTB", [128, 4, 128], BF16, s2)
            PT = [K.sb("PT%dB" % i, [128, 1024], BF16, s2) for i in range(2)]
            pvraw = K.sb("pvrawB", [128, 512], F32, s2)
            rsw = K.sb("rswB", [128, 512], F32, s2)
            rs = K.sb("rsB", [128, 512], F32, s2)
            sq8 = kr.v(kr.h[:].rearrange("p h (a d) -> p (h a) d", a=2))
            osq = kr
            qn = o_sb.v(o_sb.h[:].rearrange("p h (a d) -> p (h a) d", a=2))
            wiv = w_in.rearrange("(k p) n -> p k n", p=128)
            K.dma("pool", wo_r[:], w_out[0:512, :].rearrange("(k p) n -> p k n", p=128), "w2")
            for g in range(2):
                K.dma("pool", wo_a[g * 64:(g + 1) * 64, :, :],
                      w_out[512 + g * 256:512 + (g + 1) * 256, :].rearrange("(pr d) n -> d pr n", d=64), "w%d" % (3 + g))
            for i in range(2):
                for g in range(2):
                    I("dve", "memset", ap=QTgs[i][g][:], constant=0.0)
            G = [G0, G1]
            Gb = [psT, psT2]

            def load_tile(t):
                b = t % 2
                K.dma("sp", xts[t % 3][:], x[t * 128:(t + 1) * 128, :], "x%d" % (t % 3))
                K.dma("sp", crs[b][:], rope_r[t * 128:(t + 1) * 128, :], "rpa%d" % b)
                K.dma("sp", cas[b][:], rope_a[t * 128:(t + 1) * 128, :], "rpb%d" % b)

            def proj(j, Gt):
                for k in range(8):
                    K.mm(Gt[:], hT[:, k, :], wi[:, k, j * 512:(j + 1) * 512], start=(k == 0), stop=(k == 7))

            def prologue(t):
                b = t % 2
                xt, cr, ca = xts[t % 3], crs[b], cas[b]
                QTg = QTgs[b]
                S = []

                def step(f):
                    S.append(f)
                    return f

                def gap():
                    S.append(None)

                @step
                def _():
                    I("dve", "memset", ap=ssq[:, 0:1], constant=0.0)
                    I("act", "activation", out=tmp[:], in_=xt[:], func=AF.Square, accum_out=ssq[:, 0:1])

                @step
                def _():
                    rstd_from(ssq[:, 0:1], 1024.0, rst[:, 0:1])
                gap()

                @step
                def _():
                    I("dve", "scalar_tensor_tensor", out=tmp[:], in0=xt[:], scalar=rst[:, 0:1], in1=A1_b, op0=ALU.mult, op1=ALU.mult)
                    I("dve", "tensor_tensor", out=hm[:], in0=tmp[:], in1=sh1_b, op=ALU.add)
                gap()

                @step
                def _():
                    for k in range(8):
                        K.tr(psT[:, k * 128:(k + 1) * 128], hm[:, k * 128:(k + 1) * 128], idb[:])

                @step
                def _():
                    I("dve", "tensor_copy", out=hT.v(hT.h[:].rearrange("p a b -> p (a b)")), in_=psT[:])

                @step
                def _():
                    proj(0, G1)

                @step
                def _():
                    proj(1, G0)
                    rq4 = G1.v(G1.h[:].rearrange("p (h d) -> p h d", h=4))
                    rope(rq4, 4, 64, cr[:, 0:64], cr[:, 64:128], t1[:], t2[:], qr[:, :, 0:64], qr[:, :, 64:128])

                @step
                def _():
                    rk4 = G0.v(G0.h[:].rearrange("p (h d) -> p h d", h=4))
                    rope(rk4, 4, 64, cr[:, 0:64], cr[:, 64:128], t1[:], t2[:], kr[:, :, 0:64], kr[:, :, 64:128])
                    I("dve", "tensor_scalar", out=k_s[:], in0=kr[:], scalar1=RET_SCALE, scalar2=None, op0=ALU.mult)
                    I("dve", "tensor_tensor", out=kdf[:], in0=kr[:], in1=KD.v(KD.h[:, 0, :].unsqueeze(2).to_broadcast([128, 4, 128])), op=ALU.mult)

                @step
                def _():
                    proj(2, G1)
                gap()

                @step
                def _():
                    proj(3, G0)
                    I("dve", "tensor_copy", out=vbf.v(vbf.h[:].rearrange("p h d -> p (h d)")), in_=G1[:])
                gap()

                @step
                def _():
                    I("act", "activation", out=gate[:], in_=G0[:], func=AF.Exp, scale=-1.0)

                @step
                def _():
                    I("act", "activation", out=gate[:], in_=gate[:], func=AF.Ln, scale=1.0, bias=1.0)
                    I("act", "activation", out=gate[:], in_=gate[:], func=AF.Exp, scale=-1.0)
                    proj(4, G1)
                gap()

                @step
                def _():
                    I("dve", "tensor_tensor", out=gate[:], in0=G0[:], in1=gate[:], op=ALU.mult)
                    aq8 = G1.v(G1.h[:].rearrange("p (h d) -> p h d", h=8))
                    I("act", "activation", out=sq8, in_=aq8, func=AF.Square)

                @step
                def _():
                    I("dve", "tensor_reduce", out=ssq[:, 0:8], in_=sq8, axis=AX.X, op=ALU.add)
                gap()

                @step
                def _():
                    rstd_from(ssq[:, 0:8], 64.0, rst[:, 0:8])
                gap()

                @step
                def _():
                    aq8 = G1.v(G1.h[:].rearrange("p (h d) -> p h d", h=8))
                    I("dve", "tensor_tensor", out=qn, in0=aq8, in1=rst.v(rst.h[:, 0:8].unsqueeze(2).to_broadcast([128, 8, 64])), op=ALU.mult)
                    I("dve", "tensor_tensor", out=qn, in0=qn, in1=qg_b.v(qg_b.h[:].unsqueeze(1).to_broadcast([128, 8, 64])), op=ALU.mult)

                @step
                def _():
                    t1v = t1.v(t1.h[:].rearrange("p h (a d) -> p (h a) d", a=2))
                    t2v = t2.v(t2.h[:].rearrange("p h (a d) -> p (h a) d", a=2))
                    rope(qn, 8, 32, ca[:, 0:32], ca[:, 32:64], t1v, t2v, qro[:, :, 0:32], qro[:, :, 32:64])
                    for e in range(2):
                        I("dve", "tensor_copy", out=qrp[:, :, e, :], in_=qro[:, e * 4:(e + 1) * 4, :])

                @step
                def _():
                    for h in range(4):
                        K.tr(psT[:, h * 128:(h + 1) * 128], qr[:, h, :], idb[:])
                    for h in range(4):
                        K.tr(psT[:, 512 + h * 128:512 + (h + 1) * 128], k_s[:, h, :], idb[:])
                gap()

                @step
                def _():
                    for pr in range(4):
                        K.tr(psT2[:, pr * 128:(pr + 1) * 128], qrp.v(qrp.h[:, pr, :, :].rearrange("p e d -> p (e d)")), idb[:])
                    I("dve", "tensor_copy", out=qT[:], in_=psT[:, 0:512])
                    I("dve", "tensor_tensor", out=qdTf[:], in0=psT[:, 0:512], in1=QD[:, 0, :], op=ALU.mult)

                @step
                def _():
                    I("dve", "tensor_tensor", out=qdTb[:], in0=psT[:, 0:512], in1=QD[:, 1, :], op=ALU.mult)
                    I("dve", "tensor_copy", out=kT[:], in_=psT[:, 512:1024])

                @step
                def _():
                    I("dve", "tensor_copy", out=QTg[0][0:64, :], in_=psT2[0:64, 0:512])
                    I("dve", "tensor_copy", out=QTg[1][64:128, :], in_=psT2[64:128, 0:512])
                gap()

                @step
                def _():
                    for h in range(4):
                        hs = slice(h * 128, (h + 1) * 128)
                        K.mm(G0[:, hs], kT[:, hs], qT[:, hs])

                @step
                def _():
                    for h in range(4):
                        hs = slice(h * 128, (h + 1) * 128)
                        K.mm(G1[:, hs], kdf[:, h, :], vbf[:, h, :])
                    I("dve", "tensor_tensor", out=STf[:], in0=G0[:], in1=intraT[:, 0, :], op=ALU.mult)
                    I("dve", "tensor_tensor", out=STb[:], in0=G0[:], in1=intraT[:, 1, :], op=ALU.mult)
                gap()

                @step
                def _():
                    for h in range(4):
                        hs = slice(h * 128, (h + 1) * 128)
                        K.mm(G0[:, hs], STf[:, hs], vbf[:, h, :], start=True, stop=False)
                        K.mm(G0[:, hs], STb[:, hs], vbf[:, h, :], start=False, stop=False)
                        K.mm(G0[:, hs], qdTf[:, hs], Sf_bf[:, hs], start=False, stop=False)
                        K.mm(G0[:, hs], qdTb[:, hs], SbAll[:, t, hs], start=False, stop=True)

                @step
                def _():
                    for h in range(4):
                        hs = slice(h * 128, (h + 1) * 128)
                        I("dve", "scalar_tensor_tensor", out=Sf[:, hs], in0=Sf[:, hs], scalar=cdec[:, h:h + 1], in1=G1[:, hs], op0=ALU.mult, op1=ALU.add)
                    I("dve", "tensor_copy", out=Sf_bf[:], in_=Sf[:])

                @step
                def _():
                    I("dve", "tensor_copy", out=o_sb.v(o_sb.h[:].rearrange("p h d -> p (h d)")), in_=G0[:])
                    I("dve", "tensor_reduce", out=st4[:, 0:4], in_=o_sb[:], axis=AX.X, op=ALU.add)
                    I("dve", "tensor_scalar", out=st4[:, 0:4], in0=st4[:, 0:4], scalar1=1.0 / 128.0, scalar2=None, op0=ALU.mult)
                    I("dve", "tensor_tensor", out=o_sb[:], in0=o_sb[:], in1=st4.v(st4.h[:, 0:4].unsqueeze(2).to_broadcast([128, 4, 128])), op=ALU.subtract)

                @step
                def _():
                    I("dve", "tensor_tensor", out=osq[:], in0=o_sb[:], in1=o_sb[:], op=ALU.mult)
                    I("dve", "tensor_reduce", out=st4[:, 4:8], in_=osq[:], axis=AX.X, op=ALU.add)
                gap()

                @step
                def _():
                    I("act", "activation", out=st4[:, 12:16], in_=st4[:, 4:8], func=AF.Ln, scale=1.0 / 128.0, bias=EPS)
                    I("act", "activation", out=st4[:, 12:16], in_=st4[:, 12:16], func=AF.Exp, scale=-0.5)
                gap()

                @step
                def _():
                    I("dve", "tensor_tensor", out=o_sb[:], in0=o_sb[:], in1=st4.v(st4.h[:, 12:16].unsqueeze(2).to_broadcast([128, 4, 128])), op=ALU.mult)
                    o2 = o_sb.v(o_sb.h[:].rearrange("p h d -> p (h d)"))
                    I("dve", "tensor_tensor", out=o2, in0=o2, in1=gn_b[:], op=ALU.mult)
                    I("dve", "tensor_tensor", out=mixr[:], in0=o2, in1=gate[:], op=ALU.mult)
                gap()

                @step
                def _():
                    for k in range(4):
                        K.tr(psT2[:, k * 128:(k + 1) * 128], mixr[:, k * 128:(k + 1) * 128], idb[:])

                @step
                def _():
                    I("dve", "tensor_copy", out=mixT_rs[b].v(mixT_rs[b].h[:].rearrange("p a b -> p (a b)")), in_=psT2[:, 0:512])
                return S

            def attention(t):
                QTg = QTgs[t % 2]
                Qs = [QA, QB]

                def qk(kt):
                    Q = Qs[kt % 2]
                    K.mm(Q[:, 0:512], attKT[:, kt * 128:(kt + 1) * 128], QTg[0][:])
                    K.mm(Q[:, 512:1024], attKT[:, kt * 128:(kt + 1) * 128], QTg[1][:])

                qk(0)
                for kt in range(NKT):
                    if kt + 1 < NKT:
                        qk(kt + 1)
                    pt = PT[kt % 2]
                    I("act", "activation", out=pt[:], in_=Qs[kt % 2][:], func=AF.Exp, scale=ATT_SCALE)
                    K.mm(acc0[:], attV[:, kt, 0:128], pt[:, 0:512], start=(kt == 0), stop=(kt == NKT - 1))
                    K.mm(acc1[:], attV[:, kt, 64:192], pt[:, 512:1024], start=(kt == 0), stop=(kt == NKT - 1))
                    yield
                I("act", "activation", out=rsw[64:128, :], in_=acc0[64:128, :], func=AF.Ln)
                I("act", "activation", out=rsw[0:64, :], in_=acc1[0:64, :], func=AF.Ln)
                I("act", "activation", out=rsw[:], in_=rsw[:], func=AF.Exp, scale=-1.0)
                I("dve", "tensor_copy", out=pvraw[0:64, :], in_=acc0[0:64, :])
                I("dve", "tensor_copy", out=pvraw[64:128, :], in_=acc1[64:128, :])
                K.dma("sp", rs[0:64, :], rsw[64:128, :], "swp")
                K.dma("sp", rs[64:128, :], rsw[0:64, :], "swp")
                yield

            def epilogue(t):
                b = t % 2
                xt = xts[t % 3]
                S = []

                def step(f):
                    S.append(f)
                    return f

                def gap():
                    S.append(None)

                @step
                def _():
                    I("dve", "tensor_tensor", out=attT.v(attT.h[:].rearrange("p a b -> p (a b)")), in0=pvraw[:], in1=rs[:], op=ALU.mult)

                @step
                def _():
                    for half in range(2):
                        cs = slice(half * 512, (half + 1) * 512)
                        Gy = G[half]
                        for k in range(4):
                            K.mm(Gy[:], mixT_rs[b][:, k, :], wo_r[:, k, cs], start=(k == 0), stop=False)
                        for pr in range(4):
                            K.mm(Gy[:], attT[:, pr, :], wo_a[:, pr, cs], start=False, stop=(pr == 3))

                @step
                def _():
                    for half in range(2):
                        cs = slice(half * 512, (half + 1) * 512)
                        I("dve", "tensor_tensor", out=tmp[:, cs], in0=G[half][:], in1=V(g1_b.ap[:, cs], g1_b.deps), op=ALU.mult)
                        I("dve", "tensor_tensor", out=xt[:, cs], in0=tmp[:, cs], in1=xt[:, cs], op=ALU.add)
                gap()

                @step
                def _():
                    K.dma("sp", x1s[t * 128:(t + 1) * 128, :], xt[:], "st1")
                    I("dve", "memset", ap=ssq[:, 0:1], constant=0.0)
                    I("act", "activation", out=tmp[:], in_=xt[:], func=AF.Square, accum_out=ssq[:, 0:1])

                @step
                def _():
                    rstd_from(ssq[:, 0:1], 1024.0, rst[:, 0:1])
                gap()

                @step
                def _():
                    I("dve", "scalar_tensor_tensor", out=tmp[:], in0=xt[:], scalar=rst[:, 0:1], in1=A2_b, op0=ALU.mult, op1=ALU.mult)
                    I("dve", "tensor_tensor", out=hm[:], in0=tmp[:], in1=sh2_b, op=ALU.add)
                gap()

                @step
                def _():
                    for k in range(8):
                        K.tr(psT[:, k * 128:(k + 1) * 128], hm[:, k * 128:(k + 1) * 128], idb[:])

                @step
                def _():
                    I("dve", "tensor_copy", out=hT.v(hT.h[:].rearrange("p a b -> p (a b)")), in_=psT[:])
                    K.dma("sp", h2s[:, :, t * 128:(t + 1) * 128], hT[:], "st2")
                return S

            def run(steps):
                for f in steps:
                    if f is not None:
                        f()

            load_tile(0)
            if NB > 1:
                load_tile(1)
            if NB > 0:
                run(prologue(0))
            for t in range(NB):
                side = (epilogue(t - 1) if t > 0 else []) + (prologue(t + 1) if t + 1 < NB else [])
                n = len(side)
                done = 0
                it = 0
                for _ in attention(t):
                    it += 1
                    upto = min(n, (it * n + NKT - 1) // NKT)
                    run(side[done:upto])
                    done = upto
                run(side[done:])
                if t + 2 < NB:
                    load_tile(t + 2)
            if NB > 0:
                run(epilogue(NB - 1))
            K.barrier()

        es_ab.close()
        with ExitStack() as s3:
            W1c = [K.sb("W1c%d" % j, [128, 8, 1024], BF16, s3) for j in range(4)]
            W2c = [K.sb("W2c%d" % j, [128, 8, 1024], BF16, s3) for j in range(4)]
            hb = K.sb("hbC", [128, 8, 512], BF16, s3)
            act = K.sb("actC", [128, 32, 512], BF16, s3)
            rl = [K.sb("rlC%d" % i, [128, 512], F32, s3) for i in range(2)]
            x1 = K.sb("x1C", [128, 1024], F32, s3)
            x2 = K.sb("x2C", [128, 1024], F32, s3)
            tmp = K.sb("tmpC", [128, 1024], F32, s3)
            ot = K.sb("otC", [128, 1024], F32, s3)
            gf_b = K.sb("gf_b", [128, 1024], F32, s3)
            K.dma("sp", gf_b[:], fng.partition_broadcast(128), "cst8")
            w1v = w1.rearrange("(k p) n -> p k n", p=128)
            w2v = w2.rearrange("(k p) n -> p k n", p=128)
            for j in range(4):
                K.dma("pool", W1c[j][:], w1v[:, :, j * 1024:(j + 1) * 1024], "w%d" % j)
            for j in range(4):
                K.dma("pool", W2c[j][:], w2v[:, j * 8:(j + 1) * 8, :], "v%d" % j)
            for blk in range(NC):
                K.dma("sp", hb[:], h2s[:, :, blk * 512:(blk + 1) * 512], "hl")
                for fc in range(32):
                    Fo = F[fc % 2]
                    for k in range(8):
                        K.mm(Fo[:], W1c[fc // 8][:, k, (fc % 8) * 128:(fc % 8 + 1) * 128], hb[:, k, :], start=(k == 0), stop=(k == 7))
                    r = rl[fc % 2]
                    I("act", "activation", out=r[:], in_=Fo[:], func=AF.Relu)
                    I("pool", "tensor_tensor", out=act[:, fc, :], in0=r[:], in1=r[:], op=ALU.mult)
                for tt in range(4):
                    t = blk * 4 + tt
                    K.dma("sp", x1[:], x1s[t * 128:(t + 1) * 128, :], "xl")
                    for half in range(2):
                        cs = slice(half * 512, (half + 1) * 512)
                        Fy = F[2 + half]
                        for fc in range(32):
                            K.mm(Fy[:], act[:, fc, tt * 128:(tt + 1) * 128], W2c[fc // 8][:, fc % 8, cs], start=(fc == 0), stop=(fc == 31))
                        I("dve", "tensor_tensor", out=tmp[:, cs], in0=Fy[:], in1=V(g2_b.ap[:, cs], g2_b.deps), op=ALU.mult)
                        I("dve", "tensor_tensor", out=x2[:, cs], in0=tmp[:, cs], in1=x1[:, cs], op=ALU.add)
                    row_rstd(x2[:], tmp[:], 1024.0)
                    I("dve", "scalar_tensor_tensor", out=ot[:], in0=x2[:], scalar=rst[:, 0:1], in1=gf_b[:], op0=ALU.mult, op1=ALU.mult)
                    K.dma("sp", out[t * 128:(t + 1) * 128, :], ot[:], "out")

        nw, cnt = K.emit()
        K.finish()
    return nc


def _tables():
    Lq = 4096
    t = np.arange(Lq, dtype=np.float32)
    fr = (np.float32(10000.0) ** (-(np.arange(64, dtype=np.float32) / np.float32(64)))).astype(np.float32)
    ang = (t[:, None] * fr[None, :]).astype(np.float32).astype(np.float64)
    rope_r = np.concatenate([np.cos(ang), np.sin(ang)], axis=1).astype(np.float32)
    af = (np.float32(10000.0) ** (-(np.arange(16, dtype=np.float32) / np.float32(16)))).astype(np.float32)
    row = np.repeat(np.arange(64, dtype=np.float32), 64)
    col = np.tile(np.arange(64, dtype=np.float32), 64)
    aang = np.concatenate([(row[:, None] * af[None, :]).astype(np.float32), (col[:, None] * af[None, :]).astype(np.float32)], axis=1).astype(np.float64)
    rope_a = np.concatenate([np.cos(aang), np.sin(aang)], axis=1).astype(np.float32)
    j = np.arange(128, dtype=np.float32)[:, None]
    i = np.arange(128, dtype=np.float32)[None, :]
    cm = np.zeros((128, 6, 128), np.float32)
    cm[:, 0, :] = np.maximum(i - j, 0.0)
    cm[:, 1, :] = (i >= j).astype(np.float32)
    cm[:, 2, :] = np.maximum(j - i, 0.0)
    cm[:, 3, :] = (j >= i).astype(np.float32)
    cm[:, 4, :] = i + 1.0 + 0.0 * j
    cm[:, 5, :] = 128.0 - i + 0.0 * j
    p = np.arange(128, dtype=np.float32)
    pcols = np.stack([p, 127.0 - p, 255.0 - p, 128.0 + p], axis=1).astype(np.float32)
    return rope_r, rope_a, cm, pcols, np.eye(128, dtype=np.float32)


def kernel(x, c, ctx, c_ctx, w_mod, b_mod, norm1_g, norm2_g, w_in, w_out, ret_log_rate, ret_gn_g,
           q_norm_g, k_norm_g, w_ff1, w_ff2, final_norm_g):
    f = lambda a: np.ascontiguousarray(np.asarray(a, dtype=np.float32))
    x = f(x); c = f(c); ctx = f(ctx); c_ctx = f(c_ctx)
    rope_r, rope_a, cm, pcols, ident = _tables()
    shared = {
        "w_mod": f(w_mod)[0], "b_mod": f(b_mod)[0], "norm1_g": f(norm1_g)[0], "norm2_g": f(norm2_g)[0],
        "w_in": f(w_in)[0], "w_out": f(w_out)[0], "rate": f(ret_log_rate)[0].reshape(8),
        "ret_gn_g": f(ret_gn_g)[0], "q_norm_g": f(q_norm_g)[0], "k_norm_g": f(k_norm_g)[0],
        "w_ff1": f(w_ff1)[0], "w_ff2": f(w_ff2)[0], "final_norm_g": f(final_norm_g),
        "rope_r": rope_r, "rope_a": rope_a, "cmat": cm, "pcols": pcols, "ident": ident,
    }
    n = x.shape[0]
    in_maps = []
    for b in range(n):
        m = dict(shared)
        m["x"] = x[b]
        m["ctx"] = ctx[b]
        m["cvec"] = np.ascontiguousarray(np.stack([c[b], c_ctx], axis=0))
        in_maps.append(m)
    nc = build_nc()
    res = run_bass_kernel_spmd(nc, in_maps, core_ids=list(range(n)))
    return np.stack([np.asarray(r["out"], dtype=np.float32) for r in res.results], axis=0)
```
